# Optimizing a Trainium2 kernel written in Bass

```python
import jax, jax.numpy as jnp
from jax import lax
import numpy as np

D_MODEL = 1024
BATCH = 4
SEQ = 4096
DEPTH = 1
DEC_BATCH = 128
DEC_SEQ = 4
PAST_LEN = 8192
PAGE_SIZE = 128

N_HEADS = 8
HEAD_DIM = 64
D_ATT = N_HEADS * HEAD_DIM
D_LRU = D_MODEL // 2
LRU_BLOCKS = 8
LRU_BLOCK_DIM = D_LRU // LRU_BLOCKS
D_MIX = D_ATT + D_LRU
D_IN = 4 * D_ATT + 2 * D_LRU
CONV_WIDTH = 4
LRU_C = 8.0
DIL_PATTERNS = ((128, 1), (512, 4), (2048, 16))
N_DIL_KEYS = 128
ATT_WINDOW_MAX = 2048
BAND_BLOCK = 128
ALPHA = (2.0 * DEPTH) ** 0.25
BETA = (8.0 * DEPTH) ** -0.25
LN_EPS = 1e-5
NEG_INF = -1e30

kernel_name = "hymba_dilated_attn_rglru_deepnorm_step"


def _project(x, w_in):
    h = jnp.einsum('btd,de->bte', x, w_in)
    cuts = [D_ATT, 2 * D_ATT, 3 * D_ATT, 4 * D_ATT, 4 * D_ATT + D_LRU]
    return jnp.split(h, cuts, axis=-1)


def _band_attention(q, k, v):
    n, l, h, dh = q.shape
    nb = -(-l // BAND_BLOCK)
    pad = nb * BAND_BLOCK - l
    padw = ((0, 0), (0, pad), (0, 0), (0, 0))
    qb = jnp.pad(q, padw).reshape(n, nb, BAND_BLOCK, h, dh)
    kb = jnp.pad(k, padw).reshape(n, nb, BAND_BLOCK, h, dh)
    vb = jnp.pad(v, padw).reshape(n, nb, BAND_BLOCK, h, dh)
    prev = ((0, 0), (1, 0), (0, 0), (0, 0), (0, 0))
    kc = jnp.concatenate([jnp.pad(kb, prev)[:, :-1], kb], axis=2)
    vc = jnp.concatenate([jnp.pad(vb, prev)[:, :-1], vb], axis=2)
    s = jnp.einsum('nbqhd,nbkhd->nbhqk', qb, kc) * (dh ** -0.5)
    qi = jnp.arange(BAND_BLOCK)[:, None]
    ki = jnp.arange(2 * BAND_BLOCK)[None, :]
    dist = qi + BAND_BLOCK - ki
    key_pos = jnp.arange(nb)[:, None, None] * BAND_BLOCK + ki[None] - BAND_BLOCK
    valid = (dist >= 0) & (dist <= N_DIL_KEYS) & (key_pos >= 0)
    s = jnp.where(valid[None, :, None], s, NEG_INF)
    m = s.max(-1)
    p = jnp.exp(s - m[..., None])
    den = p.sum(-1)
    o = jnp.einsum('nbhqk,nbkhd->nbqhd', p, vc) / den.transpose(0, 1, 3, 2)[..., None]
    o = o.reshape(n, nb * BAND_BLOCK, h, dh)[:, :l]
    m = m.transpose(0, 1, 3, 2).reshape(n, nb * BAND_BLOCK, h)[:, :l]
    den = den.transpose(0, 1, 3, 2).reshape(n, nb * BAND_BLOCK, h)[:, :l]
    return o, m, den


def _dilated_prompt(q, k, v, d):
    b, s, h, dh = q.shape
    l = s // d

    def sub(t):
        return t.reshape(b, l, d, h, dh).transpose(0, 2, 1, 3, 4).reshape(b * d, l, h, dh)

    o, m, den = _band_attention(sub(q), sub(k), sub(v))
    o = o.reshape(b, d, l, h, dh).transpose(0, 2, 1, 3, 4).reshape(b, s, h, dh)
    m = m.reshape(b, d, l, h).transpose(0, 2, 1, 3).reshape(b, s, h)
    den = den.reshape(b, d, l, h).transpose(0, 2, 1, 3).reshape(b, s, h)
    return o, m, den


def _dilated_sample(q, k_all, v_all, d):
    t = q.shape[1]
    w_buf = k_all.shape[1] - t
    idx = w_buf + jnp.arange(t)[:, None] - d * jnp.arange(N_DIL_KEYS + 1)[None, :]
    valid = idx >= 0
    idx = jnp.maximum(idx, 0)
    kg = jnp.take(k_all, idx, axis=1)
    vg = jnp.take(v_all, idx, axis=1)
    s = jnp.einsum('bqhd,bqjhd->bhqj', q, kg) * (q.shape[-1] ** -0.5)
    s = jnp.where(valid[None, None], s, NEG_INF)
    m = s.max(-1)
    p = jnp.exp(s - m[..., None])
    den = p.sum(-1)
    o = jnp.einsum('bhqj,bqjhd->bqhd', p, vg) / den.transpose(0, 2, 1)[..., None]
    return o, m.transpose(0, 2, 1), den.transpose(0, 2, 1)


def _merge(parts):
    m_max = jnp.max(jnp.stack([p[1] for p in parts]), axis=0)
    ws = [den * jnp.exp(m - m_max) for (_, m, den) in parts]
    total = sum(ws)
    return sum(w[..., None] * o for w, (o, _, _) in zip(ws, parts)) / total[..., None]


def _lin_combine(e1, e2):
    a1, b1 = e1
    a2, b2 = e2
    return a1 * a2, a2 * b1 + b2


def _lru_branch(xl, conv_state, h0, conv_w, conv_b, w_ra, b_ra, w_ri, b_ri, lru_lambda):
    b, t, _ = xl.shape
    xc = jnp.concatenate([conv_state.astype(xl.dtype), xl], axis=1)
    u = conv_b + sum(xc[:, w:w + t] * conv_w[w] for w in range(CONV_WIDTH))
    new_conv = xc[:, t:]
    u32 = u.astype(jnp.float32)
    ub = u32.reshape(b, t, LRU_BLOCKS, LRU_BLOCK_DIM)
    r = jax.nn.sigmoid(jnp.einsum('bthi,hij->bthj', ub, w_ra.astype(jnp.float32)).reshape(b, t, D_LRU)
                       + b_ra.astype(jnp.float32))
    ig = jax.nn.sigmoid(jnp.einsum('bthi,hij->bthj', ub, w_ri.astype(jnp.float32)).reshape(b, t, D_LRU)
                        + b_ri.astype(jnp.float32))
    log_a = -LRU_C * r * jax.nn.softplus(-lru_lambda.astype(jnp.float32))
    a = jnp.exp(log_a)
    bx = jnp.sqrt(-jnp.expm1(2.0 * log_a)) * ig * u32
    bx = bx.at[:, 0].add(a[:, 0] * h0.astype(jnp.float32))
    _, h = lax.associative_scan(_lin_combine, (a, bx), axis=1)
    return h, new_conv, h[:, -1]


def _finish(x, y_att, g_att, h_lru, g_lru, w_out, ln_g, ln_b):
    mixed = jnp.concatenate([y_att * jax.nn.silu(g_att.astype(jnp.float32)),
                             h_lru * jax.nn.silu(g_lru.astype(jnp.float32))], axis=-1)
    sub = jnp.einsum('bte,ed->btd', mixed, w_out.astype(jnp.float32))
    z = ALPHA * x.astype(jnp.float32) + sub
    mu = z.mean(-1, keepdims=True)
    var = jnp.square(z - mu).mean(-1, keepdims=True)
    zn = (z - mu) * lax.rsqrt(var + LN_EPS)
    return (zn * ln_g.astype(jnp.float32) + ln_b.astype(jnp.float32)).astype(x.dtype)


def _layer(x_p, x_s, ck, cv, sc, sh, w_in, conv_w, conv_b, w_ra, b_ra, w_ri, b_ri, lru_lambda,
           w_out, ln_g, ln_b):
    f32 = jnp.float32
    b, s, _ = x_p.shape
    q, k, v, ga, xl, gl = _project(x_p, w_in)
    qh, kh, vh = (t_.reshape(b, s, N_HEADS, HEAD_DIM) for t_ in (q, k, v))
    y_att = _merge([_dilated_prompt(qh.astype(f32), kh.astype(f32), vh.astype(f32), d)
                    for _, d in DIL_PATTERNS]).reshape(b, s, D_ATT)
    h_lru, conv_p, h_p = _lru_branch(xl, jnp.zeros((b, CONV_WIDTH - 1, D_LRU), xl.dtype),
                                     jnp.zeros((b, D_LRU), f32), conv_w, conv_b, w_ra, b_ra,
                                     w_ri, b_ri, lru_lambda)
    y_p = _finish(x_p, y_att, ga, h_lru, gl, w_out, ln_g, ln_b)
    n_keep = min(ATT_WINDOW_MAX, s)
    k_p, v_p = kh[:, s - n_keep:], vh[:, s - n_keep:]
    bs, t, _ = x_s.shape
    q2, k2, v2, ga2, xl2, gl2 = _project(x_s, w_in)
    qh2, kh2, vh2 = (t_.reshape(bs, t, N_HEADS, HEAD_DIM) for t_ in (q2, k2, v2))
    k_all = jnp.concatenate([ck.astype(f32), kh2.astype(f32)], axis=1)
    v_all = jnp.concatenate([cv.astype(f32), vh2.astype(f32)], axis=1)
    y_att2 = _merge([_dilated_sample(qh2.astype(f32), k_all, v_all, d)
                     for _, d in DIL_PATTERNS]).reshape(bs, t, D_ATT)
    h_lru2, conv_s, h_s = _lru_branch(xl2, sc, sh, conv_w, conv_b, w_ra, b_ra, w_ri, b_ri, lru_lambda)
    y_s = _finish(x_s, y_att2, ga2, h_lru2, gl2, w_out, ln_g, ln_b)
    return y_p, y_s, k_p, v_p, conv_p, h_p, kh2, vh2, conv_s, h_s


def setup_inputs(seed: int = 0) -> dict:
    key = jax.random.key(seed)
    ks = jax.random.split(key, 17)
    w_buf = min(ATT_WINDOW_MAX, PAST_LEN)
    nrm = jax.random.normal
    col_scale = jnp.concatenate([jnp.ones((2 * D_ATT,)), jnp.full((D_ATT,), BETA), jnp.ones((D_ATT,)),
                                 jnp.full((D_LRU,), BETA), jnp.ones((D_LRU,))]).astype(jnp.float32)
    u = jax.random.uniform(ks[12], (DEPTH, D_LRU), minval=0.9, maxval=0.999)
    a_base = u ** (1.0 / LRU_C)
    return {
        "x_prompt": nrm(ks[0], (BATCH, SEQ, D_MODEL), jnp.float32),
        "x_sample": nrm(ks[1], (DEC_BATCH, DEC_SEQ, D_MODEL), jnp.float32),
        "cache_k": nrm(ks[2], (DEPTH, DEC_BATCH, w_buf, N_HEADS, HEAD_DIM), jnp.float32),
        "cache_v": nrm(ks[3], (DEPTH, DEC_BATCH, w_buf, N_HEADS, HEAD_DIM), jnp.float32) * BETA,
        "state_conv": nrm(ks[4], (DEPTH, DEC_BATCH, CONV_WIDTH - 1, D_LRU), jnp.float32) * BETA,
        "state_h": nrm(ks[5], (DEPTH, DEC_BATCH, D_LRU), jnp.float32) * 0.5,
        "w_in": nrm(ks[6], (DEPTH, D_MODEL, D_IN), jnp.float32) * (D_MODEL ** -0.5) * col_scale,
        "conv_w": nrm(ks[7], (DEPTH, CONV_WIDTH, D_LRU), jnp.float32) * (CONV_WIDTH ** -0.5),
        "conv_b": nrm(ks[8], (DEPTH, D_LRU), jnp.float32) * 0.01,
        "w_ra": nrm(ks[9], (DEPTH, LRU_BLOCKS, LRU_BLOCK_DIM, LRU_BLOCK_DIM), jnp.float32) * (LRU_BLOCK_DIM ** -0.5),
        "b_ra": nrm(ks[10], (DEPTH, D_LRU), jnp.float32) * 0.01,
        "w_ri": nrm(ks[11], (DEPTH, LRU_BLOCKS, LRU_BLOCK_DIM, LRU_BLOCK_DIM), jnp.float32) * (LRU_BLOCK_DIM ** -0.5),
        "b_ri": nrm(ks[13], (DEPTH, D_LRU), jnp.float32) * 0.01,
        "lru_lambda": jnp.log(a_base) - jnp.log1p(-a_base),
        "w_out": nrm(ks[14], (DEPTH, D_MIX, D_MODEL), jnp.float32) * (D_MIX ** -0.5) * BETA,
        "ln_g": 1.0 + 0.02 * nrm(ks[15], (DEPTH, D_MODEL), jnp.float32),
        "ln_b": 0.02 * nrm(ks[16], (DEPTH, D_MODEL), jnp.float32),
    }


def reference(x_prompt, x_sample, cache_k, cache_v, state_conv, state_h, w_in, conv_w, conv_b,
              w_ra, b_ra, w_ri, b_ri, lru_lambda, w_out, ln_g, ln_b):
    xp, xs = x_prompt, x_sample
    kp_l, vp_l, cp_l, hp_l, ks_l, vs_l, cs_l, hs_l = [], [], [], [], [], [], [], []
    for l in range(DEPTH):
        xp, xs, kp, vp, cp, hp, ks_, vs_, cs, hs = _layer(
            xp, xs, cache_k[l], cache_v[l], state_conv[l], state_h[l], w_in[l], conv_w[l], conv_b[l],
            w_ra[l], b_ra[l], w_ri[l], b_ri[l], lru_lambda[l], w_out[l], ln_g[l], ln_b[l])
        kp_l.append(kp); vp_l.append(vp); cp_l.append(cp); hp_l.append(hp)
        ks_l.append(ks_); vs_l.append(vs_); cs_l.append(cs); hs_l.append(hs)
    return (xp, xs, jnp.stack(kp_l), jnp.stack(vp_l), jnp.stack(cp_l), jnp.stack(hp_l),
            jnp.stack(ks_l), jnp.stack(vs_l), jnp.stack(cs_l), jnp.stack(hs_l))
```

```python
import contextlib
import numpy as np
import concourse.bass as bass
import concourse.mybir as mybir
from concourse.bass_utils import run_bass_kernel_spmd

F32 = mybir.dt.float32
BF16 = mybir.dt.bfloat16
ALU = mybir.AluOpType
AF = mybir.ActivationFunctionType
AX = mybir.AxisListType

ENGS = ("pe", "act", "dve", "pool", "sp")
N_DMA_SLOTS = 40
NEG = -30000.0
ALPHA = 2.0 ** 0.25
LN_EPS = 1e-5


class Op:
    __slots__ = ("eng", "fn", "deps", "is_dma", "slot", "slot_val", "has_dep", "val")

    def __init__(self, eng, fn, is_dma):
        self.eng = eng
        self.fn = fn
        self.deps = []
        self.is_dma = is_dma
        self.slot = None
        self.slot_val = 0
        self.has_dep = False
        self.val = 0


class Prog:
    def __init__(self):
        self.ops = {e: [] for e in ENGS}
        self.res = {}
        self.slot_last = [None] * N_DMA_SLOTS
        self.slot_cnt = [0] * N_DMA_SLOTS
        self.next_slot = 0

    def add(self, eng, fn, reads=(), writes=(), dma=False):
        op = Op(eng, fn, dma)
        deps = []
        for k in reads:
            st = self.res.get(k)
            if st is not None and st[0] is not None:
                deps.append(st[0])
            if st is not None and k.startswith("bank"):
                deps.extend(r for r in st[1] if r.eng != eng)
        for k in writes:
            st = self.res.get(k)
            if st is not None:
                if st[0] is not None:
                    deps.append(st[0])
                deps.extend(st[1])
        if dma:
            s = self.next_slot
            self.next_slot = (s + 1) % N_DMA_SLOTS
            prev = self.slot_last[s]
            if prev is not None:
                deps.append(prev)
            self.slot_cnt[s] += 1
            op.slot = s
            op.slot_val = 16 * self.slot_cnt[s]
            self.slot_last[s] = op
        seen = set()
        for d in deps:
            if d is op or id(d) in seen:
                continue
            seen.add(id(d))
            if (not d.is_dma) and d.eng == eng and eng == "pe":
                continue
            op.deps.append(d)
            d.has_dep = True
        for k in reads:
            st = self.res.setdefault(k, [None, []])
            st[1].append(op)
        for k in writes:
            self.res[k] = [op, []]
        self.ops[eng].append(op)
        return op

    def emit(self, nc):
        for e in ENGS:
            v = 0
            for op in self.ops[e]:
                if not op.is_dma and op.has_dep:
                    v += 1
                    op.val = v
        with contextlib.ExitStack() as st:
            esem = {e: st.enter_context(nc.semaphore("s_" + e)) for e in ENGS}
            dsem = [st.enter_context(nc.semaphore("d%d" % i)) for i in range(N_DMA_SLOTS)]
            block = st.enter_context(nc.Block())

            def run(e, eh):
                waited = {}
                for op in self.ops[e]:
                    for d in op.deps:
                        if d.is_dma:
                            key, val, sem = ("d", d.slot), d.slot_val, dsem[d.slot]
                        else:
                            key, val, sem = ("e", d.eng), d.val, esem[d.eng]
                        if waited.get(key, 0) >= val:
                            continue
                        waited[key] = val
                        eh.wait_ge(sem, val)
                    ins = op.fn(eh)
                    if op.is_dma:
                        ins.then_inc(dsem[op.slot], 16)
                    elif op.has_dep:
                        ins.then_inc(esem[e], 1)
                if e == "sp":
                    for s in range(N_DMA_SLOTS):
                        if self.slot_cnt[s] and waited.get(("d", s), 0) < 16 * self.slot_cnt[s]:
                            eh.wait_ge(dsem[s], 16 * self.slot_cnt[s])

            block.tensor(lambda eh: run("pe", eh))
            block.scalar(lambda eh: run("act", eh))
            block.vector(lambda eh: run("dve", eh))
            block.gpsimd(lambda eh: run("pool", eh))
            block.sync(lambda eh: run("sp", eh))


NTOK = 2048
NCTX = 2048
NS = 64
NB = 16
PATTERNS = (1, 4, 16)


def build_program(do_sample=True):
    nc = bass.Bass("TRN2", target_bir_lowering=False)
    P = Prog()

    def din(name, shape):
        return nc.dram_tensor(name, list(shape), F32, kind="ExternalInput").ap()

    def dout(name, shape):
        return nc.dram_tensor(name, list(shape), F32, kind="ExternalOutput").ap()

    xo = din("xo", [NTOK, 1024])
    xc = din("xc", [NCTX, 1024])
    w_in = din("w_in", [1024, 3072])
    w_out = din("w_out", [1024, 1024])
    ident_d = din("ident", [128, 128])
    mask_own_d = din("mask_own", [128, 2, 256])
    mask_ctx_d = din("mask_ctx", [128, 2, 256])
    prm_d = din("prm", [128, 32])
    wra_d = din("wra_bd", [128, 4, 128])
    wri_d = din("wri_bd", [128, 4, 128])
    lng_d = din("lng", [128, 1024])
    lnb_d = din("lnb", [128, 1024])
    flag_d = din("flag", [128, 1])
    xs_d = din("xs", [NS, 1024])
    ck_d = din("ck", [NB, 2048, 512])
    cv_d = din("cv", [NB, 2048, 512])
    sconv_d = din("sconv", [128, 4, NB, 3])
    sh_d = din("sh", [128, 4, NB])
    m1_d = din("m1", [128, 32])
    wnew_d = din("wnew", [4, 32])

    y_d = dout("y", [NTOK, 1024])
    kT_d = dout("kT", [512, NTOK])
    vT_d = dout("vT", [512, NTOK])
    convT_d = dout("convT", [128, 4, 3])
    hT_d = dout("hT", [128, 4])
    ys_d = dout("ys", [NS, 1024])
    qs_d = dout("qs", [NS, 512])
    ks_d = dout("ks", [NS, 512])
    vs_d = dout("vs", [NS, 512])
    convs_d = dout("convs", [128, 4, NB, 3])
    hs_d = dout("hs", [128, 4, NB])

    with contextlib.ExitStack() as st:
        def sb(name, shape, dt):
            return st.enter_context(nc.sbuf_tensor("sb_" + name, list(shape), dt))

        def psb(name, shape, dt):
            return st.enter_context(nc.psum_tensor("ps_" + name, list(shape), dt))

        KT = sb("KT", [128, 4, 4096], BF16)
        VT = sb("VT", [128, 4, 4096], BF16)
        QT = sb("QT", [128, 4, NTOK], BF16)
        GA = sb("GA", [128, 4, NTOK], BF16)
        ML = sb("ML", [128, 4, NTOK], BF16)
        ident = sb("identb", [128, 128], BF16)
        ones = sb("ones", [128, 128], BF16)
        mask_own = sb("mask_own", [128, 2, 256], BF16)
        mask_ctx = sb("mask_ctx", [128, 2, 256], BF16)
        prm = sb("prm", [128, 32], F32)
        cneg = sb("cneg", [128, 8], F32)
        wra = sb("wra", [128, 4, 128], BF16)
        wri = sb("wri", [128, 4, 128], BF16)
        flag = sb("flag", [128, 1], F32)
        hstate = sb("hstate", [128, 4], F32)
        zcol = sb("zcol", [128, 1], F32)
        XL = sb("XL", [128, 3 + 512], F32)
        HALO = sb("HALO", [128, 4, 3], F32)
        SIGG = sb("SIGG", [128, 512], F32)
        arena = sb("arena", [128, 20224], F32)

        WIN = arena[:, 0:12288].bitcast(BF16).rearrange("p (c n) -> p c n", c=8)
        XTC = arena[:, 12288:14336].bitcast(BF16).rearrange("p (c n) -> p c n", c=8)
        XSTs = [arena[:, 14336:14848].bitcast(BF16), arena[:, 19712:20224].bitcast(BF16)]
        LTs = [[arena[:, 14848 + 1536 * s_ + 256 * k: 14848 + 1536 * s_ + 256 * (k + 1)] for k in range(6)] for s_ in range(2)]
        LT2s = [arena[:, 14848 + 1536 * s_ + 256: 14848 + 1536 * s_ + 768] for s_ in range(2)]
        SIGT = arena[:, 17920:18432]
        STG = [arena[:, 18432:18944], arena[:, 18944:19456]]
        UBs = [arena[:, 19456:19584].bitcast(BF16), arena[:, 19584:19712].bitcast(BF16)]
        ACC = arena[:, 0:4096].rearrange("p (a n) -> p a n", a=2)
        QZ = arena[:, 4096:6144].bitcast(BF16).rearrange("p (a n) -> p a n", a=2)
        PTt = [arena[:, 6144 + 256 * i: 6144 + 256 * (i + 1)].bitcast(BF16).rearrange("p (a n) -> p a n", a=2)
               for i in range(2)]
        VG = [arena[:, 6656 + 64 * i: 6656 + 64 * (i + 1)].bitcast(BF16) for i in range(2)]
        RD = arena[:, 7168:9216]
        WOUT = arena[:, 9216:13312].bitcast(BF16).rearrange("p (c n) -> p c n", c=8)
        SKS = [arena[:, 13312 + 1024 * i: 13312 + 1024 * (i + 1)].bitcast(BF16).rearrange("p (t c) -> p t c", t=4) for i in range(2)]
        SVS = [arena[:, 15360 + 1024 * i: 15360 + 1024 * (i + 1)].bitcast(BF16).rearrange("p (t c) -> p t c", t=4) for i in range(2)] + \
              [arena[:, 9216 + 1024 * i: 9216 + 1024 * (i + 1)].bitcast(BF16).rearrange("p (t c) -> p t c", t=4) for i in range(2)]
        SQB = arena[:, 17408:18432].bitcast(BF16).rearrange("p (t c) -> p t c", t=4)
        SPR = arena[:, 18432:19456].bitcast(BF16).rearrange("p (t c) -> p t c", t=4)
        XR = [arena[:, 13312 + 1024 * i: 13312 + 1024 * (i + 1)] for i in range(2)]
        ZT = [arena[:, 15360 + 1024 * i: 15360 + 1024 * (i + 1)] for i in range(2)]
        LNG = arena[:, 17408:18432]
        LNB = arena[:, 18432:19456]
        STAT = arena[:, 19456:19520]
        PKEYS = ["XTC", "XST0", "XST1", "UB0", "UB1", "SIGT", "STG0", "STG1"] + ["L%d_%d" % (a_, k_) for a_ in range(2) for k_ in range(6)]
        SKEYS = ["SKS0", "SKS1", "SVS0", "SVS1", "SQB", "SPR0", "SPR1", "SPR2", "SPR3"]
        OKEYS = ["XR0", "XR1", "ZT0", "ZT1", "LNG", "LNB", "STAT0", "STAT1"]

        banks = [psb("bank%d" % i, [128, 512], F32) for i in range(8)]
        banks_bf = [bk[:, :].bitcast(BF16) for bk in banks]

        def bkey(i):
            return "bank%d" % i

        P.add("pool", lambda e: e.dma_start(out=WIN, in_=w_in.rearrange("(c p) n -> p c n", p=128)), writes=["WIN"], dma=True)
        P.add("pool", lambda e: e.dma_start(out=ident[:], in_=ident_d), writes=["ident"], dma=True)
        P.add("pool", lambda e: e.dma_start(out=mask_own[:], in_=mask_own_d), writes=["mask_own"], dma=True)
        P.add("pool", lambda e: e.dma_start(out=mask_ctx[:], in_=mask_ctx_d), writes=["mask_ctx"], dma=True)
        P.add("pool", lambda e: e.dma_start(out=wra[:], in_=wra_d), writes=["wra"], dma=True)
        P.add("pool", lambda e: e.dma_start(out=wri[:], in_=wri_d), writes=["wri"], dma=True)
        P.add("sp", lambda e: e.dma_start(out=prm[:], in_=prm_d), writes=["prm"], dma=True)
        P.add("sp", lambda e: e.dma_start(out=flag[:], in_=flag_d), writes=["flag"], dma=True)
        P.add("dve", lambda e: e.memset(ones[:], 1.0), writes=["ones"])
        P.add("dve", lambda e: e.memset(zcol[:], 0.0), writes=["zcol"])
        P.add("dve", lambda e: e.memset(hstate[:], 0.0), writes=["hs0", "hs1", "hs2", "hs3"])
        P.add("dve", lambda e: e.memset(HALO[:], 0.0), writes=["HALO0", "HALO1", "HALO2", "HALO3"])
        P.add("act", lambda e: e.activation(out=cneg[:, 0:4], in_=prm[:, 28:32], func=AF.Exp, scale=-1.0), reads=["prm"], writes=["cneg"])
        P.add("act", lambda e: e.activation(out=cneg[:, 0:4], in_=cneg[:, 0:4], func=AF.Ln, bias=1.0), reads=["cneg"], writes=["cneg"])
        P.add("dve", lambda e: e.tensor_scalar(out=cneg[:, 4:8], in0=cneg[:, 0:4], scalar1=-16.0, scalar2=None, op0=ALU.mult), reads=["cneg"], writes=["cneg"])
        P.add("dve", lambda e: e.tensor_scalar(out=cneg[:, 0:4], in0=cneg[:, 0:4], scalar1=-8.0, scalar2=None, op0=ALU.mult), reads=["cneg"], writes=["cneg"])

        rr = {"ev": 0, "stg": 0, "xst": 0}

        def next_bank():
            b_ = 4 + (rr["ev"] % 4)
            rr["ev"] += 1
            return b_

        def next_stg():
            i = rr["stg"] % 2
            rr["stg"] += 1
            return STG[i], "STG%d" % i

        def lk(s_, k):
            return "L%d_%d" % (s_, k)

        GBANKS = [2, 3]

        def gates_s1a(fc, n, s_):
            u = LTs[s_][0][:, 0:n]
            ub = UBs[s_][:, 0:n]
            gb = GBANKS[s_]
            gr, gi = banks[gb][:, 0:256], banks[gb][:, 256:512]
            P.add("dve", lambda e: e.tensor_copy(out=ub, in_=u), reads=[lk(s_, 0)], writes=["UB%d" % s_])
            P.add("pe", lambda e: e.matmul(gr[:, 0:n], lhsT=wra[:, fc, :], rhs=ub, start=True, stop=True, skip_group_check=True), reads=["wra", "UB%d" % s_], writes=[bkey(gb)])
            P.add("pe", lambda e: e.matmul(gi[:, 0:n], lhsT=wri[:, fc, :], rhs=ub, start=False, stop=True, skip_group_check=True), reads=["wri", "UB%d" % s_], writes=[bkey(gb)])

        def gates_s1b(fc, n, s_):
            gb = GBANKS[s_]
            gr, gi = banks[gb][:, 0:256], banks[gb][:, 256:512]
            er, ei, a = LTs[s_][1][:, 0:n], LTs[s_][2][:, 0:n], LTs[s_][4][:, 0:n]
            P.add("act", lambda e: e.activation(out=er, in_=gr[:, 0:n], func=AF.Exp, scale=-1.0, bias=prm[:, 20 + fc:21 + fc]), reads=[bkey(gb), "prm"], writes=[lk(s_, 1)])
            P.add("act", lambda e: e.activation(out=ei, in_=gi[:, 0:n], func=AF.Exp, scale=-1.0, bias=prm[:, 24 + fc:25 + fc]), reads=[bkey(gb), "prm"], writes=[lk(s_, 2)])
            if n == 256:
                both = LT2s[s_]
                P.add("act", lambda e: e.activation(out=both, in_=both, func=AF.Ln, bias=1.0), reads=[lk(s_, 1), lk(s_, 2)], writes=[lk(s_, 1), lk(s_, 2)])
                P.add("act", lambda e: e.activation(out=both, in_=both, func=AF.Exp, scale=-1.0), reads=[lk(s_, 1), lk(s_, 2)], writes=[lk(s_, 1), lk(s_, 2)])
            else:
                P.add("act", lambda e: e.activation(out=er, in_=er, func=AF.Ln, bias=1.0), reads=[lk(s_, 1)], writes=[lk(s_, 1)])
                P.add("act", lambda e: e.activation(out=ei, in_=ei, func=AF.Ln, bias=1.0), reads=[lk(s_, 2)], writes=[lk(s_, 2)])
                P.add("act", lambda e: e.activation(out=er, in_=er, func=AF.Exp, scale=-1.0), reads=[lk(s_, 1)], writes=[lk(s_, 1)])
                P.add("act", lambda e: e.activation(out=ei, in_=ei, func=AF.Exp, scale=-1.0), reads=[lk(s_, 2)], writes=[lk(s_, 2)])
            P.add("act", lambda e: e.activation(out=a, in_=er, func=AF.Exp, scale=cneg[:, fc:fc + 1]), reads=[lk(s_, 1), "cneg"], writes=[lk(s_, 4)])

        def gates_s2(fc, n, s_):
            u = LTs[s_][0][:, 0:n]
            ei, a, t1, bx = LTs[s_][2][:, 0:n], LTs[s_][4][:, 0:n], LTs[s_][3][:, 0:n], LTs[s_][5][:, 0:n]
            P.add("dve", lambda e: e.scalar_tensor_tensor(out=t1, in0=a, scalar=-1.0, in1=a, op0=ALU.mult, op1=ALU.mult), reads=[lk(s_, 4)], writes=[lk(s_, 3)])
            P.add("dve", lambda e: e.tensor_tensor(out=bx, in0=ei, in1=u, op=ALU.mult), reads=[lk(s_, 2), lk(s_, 0)], writes=[lk(s_, 5)])
            P.add("act", lambda e: e.activation(out=t1, in_=t1, func=AF.Ln, bias=1.0), reads=[lk(s_, 3)], writes=[lk(s_, 3)])
            P.add("act", lambda e: e.activation(out=t1, in_=t1, func=AF.Exp, scale=0.5), reads=[lk(s_, 3)], writes=[lk(s_, 3)])
            P.add("dve", lambda e: e.tensor_tensor(out=bx, in0=bx, in1=t1, op=ALU.mult), reads=[lk(s_, 5), lk(s_, 3)], writes=[lk(s_, 5)])
            return a, bx

        def sigmoid_times(src_bank, n, out_ap, out_key):
            eg = SIGT[:, 0:n]
            gp = banks[src_bank][:, 0:n]
            P.add("act", lambda e: e.activation(out=eg, in_=gp, func=AF.Exp, scale=-1.0), reads=[bkey(src_bank)], writes=["SIGT"])
            P.add("act", lambda e: e.activation(out=eg, in_=eg, func=AF.Ln, bias=1.0), reads=["SIGT"], writes=["SIGT"])
            P.add("act", lambda e: e.activation(out=eg, in_=eg, func=AF.Exp, scale=-1.0), reads=["SIGT"], writes=["SIGT"])
            P.add("dve", lambda e: e.tensor_tensor(out=out_ap, in0=eg, in1=gp, op=ALU.mult), reads=["SIGT", bkey(src_bank)], writes=[out_key])

        def proj(col0, n, xt_ap, bank, xkey="XTC"):
            def one(kc):
                P.add("pe", lambda e: e.matmul(banks[bank][:, 0:n], lhsT=WIN[:, kc, col0:col0 + 128], rhs=xt_ap[:, kc, :],
                                               start=(kc == 0), stop=(kc == 7)), reads=["WIN", xkey], writes=[bkey(bank)])
            for kc in range(8):
                one(kc)

        def load_xblock(src_rows, nrow, dst_ap, dst_key):
            xi = rr["xst"] % 2
            rr["xst"] += 1
            xst = XSTs[xi]
            xk = "XST%d" % xi
            P.add("pool", lambda e: e.dma_start(out=xst[0:nrow, :], in_=src_rows), writes=[xk], dma=True)
            pt = banks_bf[xi]

            def one(kc):
                P.add("pe", lambda e: e.transpose(pt[:, kc * nrow:(kc + 1) * nrow], xst[0:nrow, kc * 128:(kc + 1) * 128], ident[0:nrow, 0:nrow]),
                      reads=[xk, "ident"], writes=[bkey(xi)])
            for kc in range(8):
                one(kc)
            srcv = pt[:, 0:8 * nrow].rearrange("p (c n) -> p c n", c=8)
            P.add("dve", lambda e: e.tensor_copy(out=dst_ap, in_=srcv), reads=[bkey(xi)], writes=[dst_key])

        def kv_proj(tc, which, pr):
            own = tc >= 4
            T0 = tc * 512
            o0 = T0 - 2048
            dstT = KT if which == 0 else VT
            dkey = "KT" if which == 0 else "VT"
            dout_ap = kT_d if which == 0 else vT_d
            bank = next_bank()
            proj(512 + which * 512 + pr * 128, 512, XTC, bank)
            if own:
                stg, sk = next_stg()
                P.add("act", lambda e: e.activation(out=stg, in_=banks[bank][:, :], func=AF.Copy), reads=[bkey(bank)], writes=[sk])
                P.add("sp", lambda e: e.dma_start(out=dout_ap[pr * 128:(pr + 1) * 128, o0:o0 + 512], in_=stg), reads=[sk], dma=True)
                P.add("dve", lambda e: e.tensor_copy(out=dstT[:, pr, T0:T0 + 512], in_=banks[bank][:, :]), reads=[bkey(bank)], writes=[dkey])
            else:
                P.add("act", lambda e: e.activation(out=dstT[:, pr, T0:T0 + 512], in_=banks[bank][:, :], func=AF.Copy), reads=[bkey(bank)], writes=[dkey])

        def q_proj(tc, pr):
            o0 = tc * 512 - 2048
            bank = next_bank()
            proj(pr * 128, 512, XTC, bank)
            P.add("act", lambda e: e.activation(out=QT[:, pr, o0:o0 + 512], in_=banks[bank][:, :], func=AF.Copy), reads=[bkey(bank)], writes=["QT"])

        def ga_proj(tc, pr):
            o0 = tc * 512 - 2048
            bank = next_bank()
            proj(1536 + pr * 128, 512, XTC, bank)
            sigmoid_times(bank, 512, GA[:, pr, o0:o0 + 512], "GA")

        item_ctr = [0]

        def lru_item(tc, fc, h, fillers):
            own = tc >= 4
            o0 = tc * 512 - 2048
            c0 = 256 * h
            s_ = h
            hk = "hs%d" % fc
            hak = "HALO%d" % fc
            u = LTs[s_][0][:, :]
            tmp = LTs[s_][5][:, :]
            cw = lambda k: prm[:, fc * 4 + k: fc * 4 + k + 1]

            def stage1a():
                if h == 0:
                    bank = next_bank()
                    proj(2048 + fc * 128, 512, XTC, bank)
                    P.add("act", lambda e: e.activation(out=XL[:, 3:515], in_=banks[bank][:, :], func=AF.Copy), reads=[bkey(bank)], writes=["XL"])
                    P.add("dve", lambda e: e.tensor_copy(out=XL[:, 0:3], in_=HALO[:, fc, :]), reads=[hak], writes=["XL"])
                P.add("dve", lambda e: e.tensor_scalar(out=u, in0=XL[:, c0:c0 + 256], scalar1=cw(0), scalar2=prm[:, 16 + fc:17 + fc],
                                                        op0=ALU.mult, op1=ALU.add), reads=["XL", "prm"], writes=[lk(s_, 0)])

                def tap(k):
                    P.add("dve", lambda e: e.scalar_tensor_tensor(out=u, in0=XL[:, c0 + k:c0 + k + 256], scalar=cw(k), in1=u, op0=ALU.mult, op1=ALU.add),
                          reads=["XL", lk(s_, 0), "prm"], writes=[lk(s_, 0)])
                for k in range(1, 4):
                    tap(k)
                if h == 1:
                    P.add("dve", lambda e: e.tensor_copy(out=HALO[:, fc, :], in_=XL[:, 512:515]), reads=["XL"], writes=[hak])
                gates_s1a(fc, 256, s_)

            def stage1b():
                gates_s1b(fc, 256, s_)
                for f in fillers:
                    f()

            def stage2():
                a, bx = gates_s2(fc, 256, s_)
                hseq = LTs[s_][1][:, :]
                if h == 0:
                    init_ap, init_key = hstate[:, fc:fc + 1], hk
                else:
                    init_ap, init_key = LTs[0][1][:, 255:256], lk(0, 1)
                P.add("dve", lambda e: e.tensor_tensor_scan(out=hseq, data0=a, data1=bx, initial=init_ap, op0=ALU.mult, op1=ALU.add),
                      reads=[lk(s_, 4), lk(s_, 5), init_key], writes=[lk(s_, 1)])
                if h == 1:
                    if tc == 3:
                        P.add("dve", lambda e: e.tensor_tensor(out=hstate[:, fc:fc + 1], in0=hseq[:, 255:256], in1=flag[:, 0:1], op=ALU.mult),
                              reads=[lk(s_, 1), "flag"], writes=[hk])
                    else:
                        P.add("dve", lambda e: e.tensor_copy(out=hstate[:, fc:fc + 1], in_=hseq[:, 255:256]), reads=[lk(s_, 1)], writes=[hk])
                if own:
                    if h == 0:
                        bank2 = next_bank()
                        proj(2560 + fc * 128, 512, XTC, bank2)
                        sigmoid_times(bank2, 512, SIGG[:, :], "SIGG")
                    P.add("dve", lambda e: e.tensor_tensor(out=ML[:, fc, o0 + c0:o0 + c0 + 256], in0=SIGG[:, c0:c0 + 256], in1=hseq, op=ALU.mult),
                          reads=["SIGG", lk(s_, 1)], writes=["ML"])
            return stage1a, stage1b, stage2

        prev2 = None
        for tc in range(8):
            srcx = xo if tc >= 4 else xc
            for tb in range(4):
                r0 = (tc % 4) * 512 + tb * 128
                load_xblock(srcx[r0:r0 + 128, :], 128, XTC[:, :, tb * 128:(tb + 1) * 128], "XTC")
            others = [(lambda which=which, pr=pr, tc=tc: kv_proj(tc, which, pr)) for which in range(2) for pr in range(4)]
            if tc >= 4:
                others += [(lambda pr=pr, tc=tc: q_proj(tc, pr)) for pr in range(4)]
                others += [(lambda pr=pr, tc=tc: ga_proj(tc, pr)) for pr in range(4)]
            per = (len(others) + 7) // 8
            idx = 0
            for fc in range(4):
                for h in range(2):
                    s1a, s1b, s2 = lru_item(tc, fc, h, others[idx * per:(idx + 1) * per])
                    idx += 1
                    s1a()
                    if prev2 is not None:
                        prev2()
                    s1b()
                    prev2 = s2
        prev2()
        P.add("sp", lambda e: e.dma_start(out=convT_d, in_=HALO[:]), reads=["HALO0", "HALO1", "HALO2", "HALO3"], dma=True)
        P.add("sp", lambda e: e.dma_start(out=hT_d, in_=hstate[:]), reads=["hs0", "hs1", "hs2", "hs3"], dma=True)

        SXT = sb("SXT", [128, 8, NS], BF16)
        SGA = sb("SGA", [128, 4, NS], BF16)
        SML = sb("SML", [128, 4, NS], BF16)
        SMA = sb("SMA", [128, 4, NS], BF16)
        SXL = sb("SXL", [128, 4, NB, 7], F32)
        SH = sb("SH", [128, 4, NB], F32)
        SHO = sb("SHO", [128, 4, NB], F32)

        def s_qkv(j):
            bank = next_bank()

            def one(kc):
                P.add("pe", lambda e: e.matmul(banks[bank][0:NS, :], lhsT=SXT[:, kc, :], rhs=WIN[:, kc, j * 512:(j + 1) * 512],
                                               start=(kc == 0), stop=(kc == 7)), reads=["WIN", "SXT"], writes=[bkey(bank)])
            for kc in range(8):
                one(kc)
            stg_full, sk = next_stg()
            stg = stg_full[0:NS, :]
            P.add("act", lambda e: e.activation(out=stg, in_=banks[bank][0:NS, :], func=AF.Copy), reads=[bkey(bank)], writes=[sk])
            P.add("sp", lambda e: e.dma_start(out=(qs_d, ks_d, vs_d)[j], in_=stg), reads=[sk], writes=["scr%d" % j], dma=True)

        def s_ga(pr):
            bank = next_bank()
            proj(1536 + pr * 128, NS, SXT, bank, xkey="SXT")
            sigmoid_times(bank, NS, SGA[:, pr, :], "SGA")

        def s_lru(fc):
            n = NS
            bank = next_bank()
            proj(2048 + fc * 128, NS, SXT, bank, xkey="SXT")
            P.add("act", lambda e: e.activation(out=SXL[:, fc, :, 3:7], in_=banks[bank][:, 0:NS].rearrange("p (b t) -> p b t", t=4), func=AF.Copy),
                  reads=[bkey(bank)], writes=["SXL"])
            u = LTs[0][0][:, 0:n]
            u3 = u.rearrange("p (b t) -> p b t", t=4)
            cw = lambda k: prm[:, fc * 4 + k: fc * 4 + k + 1]
            P.add("dve", lambda e: e.tensor_scalar(out=u3, in0=SXL[:, fc, :, 0:4], scalar1=cw(0), scalar2=prm[:, 16 + fc:17 + fc],
                                                    op0=ALU.mult, op1=ALU.add), reads=["SXL", "prm"], writes=[lk(0, 0)])

            def tap(k):
                P.add("dve", lambda e: e.scalar_tensor_tensor(out=u3, in0=SXL[:, fc, :, k:k + 4], scalar=cw(k), in1=u3, op0=ALU.mult, op1=ALU.add),
                      reads=["SXL", lk(0, 0), "prm"], writes=[lk(0, 0)])
            for k in range(1, 4):
                tap(k)
            gates_s1a(fc, n, 0)
            gates_s1b(fc, n, 0)
            a, bx = gates_s2(fc, n, 0)
            a3 = a.rearrange("p (b t) -> p b t", t=4)
            b3 = bx.rearrange("p (b t) -> p b t", t=4)
            hh = LTs[0][1][:, 0:n]
            h3 = hh.rearrange("p (b t) -> p b t", t=4)

            def step(t):
                prev = SH[:, fc, :] if t == 0 else h3[:, :, t - 1]
                P.add("dve", lambda e: e.tensor_tensor(out=h3[:, :, t], in0=a3[:, :, t], in1=prev, op=ALU.mult), reads=[lk(0, 4), lk(0, 1), "SH"], writes=[lk(0, 1)])
                P.add("dve", lambda e: e.tensor_tensor(out=h3[:, :, t], in0=h3[:, :, t], in1=b3[:, :, t], op=ALU.add), reads=[lk(0, 5), lk(0, 1)], writes=[lk(0, 1)])
            for t in range(4):
                step(t)
            P.add("dve", lambda e: e.tensor_copy(out=SHO[:, fc, :], in_=h3[:, :, 3]), reads=[lk(0, 1)], writes=["SHO"])
            bank2 = next_bank()
            proj(2560 + fc * 128, NS, SXT, bank2, xkey="SXT")
            sigmoid_times(bank2, NS, SIGG[:, 0:NS], "SIGG")
            P.add("dve", lambda e: e.tensor_tensor(out=SML[:, fc, :], in0=SIGG[:, 0:NS], in1=hh, op=ALU.mult), reads=["SIGG", lk(0, 1)], writes=["SML"])

        if do_sample:
            load_xblock(xs_d, NS, SXT[:], "SXT")
            for j in range(3):
                s_qkv(j)
            for pr in range(4):
                s_ga(pr)
            P.add("sp", lambda e: e.dma_start(out=SXL[:, :, :, 0:3], in_=sconv_d), writes=["SXL"], dma=True)
            P.add("sp", lambda e: e.dma_start(out=SH[:], in_=sh_d), writes=["SH"], dma=True)
            for fc in range(4):
                s_lru(fc)
            P.add("sp", lambda e: e.dma_start(out=hs_d, in_=SHO[:]), reads=["SHO"], dma=True)
            P.add("sp", lambda e: e.dma_start(out=convs_d, in_=SXL[:, :, :, 4:7]), reads=["SXL"], dma=True)

        P.add("dve", lambda e: e.memset(STAT, 0.0), writes=PKEYS + SKEYS)

        P.add("dve", lambda e: e.memset(QZ, 0.0), reads=SKEYS[:1], writes=["QZ", "WIN", "SVS2", "SVS3"])
        unit_i = [0]

        sample_tasks = []
        task_ctr = [0]

        pipe = [None, None, None]

        def pop_sample_task(flush=False):
            nxt = sample_tasks.pop(0) if sample_tasks else None
            if pipe[2] is not None:
                pipe[2][3]()
            if pipe[1] is not None:
                pipe[1][2]()
            if pipe[0] is not None:
                pipe[0][1]()
            if nxt is not None:
                nxt[0]()
            pipe[2], pipe[1], pipe[0] = pipe[1], pipe[0], nxt

        pending_pieces = []

        def flush_pieces(nmax):
            for _ in range(min(nmax, len(pending_pieces))):
                pending_pieces.pop(0)()

        def maybe_sample_task():
            task_ctr[0] += 1
            if task_ctr[0] % 4 == 0 and (sample_tasks or any(p_ is not None for p_ in pipe)):
                flush_pieces(len(pending_pieces))
                pop_sample_task()
            else:
                flush_pieces(4)

        def att_unit(pr, pi, d, r, kb, nbq):
            is_ctx = kb == nbq - 1
            is_last = kb == 2 * nbq - 1
            u0 = 128 if is_ctx else 0
            n = 128 if (is_ctx or is_last) else 256
            ui = unit_i[0]
            unit_i[0] += 1
            sbank = 4 + ui % 2
            S = banks[sbank][:, :].rearrange("p (a n) -> p a n", a=2)
            msk = mask_ctx if is_ctx else mask_own
            kstart = r + 128 * d * kb
            keys = KT[:, pr, kstart: kstart + 127 * d + 1: d]
            qstart = r + d * (128 * kb + u0) - 2048
            qs_ap = QZ[:, :, qstart: qstart + (n - 1) * d + 1: d]
            vi = ui % 2
            vg = VG[vi]
            vbank = 6 + vi
            vps = banks_bf[vbank][:, 0:128]
            pt_t = PTt[ui % 2]
            pk = "PT%d" % (ui % 2)
            vk = "VG%d" % vi

            def stage1():
                P.add("pe", lambda e: e.transpose(vps, VT[:, pr, kstart: kstart + 127 * d + 1: d], ident[:]), reads=["VT", "ident"], writes=[bkey(vbank)])
                P.add("pe", lambda e: e.matmul(S[:, :, 0:n], lhsT=ident[:], rhs=msk[:, :, u0:u0 + n], start=True, stop=False),
                      reads=["ident", "mask_own", "mask_ctx"], writes=[bkey(sbank)])
                P.add("pe", lambda e: e.matmul(S[:, :, 0:n], lhsT=keys, rhs=qs_ap, start=False, stop=True), reads=["KT", "QZ"], writes=[bkey(sbank)])
                P.add("act", lambda e: e.activation(out=vg, in_=vps, func=AF.Copy), reads=[bkey(vbank)], writes=[vk])
                P.add("act", lambda e: e.activation(out=pt_t[:, :, 0:n], in_=S[:, :, 0:n], func=AF.Exp, scale=0.125), reads=[bkey(sbank)], writes=[pk])

            def pv(sub):
                qb = kb + (u0 // 128) + sub
                first = (qb == kb + 1)
                obank = qb % 2
                OD = banks[obank]
                c0 = sub * 128
                okey = bkey(obank)
                P.add("pe", lambda e: e.matmul(OD[0:64, 0:128], lhsT=vg[:, 0:64], rhs=pt_t[:, 0, c0:c0 + 128], start=first, stop=False, skip_group_check=True),
                      reads=[vk, pk], writes=[okey])
                P.add("pe", lambda e: e.matmul(OD[64:128, 0:128], lhsT=vg[:, 64:128], rhs=pt_t[:, 1, c0:c0 + 128], start=first, stop=False, skip_group_check=True),
                      reads=[vk, pk], writes=[okey])
                P.add("pe", lambda e: e.matmul(OD[0:64, 128:256], lhsT=ones[:, 0:64], rhs=pt_t[:, 0, c0:c0 + 128], start=False, stop=False, skip_group_check=True),
                      reads=["ones", pk], writes=[okey])
                P.add("pe", lambda e: e.matmul(OD[64:128, 128:256], lhsT=ones[:, 0:64], rhs=pt_t[:, 1, c0:c0 + 128], start=False, stop=False, skip_group_check=True),
                      reads=["ones", pk], writes=[okey])
                if not first:
                    t0 = r + d * 128 * qb - 2048
                    dst = ACC[:, :, t0: t0 + 127 * d + 1: d]
                    srcv = OD[:, 0:256].rearrange("p (a n) -> p a n", a=2)
                    if pi == 0:
                        P.add("act", lambda e: e.activation(out=dst, in_=srcv, func=AF.Copy), reads=[okey], writes=["ACC"])
                    else:
                        P.add("dve", lambda e: e.tensor_tensor(out=dst, in0=srcv, in1=dst, op=ALU.add), reads=[okey, "ACC"], writes=["ACC"])

            def stage2():
                for sub in range(n // 128):
                    pv(sub)
            return stage1, stage2

        def att_pair(pr):
            P.add("act", lambda e: e.activation(out=QZ[0:64, 0, :], in_=QT[0:64, pr, :], func=AF.Copy), reads=["QT"], writes=["QZ"])
            P.add("act", lambda e: e.activation(out=QZ[64:128, 1, :], in_=QT[64:128, pr, :], func=AF.Copy), reads=["QT"], writes=["QZ"])
            units = []
            for pi, d in enumerate(PATTERNS):
                nbq = NTOK // (128 * d)
                for r in range(d):
                    for kb in range(nbq - 1, 2 * nbq):
                        units.append((pr, pi, d, r, kb, nbq))
            prev2 = None
            for uargs in units:
                s1, s2 = att_unit(*uargs)
                s1()
                if prev2 is not None:
                    prev2()
                prev2 = s2
                maybe_sample_task()
            prev2()
            P.add("act", lambda e: e.activation(out=RD, in_=ACC[:, 1, :], func=AF.Ln), reads=["ACC"], writes=["RD"])
            P.add("act", lambda e: e.activation(out=RD, in_=RD, func=AF.Exp, scale=-1.0), reads=["RD"], writes=["RD"])
            P.add("dve", lambda e: e.tensor_tensor(out=RD, in0=RD, in1=ACC[:, 0, :], op=ALU.mult), reads=["RD", "ACC"], writes=["RD"])
            P.add("dve", lambda e: e.tensor_tensor(out=GA[:, pr, :], in0=RD, in1=GA[:, pr, :], op=ALU.mult), reads=["RD", "GA"], writes=["GA"])

        KN = sb("KN", [4, 512], BF16)
        VN = sb("VN", [4, 512], BF16)
        SC = sb("SC", [128, 4, 32], F32)
        PS = sb("PS", [128, 4, 32], BF16)
        M1 = sb("M1", [128, 32], F32)
        WN = sb("WN", [4, 32], F32)
        OSEL = sb("OSEL", [128, 2, 16], F32)
        qs_flat = qs_d.rearrange("(b t) c -> b (t c)", t=4)
        sbuf_i = [0]

        def sample_b_tasks(b):
            obank = 2 + b % 2
            OD = banks[obank]
            okey = bkey(obank)
            firstmm = [True]

            def mm(out_ap, lhsT, rhs, rd):
                f = firstmm[0]
                firstmm[0] = False
                P.add("pe", lambda e: e.matmul(out_ap, lhsT=lhsT, rhs=rhs, start=f, stop=False, skip_group_check=True), reads=rd, writes=[okey])

            def slot_fns(slot):
                st_ = {}
                psk = "PS%d" % slot

                def stage_a0():
                    if slot == 0:
                        P.add("pool", lambda e: e.dma_start(out=SQB.rearrange("p t c -> p (t c)"), in_=qs_flat[b:b + 1, :].broadcast_to([128, 2048])),
                              reads=["scr0"], writes=["SQB"], dma=True)
                    if slot == 3:
                        P.add("pool", lambda e: e.dma_start(out=KN[:], in_=ks_d[4 * b:4 * b + 4, :]), reads=["scr1"], writes=["KN"], dma=True)
                        P.add("pool", lambda e: e.dma_start(out=VN[:], in_=vs_d[4 * b:4 * b + 4, :]), reads=["scr2"], writes=["VN"], dma=True)
                    if slot < 3:
                        ki = sbuf_i[0] % 2
                        vi_ = sbuf_i[0] % 4
                        sbuf_i[0] += 1
                        ks, vs = SKS[ki], SVS[vi_]
                        kk, vk = "SKS%d" % ki, "SVS%d" % vi_
                        if slot == 0:
                            ksrc = ck_d[b, :, :].rearrange("(i t) c -> i t c", t=16)[:, 0:4, :]
                            vsrc = cv_d[b, :, :].rearrange("(i t) c -> i t c", t=16)[:, 0:4, :]
                            kdst, vdst = ks, vs
                        elif slot == 1:
                            ksrc = ck_d[b, 1536:2048, :].rearrange("(i t) c -> i t c", t=4)
                            vsrc = cv_d[b, 1536:2048, :].rearrange("(i t) c -> i t c", t=4)
                            kdst, vdst = ks, vs
                        else:
                            ksrc = ck_d[b, 1920:2048, :]
                            vsrc = cv_d[b, 1920:2048, :]
                            kdst, vdst = ks[:, 0, :], vs[:, 0, :]
                        P.add("pool", lambda e: e.dma_start(out=kdst, in_=ksrc), writes=[kk], dma=True)
                        P.add("pool", lambda e: e.dma_start(out=vdst, in_=vsrc), writes=[vk], dma=True)
                        st_["kin"] = ks if slot < 2 else ks[:, 0:1, :].broadcast_to([128, 4, 512])
                        st_["kk"] = kk
                        st_["npart"] = 128
                        st_["vs"], st_["vk"] = vs, vk
                    else:
                        st_["kin"] = KN[:].unsqueeze(1).broadcast_to([4, 4, 512])
                        st_["kk"] = "KN"
                        st_["npart"] = 4

                def stage_a1():
                    kin, kk, npart = st_["kin"], st_["kk"], st_["npart"]

                    def mpiece(t):
                        P.add("dve", lambda e: e.tensor_tensor(out=SPR[0:npart, t, :], in0=kin[:, t, :], in1=SQB[0:npart, t, :], op=ALU.mult),
                              reads=[kk, "SQB"], writes=["SPR%d" % t])

                    def rpiece(t):
                        P.add("dve", lambda e: e.tensor_reduce(out=SC[0:npart, slot, t * 8:(t + 1) * 8],
                                                               in_=SPR[0:npart, t, :].rearrange("p (h d) -> p h d", d=64),
                                                               axis=AX.X, op=ALU.add), reads=["SPR%d" % t], writes=["SC%d" % slot])
                    if npart == 4:
                        for t in range(4):
                            mpiece(t)
                            rpiece(t)
                    else:
                        for t in range(4):
                            pending_pieces.append(lambda t=t: mpiece(t))
                            pending_pieces.append(lambda t=t: rpiece(t))

                def stage_a2():
                    npart = st_["npart"]
                    sck = "SC%d" % slot
                    P.add("act", lambda e: e.activation(out=SC[0:npart, slot, :], in_=SC[0:npart, slot, :], func=AF.Exp, scale=0.125), reads=[sck], writes=[sck])
                    if slot == 2:
                        P.add("dve", lambda e: e.tensor_tensor(out=PS[:, 2, :], in0=SC[:, 2, :], in1=M1[:], op=ALU.mult), reads=[sck, "M1"], writes=[psk])
                    elif slot == 3:
                        P.add("dve", lambda e: e.tensor_tensor(out=PS[0:4, 3, :], in0=SC[0:4, 3, :], in1=WN[:], op=ALU.mult), reads=[sck, "WN"], writes=[psk])
                    else:
                        P.add("dve", lambda e: e.tensor_copy(out=PS[0:npart, slot, :], in_=SC[0:npart, slot, :]), reads=[sck], writes=[psk])

                def stage_b():
                    for t in range(4):
                        for pr in range(4):
                            oc = (pr * 4 + t) * 2
                            c = t * 8 + 2 * pr
                            if slot < 2:
                                mm(OD[:, oc:oc + 2], st_["vs"][:, t, pr * 128:(pr + 1) * 128], PS[:, slot, c:c + 2], [st_["vk"], psk])
                            elif slot == 2:
                                mm(OD[:, oc:oc + 2], st_["vs"][:, 0, pr * 128:(pr + 1) * 128], PS[:, 2, c:c + 2], [st_["vk"], psk])
                            else:
                                mm(OD[:, oc:oc + 2], VN[:, pr * 128:(pr + 1) * 128], PS[0:4, 3, c:c + 2], ["VN", psk])
                    if slot < 3:
                        mm(OD[:, 32:64], ones[:, :], PS[:, slot, :], ["ones", psk])
                    else:
                        mm(OD[:, 32:64], ones[0:4, :], PS[0:4, 3, :], ["ones", psk])
                        epilogue()
                return stage_a0, stage_a1, stage_a2, stage_b

            def epilogue():
                num = OD[:, 0:32].rearrange("p (q t two) -> p q t two", q=4, t=4)
                den = OD[:, 32:64].rearrange("p (t q two) -> p q t two", q=4, two=2)
                osel = OSEL[:].rearrange("p a (q t) -> p a q t", q=4)

                def sel_n(hb, lo, hi):
                    P.add("dve", lambda e: e.tensor_copy(out=osel[lo:hi, 0], in_=num[lo:hi, :, :, hb]), reads=[okey], writes=["OSEL"])

                def sel_d(hb, lo, hi):
                    P.add("dve", lambda e: e.tensor_copy(out=osel[lo:hi, 1], in_=den[lo:hi, :, :, hb]), reads=[okey], writes=["OSEL"])

                def fin1():
                    P.add("dve", lambda e: e.reciprocal(out=OSEL[:, 1, :], in_=OSEL[:, 1, :]), reads=["OSEL"], writes=["OSEL"])

                def fin2():
                    P.add("dve", lambda e: e.tensor_tensor(out=OSEL[:, 0, :], in0=OSEL[:, 0, :], in1=OSEL[:, 1, :], op=ALU.mult), reads=["OSEL"], writes=["OSEL"])

                def fin3():
                    P.add("dve", lambda e: e.tensor_tensor(out=SMA[:, :, 4 * b:4 * b + 4], in0=osel[:, 0], in1=SGA[:, :, 4 * b:4 * b + 4], op=ALU.mult),
                          reads=["OSEL", "SGA"], writes=["SMA"])
                pending_pieces.extend([lambda: sel_n(0, 0, 64), lambda: sel_d(0, 0, 64), lambda: sel_n(1, 64, 128), lambda: sel_d(1, 64, 128),
                                       fin1, fin2, fin3])

            return [slot_fns(slot) for slot in range(4)]

        if do_sample:
            P.add("sp", lambda e: e.dma_start(out=M1[:], in_=m1_d), writes=["M1"], dma=True)
            P.add("sp", lambda e: e.dma_start(out=WN[:], in_=wnew_d), writes=["WN"], dma=True)
            P.add("dve", lambda e: e.memset(SC[:], 0.0), writes=["SC0", "SC1", "SC2", "SC3"])
        if do_sample:
            for b in range(NB):
                sample_tasks.extend(sample_b_tasks(b))
        for pr in range(4):
            att_pair(pr)
        while sample_tasks or any(p_ is not None for p_ in pipe):
            flush_pieces(len(pending_pieces))
            pop_sample_task()
        flush_pieces(len(pending_pieces))
        P.add("pool", lambda e: e.dma_start(out=WOUT, in_=w_out.rearrange("(c p) n -> p c n", p=128)), writes=["WOUT", "SVS2", "SVS3"], dma=True)

        XR3 = [arena[:, 1024 * i: 1024 * (i + 1)] for i in range(3)]
        ZT3 = [arena[:, 3072 + 1024 * i: 3072 + 1024 * (i + 1)] for i in range(3)]
        O3KEYS = ["XR%d" % i for i in range(3)] + ["ZT%d" % i for i in range(3)]
        P.add("dve", lambda e: e.memset(STAT, 0.0), writes=SKEYS + OKEYS + O3KEYS + ["ACC", "QZ", "PT0", "PT1", "VG0", "VG1", "RD"])
        P.add("sp", lambda e: e.dma_start(out=LNG, in_=lng_d), writes=["LNG"], dma=True)
        P.add("sp", lambda e: e.dma_start(out=LNB, in_=lnb_d), writes=["LNB"], dma=True)

        def out_block(i, nrow, xrows, mixT, ydst, mkeys):
            xr = XR3[i % 3][0:nrow, :]
            z = ZT3[i % 3][0:nrow, :]
            xk, zk, sk = "XR%d" % (i % 3), "ZT%d" % (i % 3), "STAT%d" % (i % 2)
            so = 16 * (i % 2)
            st6 = STAT[0:nrow, so:so + 12].rearrange("p (c s) -> p c s", c=2)
            mv = STAT[0:nrow, so + 12:so + 14]
            rstd = STAT[0:nrow, so + 14:so + 15]

            def stage1():
                def half_fn(half):
                    bank = 2 + 2 * (i % 3) + half

                    def one(kc):
                        P.add("pe", lambda e: e.matmul(banks[bank][0:nrow, :], lhsT=mixT(kc), rhs=WOUT[:, kc, half * 512:(half + 1) * 512],
                                                       start=(kc == 0), stop=(kc == 7)), reads=mkeys + ["WOUT"], writes=[bkey(bank)])
                    for kc in range(8):
                        one(kc)
                    P.add("dve", lambda e: e.scalar_tensor_tensor(out=z[:, half * 512:(half + 1) * 512], in0=xr[:, half * 512:(half + 1) * 512],
                                                                  scalar=ALPHA, in1=banks[bank][0:nrow, :], op0=ALU.mult, op1=ALU.add),
                          reads=[bkey(bank), xk], writes=[zk])
                    P.add("dve", lambda e: e.bn_stats(out=st6[:, half, :], in_=z[:, half * 512:(half + 1) * 512]), reads=[zk], writes=[sk])
                half_fn(0)
                half_fn(1)
                P.add("dve", lambda e: e.bn_aggr(out=mv, in_=st6), reads=[sk], writes=[sk])
                P.add("dve", lambda e: e.tensor_scalar(out=rstd, in0=mv[:, 1:2], scalar1=LN_EPS, scalar2=None, op0=ALU.add), reads=[sk], writes=[sk])
                P.add("act", lambda e: e.activation(out=rstd, in_=rstd, func=AF.Ln), reads=[sk], writes=[sk])
                P.add("act", lambda e: e.activation(out=rstd, in_=rstd, func=AF.Exp, scale=-0.5), reads=[sk], writes=[sk])

            def xload():
                P.add("sp", lambda e: e.dma_start(out=xr, in_=xrows), writes=[xk], dma=True)

            def stage2():
                P.add("dve", lambda e: e.tensor_scalar(out=z, in0=z, scalar1=mv[:, 0:1], scalar2=rstd, op0=ALU.subtract, op1=ALU.mult), reads=[zk, sk], writes=[zk])
                P.add("pool", lambda e: e.tensor_tensor(out=z, in0=z, in1=LNG[0:nrow, :], op=ALU.mult), reads=[zk, "LNG"], writes=[zk])

            def stage3():
                P.add("dve", lambda e: e.tensor_tensor(out=z, in0=z, in1=LNB[0:nrow, :], op=ALU.add), reads=[zk, "LNB"], writes=[zk])
                P.add("pool", lambda e: e.dma_start(out=ydst, in_=z), reads=[zk], dma=True)
            return stage1, stage2, stage3, xload

        def prompt_block(tb):
            def mixT(kc):
                src = GA if kc < 4 else ML
                return src[:, kc % 4, tb * 128:(tb + 1) * 128]
            return out_block(tb, 128, xo[tb * 128:(tb + 1) * 128, :], mixT, y_d[tb * 128:(tb + 1) * 128, :], ["GA", "ML"])

        blocks = [prompt_block(tb) for tb in range(16)]
        if do_sample:
            def mixTs(kc):
                src = SMA if kc < 4 else SML
                return src[:, kc % 4, :]
            blocks.append(out_block(16, NS, xs_d, mixTs, ys_d, ["SMA", "SML"]))
        nblk = len(blocks)
        blocks[0][3]()
        for j in range(nblk + 2):
            if j + 1 < nblk:
                blocks[j + 1][3]()
            if j < nblk:
                blocks[j][0]()
            if 0 <= j - 1 < nblk:
                blocks[j - 1][1]()
            if 0 <= j - 2 < nblk:
                blocks[j - 2][2]()

        P.emit(nc)
    return nc


def _band_mask():
    s = np.arange(128)[:, None]
    u = np.arange(256)[None, :]
    ok = (u - s >= 0) & (u - s <= 128)
    m = np.where(ok, 0.0, NEG).astype(np.float32)
    return np.ascontiguousarray(np.broadcast_to(m[:, None, :], (128, 2, 256)))


def _fm(v):
    return np.ascontiguousarray(np.asarray(v, np.float32).reshape(4, 128).T)


_NC_CACHE = {}


def kernel(x_prompt, x_sample, cache_k, cache_v, state_conv, state_h, w_in, conv_w, conv_b,
           w_ra, b_ra, w_ri, b_ri, lru_lambda, w_out, ln_g, ln_b):
    f32 = np.float32
    x_prompt = np.asarray(x_prompt, f32)
    x_sample = np.asarray(x_sample, f32)
    cache_k = np.asarray(cache_k, f32)
    cache_v = np.asarray(cache_v, f32)
    state_conv = np.asarray(state_conv, f32)
    state_h = np.asarray(state_h, f32)
    w_in0 = np.ascontiguousarray(np.asarray(w_in, f32)[0])
    w_out0 = np.ascontiguousarray(np.asarray(w_out, f32)[0])
    conv_w = np.asarray(conv_w, f32)[0]
    w_ra = np.asarray(w_ra, f32)[0]
    w_ri = np.asarray(w_ri, f32)[0]

    if "nc" not in _NC_CACHE:
        _NC_CACHE["nc"] = build_program(True)
    nc = _NC_CACHE["nc"]

    prm = np.zeros((128, 32), f32)
    for fc in range(4):
        for k in range(4):
            prm[:, fc * 4 + k] = conv_w[k, fc * 128:(fc + 1) * 128]
    prm[:, 16:20] = _fm(np.asarray(conv_b, f32)[0])
    prm[:, 20:24] = -_fm(np.asarray(b_ra, f32)[0])
    prm[:, 24:28] = -_fm(np.asarray(b_ri, f32)[0])
    prm[:, 28:32] = _fm(np.asarray(lru_lambda, f32)[0])

    def bd(w):
        o = np.zeros((128, 4, 128), f32)
        for fc in range(4):
            o[0:64, fc, 0:64] = w[2 * fc]
            o[64:128, fc, 64:128] = w[2 * fc + 1]
        return o

    wra_bd, wri_bd = bd(w_ra), bd(w_ri)
    lng = np.ascontiguousarray(np.broadcast_to(np.asarray(ln_g, f32)[0][None, :], (128, 1024)))
    lnb = np.ascontiguousarray(np.broadcast_to(np.asarray(ln_b, f32)[0][None, :], (128, 1024)))
    ident = np.eye(128, dtype=f32)
    band = _band_mask()
    allneg = np.full((128, 2, 256), NEG, f32)
    m1 = (np.arange(128)[:, None, None] >= np.arange(4)[None, :, None]).astype(f32)
    m1 = np.ascontiguousarray(np.broadcast_to(m1, (128, 4, 8)).reshape(128, 32))
    wn = np.zeros((4, 4), f32)
    for tp in range(4):
        for t in range(4):
            wn[tp, t] = 3.0 if tp == t else (1.0 if tp < t else 0.0)
    wnew = np.ascontiguousarray(np.broadcast_to(wn[:, :, None], (4, 4, 8)).reshape(4, 32))

    in_maps = []
    for c in range(8):
        b, h = c // 2, c % 2
        xo = np.ascontiguousarray(x_prompt[b, h * 2048:(h + 1) * 2048])
        xc = np.ascontiguousarray(x_prompt[b, 0:2048]) if h == 1 else np.zeros((2048, 1024), f32)
        sl = slice(16 * c, 16 * (c + 1))
        sconv = state_conv[0, sl]
        sconv_fm = np.ascontiguousarray(sconv.reshape(16, 3, 4, 128).transpose(3, 2, 0, 1))
        sh_fm = np.ascontiguousarray(state_h[0, sl].reshape(16, 4, 128).transpose(2, 1, 0))
        in_maps.append({
            "xo": xo, "xc": xc, "w_in": w_in0, "w_out": w_out0, "ident": ident,
            "mask_own": band, "mask_ctx": band if h == 1 else allneg, "prm": prm,
            "wra_bd": wra_bd, "wri_bd": wri_bd, "lng": lng, "lnb": lnb,
            "flag": np.full((128, 1), float(h), f32),
            "xs": np.ascontiguousarray(x_sample[sl].reshape(64, 1024)),
            "ck": np.ascontiguousarray(cache_k[0, sl].reshape(16, 2048, 512)),
            "cv": np.ascontiguousarray(cache_v[0, sl].reshape(16, 2048, 512)),
            "sconv": sconv_fm, "sh": sh_fm, "m1": m1, "wnew": wnew,
        })
    res = run_bass_kernel_spmd(nc, in_maps, core_ids=list(range(8)))
    R = res.results

    y_prompt = np.zeros((4, 4096, 1024), f32)
    k_prompt = np.zeros((1, 4, 2048, 8, 64), f32)
    v_prompt = np.zeros((1, 4, 2048, 8, 64), f32)
    conv_prompt = np.zeros((1, 4, 3, 512), f32)
    h_prompt = np.zeros((1, 4, 512), f32)
    y_sample = np.zeros((128, 4, 1024), f32)
    k_sample = np.zeros((1, 128, 4, 8, 64), f32)
    v_sample = np.zeros((1, 128, 4, 8, 64), f32)
    conv_sample = np.zeros((1, 128, 3, 512), f32)
    h_sample = np.zeros((1, 128, 512), f32)
    for c in range(8):
        b, h = c // 2, c % 2
        r = R[c]
        y_prompt[b, h * 2048:(h + 1) * 2048] = r["y"]
        if h == 1:
            k_prompt[0, b] = r["kT"].T.reshape(2048, 8, 64)
            v_prompt[0, b] = r["vT"].T.reshape(2048, 8, 64)
            conv_prompt[0, b] = r["convT"].transpose(2, 1, 0).reshape(3, 512)
            h_prompt[0, b] = r["hT"].T.reshape(512)
        sl = slice(16 * c, 16 * (c + 1))
        y_sample[sl] = r["ys"].reshape(16, 4, 1024)
        k_sample[0, sl] = r["ks"].reshape(16, 4, 8, 64)
        v_sample[0, sl] = r["vs"].reshape(16, 4, 8, 64)
        conv_sample[0, sl] = r["convs"].transpose(2, 3, 1, 0).reshape(16, 3, 512)
        h_sample[0, sl] = r["hs"].transpose(2, 1, 0).reshape(16, 512)
    return (y_prompt, y_sample, k_prompt, v_prompt, conv_prompt, h_prompt,
            k_sample, v_sample, conv_sample, h_sample)
```

```python
import contextlib
import numpy as np
import concourse.bass as bass
import concourse.mybir as mybir
from concourse.bass_utils import run_bass_kernel_spmd

F32 = mybir.dt.float32
BF16 = mybir.dt.bfloat16
ALU = mybir.AluOpType
AF = mybir.ActivationFunctionType
AX = mybir.AxisListType

ENGS = ("pe", "act", "dve", "pool", "sp")
N_DMA_SLOTS = 40
NEG = -30000.0
ALPHA = 2.0 ** 0.25
LN_EPS = 1e-5


class Op:
    __slots__ = ("eng", "fn", "deps", "is_dma", "slot", "slot_val", "has_dep", "val")

    def __init__(self, eng, fn, is_dma):
        self.eng = eng
        self.fn = fn
        self.deps = []
        self.is_dma = is_dma
        self.slot = None
        self.slot_val = 0
        self.has_dep = False
        self.val = 0


class Prog:
    def __init__(self):
        self.ops = {e: [] for e in ENGS}
        self.res = {}
        self.slot_last = [None] * N_DMA_SLOTS
        self.slot_cnt = [0] * N_DMA_SLOTS
        self.next_slot = 0

    def add(self, eng, fn, reads=(), writes=(), dma=False):
        op = Op(eng, fn, dma)
        deps = []
        for k in reads:
            st = self.res.get(k)
            if st is not None and st[0] is not None:
                deps.append(st[0])
            if st is not None and k.startswith("bank"):
                deps.extend(r for r in st[1] if r.eng != eng)
        for k in writes:
            st = self.res.get(k)
            if st is not None:
                if st[0] is not None:
                    deps.append(st[0])
                deps.extend(st[1])
        if dma:
            s = self.next_slot
            self.next_slot = (s + 1) % N_DMA_SLOTS
            prev = self.slot_last[s]
            if prev is not None:
                deps.append(prev)
            self.slot_cnt[s] += 1
            op.slot = s
            op.slot_val = 16 * self.slot_cnt[s]
            self.slot_last[s] = op
        seen = set()
        for d in deps:
            if d is op or id(d) in seen:
                continue
            seen.add(id(d))
            if (not d.is_dma) and d.eng == eng and eng == "pe":
                continue
            op.deps.append(d)
            d.has_dep = True
        for k in reads:
            st = self.res.setdefault(k, [None, []])
            st[1].append(op)
        for k in writes:
            self.res[k] = [op, []]
        self.ops[eng].append(op)
        return op

    def emit(self, nc):
        for e in ENGS:
            v = 0
            for op in self.ops[e]:
                if not op.is_dma and op.has_dep:
                    v += 1
                    op.val = v
        with contextlib.ExitStack() as st:
            esem = {e: st.enter_context(nc.semaphore("s_" + e)) for e in ENGS}
            dsem = [st.enter_context(nc.semaphore("d%d" % i)) for i in range(N_DMA_SLOTS)]
            block = st.enter_context(nc.Block())

            def run(e, eh):
                waited = {}
                for op in self.ops[e]:
                    for d in op.deps:
                        if d.is_dma:
                            key, val, sem = ("d", d.slot), d.slot_val, dsem[d.slot]
                        else:
                            key, val, sem = ("e", d.eng), d.val, esem[d.eng]
                        if waited.get(key, 0) >= val:
                            continue
                        waited[key] = val
                        eh.wait_ge(sem, val)
                    ins = op.fn(eh)
                    if op.is_dma:
                        ins.then_inc(dsem[op.slot], 16)
                    elif op.has_dep:
                        ins.then_inc(esem[e], 1)
                if e == "sp":
                    for s in range(N_DMA_SLOTS):
                        if self.slot_cnt[s] and waited.get(("d", s), 0) < 16 * self.slot_cnt[s]:
                            eh.wait_ge(dsem[s], 16 * self.slot_cnt[s])

            block.tensor(lambda eh: run("pe", eh))
            block.scalar(lambda eh: run("act", eh))
            block.vector(lambda eh: run("dve", eh))
            block.gpsimd(lambda eh: run("pool", eh))
            block.sync(lambda eh: run("sp", eh))


NTOK = 2048
NCTX = 2048
NS = 64
NB = 16
PATTERNS = (1, 4, 16)


def build_program(do_sample=True):
    nc = bass.Bass("TRN2", target_bir_lowering=False)
    P = Prog()

    def din(name, shape):
        return nc.dram_tensor(name, list(shape), F32, kind="ExternalInput").ap()

    def dout(name, shape):
        return nc.dram_tensor(name, list(shape), F32, kind="ExternalOutput").ap()

    xo = din("xo", [NTOK, 1024])
    xc = din("xc", [NCTX, 1024])
    w_in = din("w_in", [1024, 3072])
    w_out = din("w_out", [1024, 1024])
    ident_d = din("ident", [128, 128])
    mask_own_d = din("mask_own", [128, 2, 256])
    mask_ctx_d = din("mask_ctx", [128, 2, 256])
    prm_d = din("prm", [128, 32])
    wra_d = din("wra_bd", [128, 4, 128])
    wri_d = din("wri_bd", [128, 4, 128])
    lng_d = din("lng", [128, 1024])
    lnb_d = din("lnb", [128, 1024])
    flag_d = din("flag", [128, 1])
    xs_d = din("xs", [NS, 1024])
    ck_d = din("ck", [NB, 2048, 512])
    cv_d = din("cv", [NB, 2048, 512])
    sconv_d = din("sconv", [128, 4, NB, 3])
    sh_d = din("sh", [128, 4, NB])
    m1_d = din("m1", [128, 32])
    wnew_d = din("wnew", [4, 32])

    y_d = dout("y", [NTOK, 1024])
    kT_d = dout("kT", [512, NTOK])
    vT_d = dout("vT", [512, NTOK])
    convT_d = dout("convT", [128, 4, 3])
    hT_d = dout("hT", [128, 4])
    ys_d = dout("ys", [NS, 1024])
    qs_d = dout("qs", [NS, 512])
    ks_d = dout("ks", [NS, 512])
    vs_d = dout("vs", [NS, 512])
    convs_d = dout("convs", [128, 4, NB, 3])
    hs_d = dout("hs", [128, 4, NB])

    with contextlib.ExitStack() as st:
        def sb(name, shape, dt):
            return st.enter_context(nc.sbuf_tensor("sb_" + name, list(shape), dt))

        def psb(name, shape, dt):
            return st.enter_context(nc.psum_tensor("ps_" + name, list(shape), dt))

        KT = sb("KT", [128, 4, 4096], BF16)
        VT = sb("VT", [128, 4, 4096], BF16)
        QT = sb("QT", [128, 4, NTOK], BF16)
        GA = sb("GA", [128, 4, NTOK], BF16)
        ML = sb("ML", [128, 4, NTOK], BF16)
        ident = sb("identb", [128, 128], BF16)
        ones = sb("ones", [128, 128], BF16)
        mask_own = sb("mask_own", [128, 2, 256], BF16)
        mask_ctx = sb("mask_ctx", [128, 2, 256], BF16)
        prm = sb("prm", [128, 32], F32)
        cneg = sb("cneg", [128, 8], F32)
        wra = sb("wra", [128, 4, 128], BF16)
        wri = sb("wri", [128, 4, 128], BF16)
        flag = sb("flag", [128, 1], F32)
        hstate = sb("hstate", [128, 4], F32)
        zcol = sb("zcol", [128, 1], F32)
        XL = sb("XL", [128, 3 + 512], F32)
        HALO = sb("HALO", [128, 4, 3], F32)
        SIGG = sb("SIGG", [128, 512], F32)
        arena = sb("arena", [128, 20224], F32)

        WIN = arena[:, 0:12288].bitcast(BF16).rearrange("p (c n) -> p c n", c=8)
        XTC = arena[:, 12288:14336].bitcast(BF16).rearrange("p (c n) -> p c n", c=8)
        XSTs = [arena[:, 14336:14848].bitcast(BF16), arena[:, 19712:20224].bitcast(BF16)]
        LTs = [[arena[:, 14848 + 1536 * s_ + 256 * k: 14848 + 1536 * s_ + 256 * (k + 1)] for k in range(6)] for s_ in range(2)]
        LT2s = [arena[:, 14848 + 1536 * s_ + 256: 14848 + 1536 * s_ + 768] for s_ in range(2)]
        SIGT = arena[:, 17920:18432]
        STG = [arena[:, 18432:18944], arena[:, 18944:19456]]
        UBs = [arena[:, 19456:19584].bitcast(BF16), arena[:, 19584:19712].bitcast(BF16)]
        ACC = arena[:, 0:4096].rearrange("p (a n) -> p a n", a=2)
        QZ = arena[:, 4096:6144].bitcast(BF16).rearrange("p (a n) -> p a n", a=2)
        PTt = [arena[:, 6144 + 256 * i: 6144 + 256 * (i + 1)].bitcast(BF16).rearrange("p (a n) -> p a n", a=2)
               for i in range(2)]
        VG = [arena[:, 6656 + 64 * i: 6656 + 64 * (i + 1)].bitcast(BF16) for i in range(2)]
        RD = arena[:, 7168:9216]
        WOUT = arena[:, 9216:13312].bitcast(BF16).rearrange("p (c n) -> p c n", c=8)
        SKS = [arena[:, 13312 + 1024 * i: 13312 + 1024 * (i + 1)].bitcast(BF16).rearrange("p (t c) -> p t c", t=4) for i in range(2)]
        SVS = [arena[:, 15360 + 1024 * i: 15360 + 1024 * (i + 1)].bitcast(BF16).rearrange("p (t c) -> p t c", t=4) for i in range(2)] + \
              [arena[:, 9216 + 1024 * i: 9216 + 1024 * (i + 1)].bitcast(BF16).rearrange("p (t c) -> p t c", t=4) for i in range(2)]
        SQB = arena[:, 17408:18432].bitcast(BF16).rearrange("p (t c) -> p t c", t=4)
        SPR = arena[:, 18432:19456].bitcast(BF16).rearrange("p (t c) -> p t c", t=4)
        XR = [arena[:, 13312 + 1024 * i: 13312 + 1024 * (i + 1)] for i in range(2)]
        ZT = [arena[:, 15360 + 1024 * i: 15360 + 1024 * (i + 1)] for i in range(2)]
        LNG = arena[:, 17408:18432]
        LNB = arena[:, 18432:19456]
        STAT = arena[:, 19456:19520]
        PKEYS = ["XTC", "XST0", "XST1", "UB0", "UB1", "SIGT", "STG0", "STG1"] + ["L%d_%d" % (a_, k_) for a_ in range(2) for k_ in range(6)]
        SKEYS = ["SKS0", "SKS1", "SVS0", "SVS1", "SQB", "SPR"]
        OKEYS = ["XR0", "XR1", "ZT0", "ZT1", "LNG", "LNB", "STAT0", "STAT1"]

        banks = [psb("bank%d" % i, [128, 512], F32) for i in range(8)]
        banks_bf = [bk[:, :].bitcast(BF16) for bk in banks]

        def bkey(i):
            return "bank%d" % i

        P.add("pool", lambda e: e.dma_start(out=WIN, in_=w_in.rearrange("(c p) n -> p c n", p=128)), writes=["WIN"], dma=True)
        P.add("pool", lambda e: e.dma_start(out=ident[:], in_=ident_d), writes=["ident"], dma=True)
        P.add("pool", lambda e: e.dma_start(out=mask_own[:], in_=mask_own_d), writes=["mask_own"], dma=True)
        P.add("pool", lambda e: e.dma_start(out=mask_ctx[:], in_=mask_ctx_d), writes=["mask_ctx"], dma=True)
        P.add("pool", lambda e: e.dma_start(out=wra[:], in_=wra_d), writes=["wra"], dma=True)
        P.add("pool", lambda e: e.dma_start(out=wri[:], in_=wri_d), writes=["wri"], dma=True)
        P.add("sp", lambda e: e.dma_start(out=prm[:], in_=prm_d), writes=["prm"], dma=True)
        P.add("sp", lambda e: e.dma_start(out=flag[:], in_=flag_d), writes=["flag"], dma=True)
        P.add("dve", lambda e: e.memset(ones[:], 1.0), writes=["ones"])
        P.add("dve", lambda e: e.memset(zcol[:], 0.0), writes=["zcol"])
        P.add("dve", lambda e: e.memset(hstate[:], 0.0), writes=["hs0", "hs1", "hs2", "hs3"])
        P.add("dve", lambda e: e.memset(HALO[:], 0.0), writes=["HALO0", "HALO1", "HALO2", "HALO3"])
        P.add("act", lambda e: e.activation(out=cneg[:, 0:4], in_=prm[:, 28:32], func=AF.Exp, scale=-1.0), reads=["prm"], writes=["cneg"])
        P.add("act", lambda e: e.activation(out=cneg[:, 0:4], in_=cneg[:, 0:4], func=AF.Ln, bias=1.0), reads=["cneg"], writes=["cneg"])
        P.add("dve", lambda e: e.tensor_scalar(out=cneg[:, 4:8], in0=cneg[:, 0:4], scalar1=-16.0, scalar2=None, op0=ALU.mult), reads=["cneg"], writes=["cneg"])
        P.add("dve", lambda e: e.tensor_scalar(out=cneg[:, 0:4], in0=cneg[:, 0:4], scalar1=-8.0, scalar2=None, op0=ALU.mult), reads=["cneg"], writes=["cneg"])

        rr = {"ev": 0, "stg": 0, "xst": 0}

        def next_bank():
            b_ = 4 + (rr["ev"] % 4)
            rr["ev"] += 1
            return b_

        def next_stg():
            i = rr["stg"] % 2
            rr["stg"] += 1
            return STG[i], "STG%d" % i

        def lk(s_, k):
            return "L%d_%d" % (s_, k)

        GBANKS = [2, 3]

        def gates_s1a(fc, n, s_):
            u = LTs[s_][0][:, 0:n]
            ub = UBs[s_][:, 0:n]
            gb = GBANKS[s_]
            gr, gi = banks[gb][:, 0:256], banks[gb][:, 256:512]
            P.add("dve", lambda e: e.tensor_copy(out=ub, in_=u), reads=[lk(s_, 0)], writes=["UB%d" % s_])
            P.add("pe", lambda e: e.matmul(gr[:, 0:n], lhsT=wra[:, fc, :], rhs=ub, start=True, stop=True, skip_group_check=True), reads=["wra", "UB%d" % s_], writes=[bkey(gb)])
            P.add("pe", lambda e: e.matmul(gi[:, 0:n], lhsT=wri[:, fc, :], rhs=ub, start=False, stop=True, skip_group_check=True), reads=["wri", "UB%d" % s_], writes=[bkey(gb)])

        def gates_s1b(fc, n, s_):
            gb = GBANKS[s_]
            gr, gi = banks[gb][:, 0:256], banks[gb][:, 256:512]
            er, ei, a = LTs[s_][1][:, 0:n], LTs[s_][2][:, 0:n], LTs[s_][4][:, 0:n]
            P.add("act", lambda e: e.activation(out=er, in_=gr[:, 0:n], func=AF.Exp, scale=-1.0, bias=prm[:, 20 + fc:21 + fc]), reads=[bkey(gb), "prm"], writes=[lk(s_, 1)])
            P.add("act", lambda e: e.activation(out=ei, in_=gi[:, 0:n], func=AF.Exp, scale=-1.0, bias=prm[:, 24 + fc:25 + fc]), reads=[bkey(gb), "prm"], writes=[lk(s_, 2)])
            if n == 256:
                both = LT2s[s_]
                P.add("act", lambda e: e.activation(out=both, in_=both, func=AF.Ln, bias=1.0), reads=[lk(s_, 1), lk(s_, 2)], writes=[lk(s_, 1), lk(s_, 2)])
                P.add("act", lambda e: e.activation(out=both, in_=both, func=AF.Exp, scale=-1.0), reads=[lk(s_, 1), lk(s_, 2)], writes=[lk(s_, 1), lk(s_, 2)])
            else:
                P.add("act", lambda e: e.activation(out=er, in_=er, func=AF.Ln, bias=1.0), reads=[lk(s_, 1)], writes=[lk(s_, 1)])
                P.add("act", lambda e: e.activation(out=ei, in_=ei, func=AF.Ln, bias=1.0), reads=[lk(s_, 2)], writes=[lk(s_, 2)])
                P.add("act", lambda e: e.activation(out=er, in_=er, func=AF.Exp, scale=-1.0), reads=[lk(s_, 1)], writes=[lk(s_, 1)])
                P.add("act", lambda e: e.activation(out=ei, in_=ei, func=AF.Exp, scale=-1.0), reads=[lk(s_, 2)], writes=[lk(s_, 2)])
            P.add("act", lambda e: e.activation(out=a, in_=er, func=AF.Exp, scale=cneg[:, fc:fc + 1]), reads=[lk(s_, 1), "cneg"], writes=[lk(s_, 4)])

        def gates_s2(fc, n, s_):
            u = LTs[s_][0][:, 0:n]
            ei, a, t1, bx = LTs[s_][2][:, 0:n], LTs[s_][4][:, 0:n], LTs[s_][3][:, 0:n], LTs[s_][5][:, 0:n]
            P.add("dve", lambda e: e.scalar_tensor_tensor(out=t1, in0=a, scalar=-1.0, in1=a, op0=ALU.mult, op1=ALU.mult), reads=[lk(s_, 4)], writes=[lk(s_, 3)])
            P.add("dve", lambda e: e.tensor_tensor(out=bx, in0=ei, in1=u, op=ALU.mult), reads=[lk(s_, 2), lk(s_, 0)], writes=[lk(s_, 5)])
            P.add("act", lambda e: e.activation(out=t1, in_=t1, func=AF.Ln, bias=1.0), reads=[lk(s_, 3)], writes=[lk(s_, 3)])
            P.add("act", lambda e: e.activation(out=t1, in_=t1, func=AF.Exp, scale=0.5), reads=[lk(s_, 3)], writes=[lk(s_, 3)])
            P.add("dve", lambda e: e.tensor_tensor(out=bx, in0=bx, in1=t1, op=ALU.mult), reads=[lk(s_, 5), lk(s_, 3)], writes=[lk(s_, 5)])
            return a, bx

        def sigmoid_times(src_bank, n, out_ap, out_key):
            eg = SIGT[:, 0:n]
            gp = banks[src_bank][:, 0:n]
            P.add("act", lambda e: e.activation(out=eg, in_=gp, func=AF.Exp, scale=-1.0), reads=[bkey(src_bank)], writes=["SIGT"])
            P.add("act", lambda e: e.activation(out=eg, in_=eg, func=AF.Ln, bias=1.0), reads=["SIGT"], writes=["SIGT"])
            P.add("act", lambda e: e.activation(out=eg, in_=eg, func=AF.Exp, scale=-1.0), reads=["SIGT"], writes=["SIGT"])
            P.add("dve", lambda e: e.tensor_tensor(out=out_ap, in0=eg, in1=gp, op=ALU.mult), reads=["SIGT", bkey(src_bank)], writes=[out_key])

        def proj(col0, n, xt_ap, bank, xkey="XTC"):
            def one(kc):
                P.add("pe", lambda e: e.matmul(banks[bank][:, 0:n], lhsT=WIN[:, kc, col0:col0 + 128], rhs=xt_ap[:, kc, :],
                                               start=(kc == 0), stop=(kc == 7)), reads=["WIN", xkey], writes=[bkey(bank)])
            for kc in range(8):
                one(kc)

        def load_xblock(src_rows, nrow, dst_ap, dst_key):
            xi = rr["xst"] % 2
            rr["xst"] += 1
            xst = XSTs[xi]
            xk = "XST%d" % xi
            P.add("pool", lambda e: e.dma_start(out=xst[0:nrow, :], in_=src_rows), writes=[xk], dma=True)
            pt = banks_bf[xi]

            def one(kc):
                P.add("pe", lambda e: e.transpose(pt[:, kc * nrow:(kc + 1) * nrow], xst[0:nrow, kc * 128:(kc + 1) * 128], ident[0:nrow, 0:nrow]),
                      reads=[xk, "ident"], writes=[bkey(xi)])
            for kc in range(8):
                one(kc)
            srcv = pt[:, 0:8 * nrow].rearrange("p (c n) -> p c n", c=8)
            P.add("dve", lambda e: e.tensor_copy(out=dst_ap, in_=srcv), reads=[bkey(xi)], writes=[dst_key])

        def kv_proj(tc, which, pr):
            own = tc >= 4
            T0 = tc * 512
            o0 = T0 - 2048
            dstT = KT if which == 0 else VT
            dkey = "KT" if which == 0 else "VT"
            dout_ap = kT_d if which == 0 else vT_d
            bank = next_bank()
            proj(512 + which * 512 + pr * 128, 512, XTC, bank)
            if own:
                stg, sk = next_stg()
                P.add("act", lambda e: e.activation(out=stg, in_=banks[bank][:, :], func=AF.Copy), reads=[bkey(bank)], writes=[sk])
                P.add("sp", lambda e: e.dma_start(out=dout_ap[pr * 128:(pr + 1) * 128, o0:o0 + 512], in_=stg), reads=[sk], dma=True)
                P.add("dve", lambda e: e.tensor_copy(out=dstT[:, pr, T0:T0 + 512], in_=banks[bank][:, :]), reads=[bkey(bank)], writes=[dkey])
            else:
                P.add("act", lambda e: e.activation(out=dstT[:, pr, T0:T0 + 512], in_=banks[bank][:, :], func=AF.Copy), reads=[bkey(bank)], writes=[dkey])

        def q_proj(tc, pr):
            o0 = tc * 512 - 2048
            bank = next_bank()
            proj(pr * 128, 512, XTC, bank)
            P.add("act", lambda e: e.activation(out=QT[:, pr, o0:o0 + 512], in_=banks[bank][:, :], func=AF.Copy), reads=[bkey(bank)], writes=["QT"])

        def ga_proj(tc, pr):
            o0 = tc * 512 - 2048
            bank = next_bank()
            proj(1536 + pr * 128, 512, XTC, bank)
            sigmoid_times(bank, 512, GA[:, pr, o0:o0 + 512], "GA")

        item_ctr = [0]

        def lru_item(tc, fc, h, fillers):
            own = tc >= 4
            o0 = tc * 512 - 2048
            c0 = 256 * h
            s_ = h
            hk = "hs%d" % fc
            hak = "HALO%d" % fc
            u = LTs[s_][0][:, :]
            tmp = LTs[s_][5][:, :]
            cw = lambda k: prm[:, fc * 4 + k: fc * 4 + k + 1]

            def stage1a():
                if h == 0:
                    bank = next_bank()
                    proj(2048 + fc * 128, 512, XTC, bank)
                    P.add("act", lambda e: e.activation(out=XL[:, 3:515], in_=banks[bank][:, :], func=AF.Copy), reads=[bkey(bank)], writes=["XL"])
                    P.add("dve", lambda e: e.tensor_copy(out=XL[:, 0:3], in_=HALO[:, fc, :]), reads=[hak], writes=["XL"])
                P.add("dve", lambda e: e.tensor_scalar(out=u, in0=XL[:, c0:c0 + 256], scalar1=cw(0), scalar2=prm[:, 16 + fc:17 + fc],
                                                        op0=ALU.mult, op1=ALU.add), reads=["XL", "prm"], writes=[lk(s_, 0)])

                def tap(k):
                    P.add("dve", lambda e: e.scalar_tensor_tensor(out=u, in0=XL[:, c0 + k:c0 + k + 256], scalar=cw(k), in1=u, op0=ALU.mult, op1=ALU.add),
                          reads=["XL", lk(s_, 0), "prm"], writes=[lk(s_, 0)])
                for k in range(1, 4):
                    tap(k)
                if h == 1:
                    P.add("dve", lambda e: e.tensor_copy(out=HALO[:, fc, :], in_=XL[:, 512:515]), reads=["XL"], writes=[hak])
                gates_s1a(fc, 256, s_)

            def stage1b():
                gates_s1b(fc, 256, s_)
                for f in fillers:
                    f()

            def stage2():
                a, bx = gates_s2(fc, 256, s_)
                hseq = LTs[s_][1][:, :]
                if h == 0:
                    init_ap, init_key = hstate[:, fc:fc + 1], hk
                else:
                    init_ap, init_key = LTs[0][1][:, 255:256], lk(0, 1)
                P.add("dve", lambda e: e.tensor_tensor_scan(out=hseq, data0=a, data1=bx, initial=init_ap, op0=ALU.mult, op1=ALU.add),
                      reads=[lk(s_, 4), lk(s_, 5), init_key], writes=[lk(s_, 1)])
                if h == 1:
                    if tc == 3:
                        P.add("dve", lambda e: e.tensor_tensor(out=hstate[:, fc:fc + 1], in0=hseq[:, 255:256], in1=flag[:, 0:1], op=ALU.mult),
                              reads=[lk(s_, 1), "flag"], writes=[hk])
                    else:
                        P.add("dve", lambda e: e.tensor_copy(out=hstate[:, fc:fc + 1], in_=hseq[:, 255:256]), reads=[lk(s_, 1)], writes=[hk])
                if own:
                    if h == 0:
                        bank2 = next_bank()
                        proj(2560 + fc * 128, 512, XTC, bank2)
                        sigmoid_times(bank2, 512, SIGG[:, :], "SIGG")
                    P.add("dve", lambda e: e.tensor_tensor(out=ML[:, fc, o0 + c0:o0 + c0 + 256], in0=SIGG[:, c0:c0 + 256], in1=hseq, op=ALU.mult),
                          reads=["SIGG", lk(s_, 1)], writes=["ML"])
            return stage1a, stage1b, stage2

        prev2 = None
        for tc in range(8):
            srcx = xo if tc >= 4 else xc
            for tb in range(4):
                r0 = (tc % 4) * 512 + tb * 128
                load_xblock(srcx[r0:r0 + 128, :], 128, XTC[:, :, tb * 128:(tb + 1) * 128], "XTC")
            others = [(lambda which=which, pr=pr, tc=tc: kv_proj(tc, which, pr)) for which in range(2) for pr in range(4)]
            if tc >= 4:
                others += [(lambda pr=pr, tc=tc: q_proj(tc, pr)) for pr in range(4)]
                others += [(lambda pr=pr, tc=tc: ga_proj(tc, pr)) for pr in range(4)]
            per = (len(others) + 7) // 8
            idx = 0
            for fc in range(4):
                for h in range(2):
                    s1a, s1b, s2 = lru_item(tc, fc, h, others[idx * per:(idx + 1) * per])
                    idx += 1
                    s1a()
                    if prev2 is not None:
                        prev2()
                    s1b()
                    prev2 = s2
        prev2()
        P.add("sp", lambda e: e.dma_start(out=convT_d, in_=HALO[:]), reads=["HALO0", "HALO1", "HALO2", "HALO3"], dma=True)
        P.add("sp", lambda e: e.dma_start(out=hT_d, in_=hstate[:]), reads=["hs0", "hs1", "hs2", "hs3"], dma=True)

        SXT = sb("SXT", [128, 8, NS], BF16)
        SGA = sb("SGA", [128, 4, NS], BF16)
        SML = sb("SML", [128, 4, NS], BF16)
        SMA = sb("SMA", [128, 4, NS], BF16)
        SXL = sb("SXL", [128, 4, NB, 7], F32)
        SH = sb("SH", [128, 4, NB], F32)
        SHO = sb("SHO", [128, 4, NB], F32)

        def s_qkv(j):
            bank = next_bank()

            def one(kc):
                P.add("pe", lambda e: e.matmul(banks[bank][0:NS, :], lhsT=SXT[:, kc, :], rhs=WIN[:, kc, j * 512:(j + 1) * 512],
                                               start=(kc == 0), stop=(kc == 7)), reads=["WIN", "SXT"], writes=[bkey(bank)])
            for kc in range(8):
                one(kc)
            stg_full, sk = next_stg()
            stg = stg_full[0:NS, :]
            P.add("act", lambda e: e.activation(out=stg, in_=banks[bank][0:NS, :], func=AF.Copy), reads=[bkey(bank)], writes=[sk])
            P.add("sp", lambda e: e.dma_start(out=(qs_d, ks_d, vs_d)[j], in_=stg), reads=[sk], writes=["scr%d" % j], dma=True)

        def s_ga(pr):
            bank = next_bank()
            proj(1536 + pr * 128, NS, SXT, bank, xkey="SXT")
            sigmoid_times(bank, NS, SGA[:, pr, :], "SGA")

        def s_lru(fc):
            n = NS
            bank = next_bank()
            proj(2048 + fc * 128, NS, SXT, bank, xkey="SXT")
            P.add("act", lambda e: e.activation(out=SXL[:, fc, :, 3:7], in_=banks[bank][:, 0:NS].rearrange("p (b t) -> p b t", t=4), func=AF.Copy),
                  reads=[bkey(bank)], writes=["SXL"])
            u = LTs[0][0][:, 0:n]
            u3 = u.rearrange("p (b t) -> p b t", t=4)
            cw = lambda k: prm[:, fc * 4 + k: fc * 4 + k + 1]
            P.add("dve", lambda e: e.tensor_scalar(out=u3, in0=SXL[:, fc, :, 0:4], scalar1=cw(0), scalar2=prm[:, 16 + fc:17 + fc],
                                                    op0=ALU.mult, op1=ALU.add), reads=["SXL", "prm"], writes=[lk(0, 0)])

            def tap(k):
                P.add("dve", lambda e: e.scalar_tensor_tensor(out=u3, in0=SXL[:, fc, :, k:k + 4], scalar=cw(k), in1=u3, op0=ALU.mult, op1=ALU.add),
                      reads=["SXL", lk(0, 0), "prm"], writes=[lk(0, 0)])
            for k in range(1, 4):
                tap(k)
            gates_s1a(fc, n, 0)
            gates_s1b(fc, n, 0)
            a, bx = gates_s2(fc, n, 0)
            a3 = a.rearrange("p (b t) -> p b t", t=4)
            b3 = bx.rearrange("p (b t) -> p b t", t=4)
            hh = LTs[0][1][:, 0:n]
            h3 = hh.rearrange("p (b t) -> p b t", t=4)

            def step(t):
                prev = SH[:, fc, :] if t == 0 else h3[:, :, t - 1]
                P.add("dve", lambda e: e.tensor_tensor(out=h3[:, :, t], in0=a3[:, :, t], in1=prev, op=ALU.mult), reads=[lk(0, 4), lk(0, 1), "SH"], writes=[lk(0, 1)])
                P.add("dve", lambda e: e.tensor_tensor(out=h3[:, :, t], in0=h3[:, :, t], in1=b3[:, :, t], op=ALU.add), reads=[lk(0, 5), lk(0, 1)], writes=[lk(0, 1)])
            for t in range(4):
                step(t)
            P.add("dve", lambda e: e.tensor_copy(out=SHO[:, fc, :], in_=h3[:, :, 3]), reads=[lk(0, 1)], writes=["SHO"])
            bank2 = next_bank()
            proj(2560 + fc * 128, NS, SXT, bank2, xkey="SXT")
            sigmoid_times(bank2, NS, SIGG[:, 0:NS], "SIGG")
            P.add("dve", lambda e: e.tensor_tensor(out=SML[:, fc, :], in0=SIGG[:, 0:NS], in1=hh, op=ALU.mult), reads=["SIGG", lk(0, 1)], writes=["SML"])

        if do_sample:
            load_xblock(xs_d, NS, SXT[:], "SXT")
            for j in range(3):
                s_qkv(j)
            for pr in range(4):
                s_ga(pr)
            P.add("sp", lambda e: e.dma_start(out=SXL[:, :, :, 0:3], in_=sconv_d), writes=["SXL"], dma=True)
            P.add("sp", lambda e: e.dma_start(out=SH[:], in_=sh_d), writes=["SH"], dma=True)
            for fc in range(4):
                s_lru(fc)
            P.add("sp", lambda e: e.dma_start(out=hs_d, in_=SHO[:]), reads=["SHO"], dma=True)
            P.add("sp", lambda e: e.dma_start(out=convs_d, in_=SXL[:, :, :, 4:7]), reads=["SXL"], dma=True)

        P.add("dve", lambda e: e.memset(STAT, 0.0), writes=PKEYS + SKEYS)

        P.add("dve", lambda e: e.memset(QZ, 0.0), reads=SKEYS[:1], writes=["QZ", "WIN", "SVS2", "SVS3"])
        unit_i = [0]

        sample_tasks = []
        task_ctr = [0]

        pipe = [None, None, None]

        def pop_sample_task(flush=False):
            nxt = sample_tasks.pop(0) if sample_tasks else None
            if pipe[2] is not None:
                pipe[2][3]()
            if pipe[1] is not None:
                pipe[1][2]()
            if pipe[0] is not None:
                pipe[0][1]()
            if nxt is not None:
                nxt[0]()
            pipe[2], pipe[1], pipe[0] = pipe[1], pipe[0], nxt

        pending_pieces = []

        def flush_pieces(nmax):
            for _ in range(min(nmax, len(pending_pieces))):
                pending_pieces.pop(0)()

        def maybe_sample_task():
            task_ctr[0] += 1
            if task_ctr[0] % 4 == 0 and (sample_tasks or any(p_ is not None for p_ in pipe)):
                flush_pieces(len(pending_pieces))
                pop_sample_task()
            else:
                flush_pieces(3)

        def att_unit(pr, pi, d, r, kb, nbq):
            is_ctx = kb == nbq - 1
            is_last = kb == 2 * nbq - 1
            u0 = 128 if is_ctx else 0
            n = 128 if (is_ctx or is_last) else 256
            ui = unit_i[0]
            unit_i[0] += 1
            sbank = 4 + ui % 2
            S = banks[sbank][:, :].rearrange("p (a n) -> p a n", a=2)
            msk = mask_ctx if is_ctx else mask_own
            kstart = r + 128 * d * kb
            keys = KT[:, pr, kstart: kstart + 127 * d + 1: d]
            qstart = r + d * (128 * kb + u0) - 2048
            qs_ap = QZ[:, :, qstart: qstart + (n - 1) * d + 1: d]
            vi = ui % 2
            vg = VG[vi]
            vbank = 6 + vi
            vps = banks_bf[vbank][:, 0:128]
            pt_t = PTt[ui % 2]
            pk = "PT%d" % (ui % 2)
            vk = "VG%d" % vi

            def stage1():
                P.add("pe", lambda e: e.transpose(vps, VT[:, pr, kstart: kstart + 127 * d + 1: d], ident[:]), reads=["VT", "ident"], writes=[bkey(vbank)])
                P.add("pe", lambda e: e.matmul(S[:, :, 0:n], lhsT=ident[:], rhs=msk[:, :, u0:u0 + n], start=True, stop=False),
                      reads=["ident", "mask_own", "mask_ctx"], writes=[bkey(sbank)])
                P.add("pe", lambda e: e.matmul(S[:, :, 0:n], lhsT=keys, rhs=qs_ap, start=False, stop=True), reads=["KT", "QZ"], writes=[bkey(sbank)])
                P.add("act", lambda e: e.activation(out=vg, in_=vps, func=AF.Copy), reads=[bkey(vbank)], writes=[vk])
                P.add("act", lambda e: e.activation(out=pt_t[:, :, 0:n], in_=S[:, :, 0:n], func=AF.Exp, scale=0.125), reads=[bkey(sbank)], writes=[pk])

            def pv(sub):
                qb = kb + (u0 // 128) + sub
                first = (qb == kb + 1)
                obank = qb % 2
                OD = banks[obank]
                c0 = sub * 128
                okey = bkey(obank)
                P.add("pe", lambda e: e.matmul(OD[0:64, 0:128], lhsT=vg[:, 0:64], rhs=pt_t[:, 0, c0:c0 + 128], start=first, stop=False, skip_group_check=True),
                      reads=[vk, pk], writes=[okey])
                P.add("pe", lambda e: e.matmul(OD[64:128, 0:128], lhsT=vg[:, 64:128], rhs=pt_t[:, 1, c0:c0 + 128], start=first, stop=False, skip_group_check=True),
                      reads=[vk, pk], writes=[okey])
                P.add("pe", lambda e: e.matmul(OD[0:64, 128:256], lhsT=ones[:, 0:64], rhs=pt_t[:, 0, c0:c0 + 128], start=False, stop=False, skip_group_check=True),
                      reads=["ones", pk], writes=[okey])
                P.add("pe", lambda e: e.matmul(OD[64:128, 128:256], lhsT=ones[:, 0:64], rhs=pt_t[:, 1, c0:c0 + 128], start=False, stop=False, skip_group_check=True),
                      reads=["ones", pk], writes=[okey])
                if not first:
                    t0 = r + d * 128 * qb - 2048
                    dst = ACC[:, :, t0: t0 + 127 * d + 1: d]
                    srcv = OD[:, 0:256].rearrange("p (a n) -> p a n", a=2)
                    if pi == 0:
                        P.add("act", lambda e: e.activation(out=dst, in_=srcv, func=AF.Copy), reads=[okey], writes=["ACC"])
                    else:
                        P.add("dve", lambda e: e.tensor_tensor(out=dst, in0=srcv, in1=dst, op=ALU.add), reads=[okey, "ACC"], writes=["ACC"])

            def stage2():
                for sub in range(n // 128):
                    pv(sub)
            return stage1, stage2

        def att_pair(pr):
            P.add("act", lambda e: e.activation(out=QZ[0:64, 0, :], in_=QT[0:64, pr, :], func=AF.Copy), reads=["QT"], writes=["QZ"])
            P.add("act", lambda e: e.activation(out=QZ[64:128, 1, :], in_=QT[64:128, pr, :], func=AF.Copy), reads=["QT"], writes=["QZ"])
            units = []
            for pi, d in enumerate(PATTERNS):
                nbq = NTOK // (128 * d)
                for r in range(d):
                    for kb in range(nbq - 1, 2 * nbq):
                        units.append((pr, pi, d, r, kb, nbq))
            prev2 = None
            for uargs in units:
                s1, s2 = att_unit(*uargs)
                s1()
                if prev2 is not None:
                    prev2()
                prev2 = s2
                maybe_sample_task()
            prev2()
            P.add("act", lambda e: e.activation(out=RD, in_=ACC[:, 1, :], func=AF.Ln), reads=["ACC"], writes=["RD"])
            P.add("act", lambda e: e.activation(out=RD, in_=RD, func=AF.Exp, scale=-1.0), reads=["RD"], writes=["RD"])
            P.add("dve", lambda e: e.tensor_tensor(out=RD, in0=RD, in1=ACC[:, 0, :], op=ALU.mult), reads=["RD", "ACC"], writes=["RD"])
            P.add("dve", lambda e: e.tensor_tensor(out=GA[:, pr, :], in0=RD, in1=GA[:, pr, :], op=ALU.mult), reads=["RD", "GA"], writes=["GA"])

        KN = sb("KN", [4, 512], BF16)
        VN = sb("VN", [4, 512], BF16)
        SC = sb("SC", [128, 4, 32], F32)
        PS = sb("PS", [128, 4, 32], BF16)
        M1 = sb("M1", [128, 32], F32)
        WN = sb("WN", [4, 32], F32)
        OSEL = sb("OSEL", [128, 2, 16], F32)
        qs_flat = qs_d.rearrange("(b t) c -> b (t c)", t=4)
        sbuf_i = [0]

        def sample_b_tasks(b):
            obank = 2 + b % 2
            OD = banks[obank]
            okey = bkey(obank)
            firstmm = [True]

            def mm(out_ap, lhsT, rhs, rd):
                f = firstmm[0]
                firstmm[0] = False
                P.add("pe", lambda e: e.matmul(out_ap, lhsT=lhsT, rhs=rhs, start=f, stop=False, skip_group_check=True), reads=rd, writes=[okey])

            def slot_fns(slot):
                st_ = {}
                psk = "PS%d" % slot

                def stage_a0():
                    if slot == 0:
                        P.add("pool", lambda e: e.dma_start(out=SQB.rearrange("p t c -> p (t c)"), in_=qs_flat[b:b + 1, :].broadcast_to([128, 2048])),
                              reads=["scr0"], writes=["SQB"], dma=True)
                    if slot == 3:
                        P.add("pool", lambda e: e.dma_start(out=KN[:], in_=ks_d[4 * b:4 * b + 4, :]), reads=["scr1"], writes=["KN"], dma=True)
                        P.add("pool", lambda e: e.dma_start(out=VN[:], in_=vs_d[4 * b:4 * b + 4, :]), reads=["scr2"], writes=["VN"], dma=True)
                    if slot < 3:
                        ki = sbuf_i[0] % 2
                        vi_ = sbuf_i[0] % 4
                        sbuf_i[0] += 1
                        ks, vs = SKS[ki], SVS[vi_]
                        kk, vk = "SKS%d" % ki, "SVS%d" % vi_
                        if slot == 0:
                            ksrc = ck_d[b, :, :].rearrange("(i t) c -> i t c", t=16)[:, 0:4, :]
                            vsrc = cv_d[b, :, :].rearrange("(i t) c -> i t c", t=16)[:, 0:4, :]
                            kdst, vdst = ks, vs
                        elif slot == 1:
                            ksrc = ck_d[b, 1536:2048, :].rearrange("(i t) c -> i t c", t=4)
                            vsrc = cv_d[b, 1536:2048, :].rearrange("(i t) c -> i t c", t=4)
                            kdst, vdst = ks, vs
                        else:
                            ksrc = ck_d[b, 1920:2048, :]
                            vsrc = cv_d[b, 1920:2048, :]
                            kdst, vdst = ks[:, 0, :], vs[:, 0, :]
                        P.add("pool", lambda e: e.dma_start(out=kdst, in_=ksrc), writes=[kk], dma=True)
                        P.add("pool", lambda e: e.dma_start(out=vdst, in_=vsrc), writes=[vk], dma=True)
                        st_["kin"] = ks if slot < 2 else ks[:, 0:1, :].broadcast_to([128, 4, 512])
                        st_["kk"] = kk
                        st_["npart"] = 128
                        st_["vs"], st_["vk"] = vs, vk
                    else:
                        st_["kin"] = KN[:].unsqueeze(1).broadcast_to([4, 4, 512])
                        st_["kk"] = "KN"
                        st_["npart"] = 4

                def stage_a1():
                    kin, kk, npart = st_["kin"], st_["kk"], st_["npart"]
                    P.add("dve", lambda e: e.tensor_tensor(out=SPR[0:npart], in0=kin, in1=SQB[0:npart], op=ALU.mult), reads=[kk, "SQB"], writes=["SPR"])

                    def piece(t):
                        P.add("dve", lambda e: e.tensor_reduce(out=SC[0:npart, slot, t * 8:(t + 1) * 8],
                                                               in_=SPR[0:npart, t, :].rearrange("p (h d) -> p h d", d=64),
                                                               axis=AX.X, op=ALU.add), reads=["SPR"], writes=["SC%d" % slot])
                    if npart == 4:
                        for t in range(4):
                            piece(t)
                    else:
                        for t in range(4):
                            pending_pieces.append(lambda t=t: piece(t))

                def stage_a2():
                    npart = st_["npart"]
                    sck = "SC%d" % slot
                    P.add("act", lambda e: e.activation(out=SC[0:npart, slot, :], in_=SC[0:npart, slot, :], func=AF.Exp, scale=0.125), reads=[sck], writes=[sck])
                    if slot == 2:
                        P.add("dve", lambda e: e.tensor_tensor(out=PS[:, 2, :], in0=SC[:, 2, :], in1=M1[:], op=ALU.mult), reads=[sck, "M1"], writes=[psk])
                    elif slot == 3:
                        P.add("dve", lambda e: e.tensor_tensor(out=PS[0:4, 3, :], in0=SC[0:4, 3, :], in1=WN[:], op=ALU.mult), reads=[sck, "WN"], writes=[psk])
                    else:
                        P.add("dve", lambda e: e.tensor_copy(out=PS[0:npart, slot, :], in_=SC[0:npart, slot, :]), reads=[sck], writes=[psk])

                def stage_b():
                    for t in range(4):
                        for pr in range(4):
                            oc = (pr * 4 + t) * 2
                            c = t * 8 + 2 * pr
                            if slot < 2:
                                mm(OD[:, oc:oc + 2], st_["vs"][:, t, pr * 128:(pr + 1) * 128], PS[:, slot, c:c + 2], [st_["vk"], psk])
                            elif slot == 2:
                                mm(OD[:, oc:oc + 2], st_["vs"][:, 0, pr * 128:(pr + 1) * 128], PS[:, 2, c:c + 2], [st_["vk"], psk])
                            else:
                                mm(OD[:, oc:oc + 2], VN[:, pr * 128:(pr + 1) * 128], PS[0:4, 3, c:c + 2], ["VN", psk])
                    if slot < 3:
                        mm(OD[:, 32:64], ones[:, :], PS[:, slot, :], ["ones", psk])
                    else:
                        mm(OD[:, 32:64], ones[0:4, :], PS[0:4, 3, :], ["ones", psk])
                        epilogue()
                return stage_a0, stage_a1, stage_a2, stage_b

            def epilogue():
                num = OD[:, 0:32].rearrange("p (q t two) -> p q t two", q=4, t=4)
                den = OD[:, 32:64].rearrange("p (t q two) -> p q t two", q=4, two=2)
                osel = OSEL[:].rearrange("p a (q t) -> p a q t", q=4)

                def sel_n(hb, lo, hi):
                    P.add("dve", lambda e: e.tensor_copy(out=osel[lo:hi, 0], in_=num[lo:hi, :, :, hb]), reads=[okey], writes=["OSEL"])

                def sel_d(hb, lo, hi):
                    P.add("dve", lambda e: e.tensor_copy(out=osel[lo:hi, 1], in_=den[lo:hi, :, :, hb]), reads=[okey], writes=["OSEL"])

                def fin1():
                    P.add("dve", lambda e: e.reciprocal(out=OSEL[:, 1, :], in_=OSEL[:, 1, :]), reads=["OSEL"], writes=["OSEL"])

                def fin2():
                    P.add("dve", lambda e: e.tensor_tensor(out=OSEL[:, 0, :], in0=OSEL[:, 0, :], in1=OSEL[:, 1, :], op=ALU.mult), reads=["OSEL"], writes=["OSEL"])

                def fin3():
                    P.add("dve", lambda e: e.tensor_tensor(out=SMA[:, :, 4 * b:4 * b + 4], in0=osel[:, 0], in1=SGA[:, :, 4 * b:4 * b + 4], op=ALU.mult),
                          reads=["OSEL", "SGA"], writes=["SMA"])
                pending_pieces.extend([lambda: sel_n(0, 0, 64), lambda: sel_d(0, 0, 64), lambda: sel_n(1, 64, 128), lambda: sel_d(1, 64, 128),
                                       fin1, fin2, fin3])

            return [slot_fns(slot) for slot in range(4)]

        if do_sample:
            P.add("sp", lambda e: e.dma_start(out=M1[:], in_=m1_d), writes=["M1"], dma=True)
            P.add("sp", lambda e: e.dma_start(out=WN[:], in_=wnew_d), writes=["WN"], dma=True)
            P.add("dve", lambda e: e.memset(SC[:], 0.0), writes=["SC0", "SC1", "SC2", "SC3"])
        if do_sample:
            for b in range(NB):
                sample_tasks.extend(sample_b_tasks(b))
        for pr in range(4):
            att_pair(pr)
        while sample_tasks or any(p_ is not None for p_ in pipe):
            flush_pieces(len(pending_pieces))
            pop_sample_task()
        flush_pieces(len(pending_pieces))
        P.add("pool", lambda e: e.dma_start(out=WOUT, in_=w_out.rearrange("(c p) n -> p c n", p=128)), writes=["WOUT", "SVS2", "SVS3"], dma=True)

        XR3 = [arena[:, 1024 * i: 1024 * (i + 1)] for i in range(3)]
        ZT3 = [arena[:, 3072 + 1024 * i: 3072 + 1024 * (i + 1)] for i in range(3)]
        O3KEYS = ["XR%d" % i for i in range(3)] + ["ZT%d" % i for i in range(3)]
        P.add("dve", lambda e: e.memset(STAT, 0.0), writes=SKEYS + OKEYS + O3KEYS + ["ACC", "QZ", "PT0", "PT1", "VG0", "VG1", "RD"])
        P.add("sp", lambda e: e.dma_start(out=LNG, in_=lng_d), writes=["LNG"], dma=True)
        P.add("sp", lambda e: e.dma_start(out=LNB, in_=lnb_d), writes=["LNB"], dma=True)

        def out_block(i, nrow, xrows, mixT, ydst, mkeys):
            xr = XR3[i % 3][0:nrow, :]
            z = ZT3[i % 3][0:nrow, :]
            xk, zk, sk = "XR%d" % (i % 3), "ZT%d" % (i % 3), "STAT%d" % (i % 2)
            so = 16 * (i % 2)
            st6 = STAT[0:nrow, so:so + 12].rearrange("p (c s) -> p c s", c=2)
            mv = STAT[0:nrow, so + 12:so + 14]
            rstd = STAT[0:nrow, so + 14:so + 15]

            def stage1():
                def half_fn(half):
                    bank = 2 + 2 * (i % 3) + half

                    def one(kc):
                        P.add("pe", lambda e: e.matmul(banks[bank][0:nrow, :], lhsT=mixT(kc), rhs=WOUT[:, kc, half * 512:(half + 1) * 512],
                                                       start=(kc == 0), stop=(kc == 7)), reads=mkeys + ["WOUT"], writes=[bkey(bank)])
                    for kc in range(8):
                        one(kc)
                    P.add("dve", lambda e: e.scalar_tensor_tensor(out=z[:, half * 512:(half + 1) * 512], in0=xr[:, half * 512:(half + 1) * 512],
                                                                  scalar=ALPHA, in1=banks[bank][0:nrow, :], op0=ALU.mult, op1=ALU.add),
                          reads=[bkey(bank), xk], writes=[zk])
                    P.add("dve", lambda e: e.bn_stats(out=st6[:, half, :], in_=z[:, half * 512:(half + 1) * 512]), reads=[zk], writes=[sk])
                half_fn(0)
                half_fn(1)
                P.add("dve", lambda e: e.bn_aggr(out=mv, in_=st6), reads=[sk], writes=[sk])
                P.add("dve", lambda e: e.tensor_scalar(out=rstd, in0=mv[:, 1:2], scalar1=LN_EPS, scalar2=None, op0=ALU.add), reads=[sk], writes=[sk])
                P.add("act", lambda e: e.activation(out=rstd, in_=rstd, func=AF.Ln), reads=[sk], writes=[sk])
                P.add("act", lambda e: e.activation(out=rstd, in_=rstd, func=AF.Exp, scale=-0.5), reads=[sk], writes=[sk])

            def xload():
                P.add("sp", lambda e: e.dma_start(out=xr, in_=xrows), writes=[xk], dma=True)

            def stage2():
                P.add("dve", lambda e: e.tensor_scalar(out=z, in0=z, scalar1=mv[:, 0:1], scalar2=rstd, op0=ALU.subtract, op1=ALU.mult), reads=[zk, sk], writes=[zk])
                P.add("dve", lambda e: e.tensor_tensor(out=z, in0=z, in1=LNG[0:nrow, :], op=ALU.mult), reads=[zk, "LNG"], writes=[zk])

            def stage3():
                P.add("dve", lambda e: e.tensor_tensor(out=z, in0=z, in1=LNB[0:nrow, :], op=ALU.add), reads=[zk, "LNB"], writes=[zk])
                P.add("pool", lambda e: e.dma_start(out=ydst, in_=z), reads=[zk], dma=True)
            return stage1, stage2, stage3, xload

        def prompt_block(tb):
            def mixT(kc):
                src = GA if kc < 4 else ML
                return src[:, kc % 4, tb * 128:(tb + 1) * 128]
            return out_block(tb, 128, xo[tb * 128:(tb + 1) * 128, :], mixT, y_d[tb * 128:(tb + 1) * 128, :], ["GA", "ML"])

        blocks = [prompt_block(tb) for tb in range(16)]
        if do_sample:
            def mixTs(kc):
                src = SMA if kc < 4 else SML
                return src[:, kc % 4, :]
            blocks.append(out_block(16, NS, xs_d, mixTs, ys_d, ["SMA", "SML"]))
        nblk = len(blocks)
        blocks[0][3]()
        for j in range(nblk + 2):
            if j + 1 < nblk:
                blocks[j + 1][3]()
            if j < nblk:
                blocks[j][0]()
            if 0 <= j - 1 < nblk:
                blocks[j - 1][1]()
            if 0 <= j - 2 < nblk:
                blocks[j - 2][2]()

        P.emit(nc)
    return nc


def _band_mask():
    s = np.arange(128)[:, None]
    u = np.arange(256)[None, :]
    ok = (u - s >= 0) & (u - s <= 128)
    m = np.where(ok, 0.0, NEG).astype(np.float32)
    return np.ascontiguousarray(np.broadcast_to(m[:, None, :], (128, 2, 256)))


def _fm(v):
    return np.ascontiguousarray(np.asarray(v, np.float32).reshape(4, 128).T)


_NC_CACHE = {}


def kernel(x_prompt, x_sample, cache_k, cache_v, state_conv, state_h, w_in, conv_w, conv_b,
           w_ra, b_ra, w_ri, b_ri, lru_lambda, w_out, ln_g, ln_b):
    f32 = np.float32
    x_prompt = np.asarray(x_prompt, f32)
    x_sample = np.asarray(x_sample, f32)
    cache_k = np.asarray(cache_k, f32)
    cache_v = np.asarray(cache_v, f32)
    state_conv = np.asarray(state_conv, f32)
    state_h = np.asarray(state_h, f32)
    w_in0 = np.ascontiguousarray(np.asarray(w_in, f32)[0])
    w_out0 = np.ascontiguousarray(np.asarray(w_out, f32)[0])
    conv_w = np.asarray(conv_w, f32)[0]
    w_ra = np.asarray(w_ra, f32)[0]
    w_ri = np.asarray(w_ri, f32)[0]

    if "nc" not in _NC_CACHE:
        _NC_CACHE["nc"] = build_program(True)
    nc = _NC_CACHE["nc"]

    prm = np.zeros((128, 32), f32)
    for fc in range(4):
        for k in range(4):
            prm[:, fc * 4 + k] = conv_w[k, fc * 128:(fc + 1) * 128]
    prm[:, 16:20] = _fm(np.asarray(conv_b, f32)[0])
    prm[:, 20:24] = -_fm(np.asarray(b_ra, f32)[0])
    prm[:, 24:28] = -_fm(np.asarray(b_ri, f32)[0])
    prm[:, 28:32] = _fm(np.asarray(lru_lambda, f32)[0])

    def bd(w):
        o = np.zeros((128, 4, 128), f32)
        for fc in range(4):
            o[0:64, fc, 0:64] = w[2 * fc]
            o[64:128, fc, 64:128] = w[2 * fc + 1]
        return o

    wra_bd, wri_bd = bd(w_ra), bd(w_ri)
    lng = np.ascontiguousarray(np.broadcast_to(np.asarray(ln_g, f32)[0][None, :], (128, 1024)))
    lnb = np.ascontiguousarray(np.broadcast_to(np.asarray(ln_b, f32)[0][None, :], (128, 1024)))
    ident = np.eye(128, dtype=f32)
    band = _band_mask()
    allneg = np.full((128, 2, 256), NEG, f32)
    m1 = (np.arange(128)[:, None, None] >= np.arange(4)[None, :, None]).astype(f32)
    m1 = np.ascontiguousarray(np.broadcast_to(m1, (128, 4, 8)).reshape(128, 32))
    wn = np.zeros((4, 4), f32)
    for tp in range(4):
        for t in range(4):
            wn[tp, t] = 3.0 if tp == t else (1.0 if tp < t else 0.0)
    wnew = np.ascontiguousarray(np.broadcast_to(wn[:, :, None], (4, 4, 8)).reshape(4, 32))

    in_maps = []
    for c in range(8):
        b, h = c // 2, c % 2
        xo = np.ascontiguousarray(x_prompt[b, h * 2048:(h + 1) * 2048])
        xc = np.ascontiguousarray(x_prompt[b, 0:2048]) if h == 1 else np.zeros((2048, 1024), f32)
        sl = slice(16 * c, 16 * (c + 1))
        sconv = state_conv[0, sl]
        sconv_fm = np.ascontiguousarray(sconv.reshape(16, 3, 4, 128).transpose(3, 2, 0, 1))
        sh_fm = np.ascontiguousarray(state_h[0, sl].reshape(16, 4, 128).transpose(2, 1, 0))
        in_maps.append({
            "xo": xo, "xc": xc, "w_in": w_in0, "w_out": w_out0, "ident": ident,
            "mask_own": band, "mask_ctx": band if h == 1 else allneg, "prm": prm,
            "wra_bd": wra_bd, "wri_bd": wri_bd, "lng": lng, "lnb": lnb,
            "flag": np.full((128, 1), float(h), f32),
            "xs": np.ascontiguousarray(x_sample[sl].reshape(64, 1024)),
            "ck": np.ascontiguousarray(cache_k[0, sl].reshape(16, 2048, 512)),
            "cv": np.ascontiguousarray(cache_v[0, sl].reshape(16, 2048, 512)),
            "sconv": sconv_fm, "sh": sh_fm, "m1": m1, "wnew": wnew,
        })
    res = run_bass_kernel_spmd(nc, in_maps, core_ids=list(range(8)))
    R = res.results

    y_prompt = np.zeros((4, 4096, 1024), f32)
    k_prompt = np.zeros((1, 4, 2048, 8, 64), f32)
    v_prompt = np.zeros((1, 4, 2048, 8, 64), f32)
    conv_prompt = np.zeros((1, 4, 3, 512), f32)
    h_prompt = np.zeros((1, 4, 512), f32)
    y_sample = np.zeros((128, 4, 1024), f32)
    k_sample = np.zeros((1, 128, 4, 8, 64), f32)
    v_sample = np.zeros((1, 128, 4, 8, 64), f32)
    conv_sample = np.zeros((1, 128, 3, 512), f32)
    h_sample = np.zeros((1, 128, 512), f32)
    for c in range(8):
        b, h = c // 2, c % 2
        r = R[c]
        y_prompt[b, h * 2048:(h + 1) * 2048] = r["y"]
        if h == 1:
            k_prompt[0, b] = r["kT"].T.reshape(2048, 8, 64)
            v_prompt[0, b] = r["vT"].T.reshape(2048, 8, 64)
            conv_prompt[0, b] = r["convT"].transpose(2, 1, 0).reshape(3, 512)
            h_prompt[0, b] = r["hT"].T.reshape(512)
        sl = slice(16 * c, 16 * (c + 1))
        y_sample[sl] = r["ys"].reshape(16, 4, 1024)
        k_sample[0, sl] = r["ks"].reshape(16, 4, 8, 64)
        v_sample[0, sl] = r["vs"].reshape(16, 4, 8, 64)
        conv_sample[0, sl] = r["convs"].transpose(2, 3, 1, 0).reshape(16, 3, 512)
        h_sample[0, sl] = r["hs"].transpose(2, 1, 0).reshape(16, 512)
    return (y_prompt, y_sample, k_prompt, v_prompt, conv_prompt, h_prompt,
            k_sample, v_sample, conv_sample, h_sample)
```

```python
import contextlib
import numpy as np
import concourse.bass as bass
import concourse.mybir as mybir
from concourse.bass_utils import run_bass_kernel_spmd

F32 = mybir.dt.float32
BF16 = mybir.dt.bfloat16
ALU = mybir.AluOpType
AF = mybir.ActivationFunctionType
AX = mybir.AxisListType

ENGS = ("pe", "act", "dve", "pool", "sp")
N_DMA_SLOTS = 40
NEG = -30000.0
ALPHA = 2.0 ** 0.25
LN_EPS = 1e-5


class Op:
    __slots__ = ("eng", "fn", "deps", "is_dma", "slot", "slot_val", "has_dep", "val")

    def __init__(self, eng, fn, is_dma):
        self.eng = eng
        self.fn = fn
        self.deps = []
        self.is_dma = is_dma
        self.slot = None
        self.slot_val = 0
        self.has_dep = False
        self.val = 0


class Prog:
    def __init__(self):
        self.ops = {e: [] for e in ENGS}
        self.res = {}
        self.slot_last = [None] * N_DMA_SLOTS
        self.slot_cnt = [0] * N_DMA_SLOTS
        self.next_slot = 0

    def add(self, eng, fn, reads=(), writes=(), dma=False):
        op = Op(eng, fn, dma)
        deps = []
        for k in reads:
            st = self.res.get(k)
            if st is not None and st[0] is not None:
                deps.append(st[0])
            if st is not None and k.startswith("bank"):
                deps.extend(r for r in st[1] if r.eng != eng)
        for k in writes:
            st = self.res.get(k)
            if st is not None:
                if st[0] is not None:
                    deps.append(st[0])
                deps.extend(st[1])
        if dma:
            s = self.next_slot
            self.next_slot = (s + 1) % N_DMA_SLOTS
            prev = self.slot_last[s]
            if prev is not None:
                deps.append(prev)
            self.slot_cnt[s] += 1
            op.slot = s
            op.slot_val = 16 * self.slot_cnt[s]
            self.slot_last[s] = op
        seen = set()
        for d in deps:
            if d is op or id(d) in seen:
                continue
            seen.add(id(d))
            if (not d.is_dma) and d.eng == eng and eng == "pe":
                continue
            op.deps.append(d)
            d.has_dep = True
        for k in reads:
            st = self.res.setdefault(k, [None, []])
            st[1].append(op)
        for k in writes:
            self.res[k] = [op, []]
        self.ops[eng].append(op)
        return op

    def emit(self, nc):
        for e in ENGS:
            v = 0
            for op in self.ops[e]:
                if not op.is_dma and op.has_dep:
                    v += 1
                    op.val = v
        with contextlib.ExitStack() as st:
            esem = {e: st.enter_context(nc.semaphore("s_" + e)) for e in ENGS}
            dsem = [st.enter_context(nc.semaphore("d%d" % i)) for i in range(N_DMA_SLOTS)]
            block = st.enter_context(nc.Block())

            def run(e, eh):
                waited = {}
                for op in self.ops[e]:
                    for d in op.deps:
                        if d.is_dma:
                            key, val, sem = ("d", d.slot), d.slot_val, dsem[d.slot]
                        else:
                            key, val, sem = ("e", d.eng), d.val, esem[d.eng]
                        if waited.get(key, 0) >= val:
                            continue
                        waited[key] = val
                        eh.wait_ge(sem, val)
                    ins = op.fn(eh)
                    if op.is_dma:
                        ins.then_inc(dsem[op.slot], 16)
                    elif op.has_dep:
                        ins.then_inc(esem[e], 1)
                if e == "sp":
                    for s in range(N_DMA_SLOTS):
                        if self.slot_cnt[s] and waited.get(("d", s), 0) < 16 * self.slot_cnt[s]:
                            eh.wait_ge(dsem[s], 16 * self.slot_cnt[s])

            block.tensor(lambda eh: run("pe", eh))
            block.scalar(lambda eh: run("act", eh))
            block.vector(lambda eh: run("dve", eh))
            block.gpsimd(lambda eh: run("pool", eh))
            block.sync(lambda eh: run("sp", eh))


NTOK = 2048
NCTX = 2048
NS = 64
NB = 16
PATTERNS = (1, 4, 16)


def build_program(do_sample=True):
    nc = bass.Bass("TRN2", target_bir_lowering=False)
    P = Prog()

    def din(name, shape):
        return nc.dram_tensor(name, list(shape), F32, kind="ExternalInput").ap()

    def dout(name, shape):
        return nc.dram_tensor(name, list(shape), F32, kind="ExternalOutput").ap()

    xo = din("xo", [NTOK, 1024])
    xc = din("xc", [NCTX, 1024])
    w_in = din("w_in", [1024, 3072])
    w_out = din("w_out", [1024, 1024])
    ident_d = din("ident", [128, 128])
    mask_own_d = din("mask_own", [128, 2, 256])
    mask_ctx_d = din("mask_ctx", [128, 2, 256])
    prm_d = din("prm", [128, 32])
    wra_d = din("wra_bd", [128, 4, 128])
    wri_d = din("wri_bd", [128, 4, 128])
    lng_d = din("lng", [128, 1024])
    lnb_d = din("lnb", [128, 1024])
    flag_d = din("flag", [128, 1])
    xs_d = din("xs", [NS, 1024])
    ck_d = din("ck", [NB, 2048, 512])
    cv_d = din("cv", [NB, 2048, 512])
    sconv_d = din("sconv", [128, 4, NB, 3])
    sh_d = din("sh", [128, 4, NB])
    m1_d = din("m1", [128, 32])
    wnew_d = din("wnew", [4, 32])

    y_d = dout("y", [NTOK, 1024])
    kT_d = dout("kT", [512, NTOK])
    vT_d = dout("vT", [512, NTOK])
    convT_d = dout("convT", [128, 4, 3])
    hT_d = dout("hT", [128, 4])
    ys_d = dout("ys", [NS, 1024])
    qs_d = dout("qs", [NS, 512])
    ks_d = dout("ks", [NS, 512])
    vs_d = dout("vs", [NS, 512])
    convs_d = dout("convs", [128, 4, NB, 3])
    hs_d = dout("hs", [128, 4, NB])

    with contextlib.ExitStack() as st:
        def sb(name, shape, dt):
            return st.enter_context(nc.sbuf_tensor("sb_" + name, list(shape), dt))

        def psb(name, shape, dt):
            return st.enter_context(nc.psum_tensor("ps_" + name, list(shape), dt))

        KT = sb("KT", [128, 4, 4096], BF16)
        VT = sb("VT", [128, 4, 4096], BF16)
        QT = sb("QT", [128, 4, NTOK], BF16)
        GA = sb("GA", [128, 4, NTOK], BF16)
        ML = sb("ML", [128, 4, NTOK], BF16)
        ident = sb("identb", [128, 128], BF16)
        ones = sb("ones", [128, 128], BF16)
        mask_own = sb("mask_own", [128, 2, 256], BF16)
        mask_ctx = sb("mask_ctx", [128, 2, 256], BF16)
        prm = sb("prm", [128, 32], F32)
        cneg = sb("cneg", [128, 8], F32)
        wra = sb("wra", [128, 4, 128], BF16)
        wri = sb("wri", [128, 4, 128], BF16)
        flag = sb("flag", [128, 1], F32)
        hstate = sb("hstate", [128, 4], F32)
        zcol = sb("zcol", [128, 1], F32)
        XL = sb("XL", [128, 3 + 512], F32)
        HALO = sb("HALO", [128, 4, 3], F32)
        SIGG = sb("SIGG", [128, 512], F32)
        arena = sb("arena", [128, 20224], F32)

        WIN = arena[:, 0:12288].bitcast(BF16).rearrange("p (c n) -> p c n", c=8)
        XTC = arena[:, 12288:14336].bitcast(BF16).rearrange("p (c n) -> p c n", c=8)
        XSTs = [arena[:, 14336:14848].bitcast(BF16), arena[:, 19712:20224].bitcast(BF16)]
        LTs = [[arena[:, 14848 + 1536 * s_ + 256 * k: 14848 + 1536 * s_ + 256 * (k + 1)] for k in range(6)] for s_ in range(2)]
        LT2s = [arena[:, 14848 + 1536 * s_ + 256: 14848 + 1536 * s_ + 768] for s_ in range(2)]
        SIGT = arena[:, 17920:18432]
        STG = [arena[:, 18432:18944], arena[:, 18944:19456]]
        UBs = [arena[:, 19456:19584].bitcast(BF16), arena[:, 19584:19712].bitcast(BF16)]
        ACC = arena[:, 0:4096].rearrange("p (a n) -> p a n", a=2)
        QZ = arena[:, 4096:6144].bitcast(BF16).rearrange("p (a n) -> p a n", a=2)
        PTt = [arena[:, 6144 + 256 * i: 6144 + 256 * (i + 1)].bitcast(BF16).rearrange("p (a n) -> p a n", a=2)
               for i in range(2)]
        VG = [arena[:, 6656 + 64 * i: 6656 + 64 * (i + 1)].bitcast(BF16) for i in range(2)]
        RD = arena[:, 7168:9216]
        WOUT = arena[:, 9216:13312].bitcast(BF16).rearrange("p (c n) -> p c n", c=8)
        SKS = [arena[:, 13312 + 1024 * i: 13312 + 1024 * (i + 1)].bitcast(BF16).rearrange("p (t c) -> p t c", t=4) for i in range(2)]
        SVS = [arena[:, 15360 + 1024 * i: 15360 + 1024 * (i + 1)].bitcast(BF16).rearrange("p (t c) -> p t c", t=4) for i in range(2)] + \
              [arena[:, 9216 + 1024 * i: 9216 + 1024 * (i + 1)].bitcast(BF16).rearrange("p (t c) -> p t c", t=4) for i in range(2)]
        SQB = arena[:, 17408:18432].bitcast(BF16).rearrange("p (t c) -> p t c", t=4)
        SPR = arena[:, 18432:19456].bitcast(BF16).rearrange("p (t c) -> p t c", t=4)
        XR = [arena[:, 13312 + 1024 * i: 13312 + 1024 * (i + 1)] for i in range(2)]
        ZT = [arena[:, 15360 + 1024 * i: 15360 + 1024 * (i + 1)] for i in range(2)]
        LNG = arena[:, 17408:18432]
        LNB = arena[:, 18432:19456]
        STAT = arena[:, 19456:19520]
        PKEYS = ["XTC", "XST0", "XST1", "UB0", "UB1", "SIGT", "STG0", "STG1"] + ["L%d_%d" % (a_, k_) for a_ in range(2) for k_ in range(6)]
        SKEYS = ["SKS0", "SKS1", "SVS0", "SVS1", "SQB", "SPR"]
        OKEYS = ["XR0", "XR1", "ZT0", "ZT1", "LNG", "LNB", "STAT0", "STAT1"]

        banks = [psb("bank%d" % i, [128, 512], F32) for i in range(8)]
        banks_bf = [bk[:, :].bitcast(BF16) for bk in banks]

        def bkey(i):
            return "bank%d" % i

        w_in_v = w_in.rearrange("(c p) n -> p c n", p=128)

        def load_win(g):
            P.add("pool", lambda e: e.dma_start(out=WIN[:, :, g * 512:(g + 1) * 512], in_=w_in_v[:, :, g * 512:(g + 1) * 512]), writes=["WIN%d" % g], dma=True)
        for g in (1, 2, 4):
            load_win(g)
        P.add("pool", lambda e: e.dma_start(out=ident[:], in_=ident_d), writes=["ident"], dma=True)
        P.add("pool", lambda e: e.dma_start(out=mask_own[:], in_=mask_own_d), writes=["mask_own"], dma=True)
        P.add("pool", lambda e: e.dma_start(out=mask_ctx[:], in_=mask_ctx_d), writes=["mask_ctx"], dma=True)
        P.add("pool", lambda e: e.dma_start(out=wra[:], in_=wra_d), writes=["wra"], dma=True)
        P.add("pool", lambda e: e.dma_start(out=wri[:], in_=wri_d), writes=["wri"], dma=True)
        P.add("sp", lambda e: e.dma_start(out=prm[:], in_=prm_d), writes=["prm"], dma=True)
        P.add("sp", lambda e: e.dma_start(out=flag[:], in_=flag_d), writes=["flag"], dma=True)
        P.add("dve", lambda e: e.memset(ones[:], 1.0), writes=["ones"])
        P.add("dve", lambda e: e.memset(zcol[:], 0.0), writes=["zcol"])
        P.add("dve", lambda e: e.memset(hstate[:], 0.0), writes=["hs0", "hs1", "hs2", "hs3"])
        P.add("dve", lambda e: e.memset(HALO[:], 0.0), writes=["HALO0", "HALO1", "HALO2", "HALO3"])
        P.add("act", lambda e: e.activation(out=cneg[:, 0:4], in_=prm[:, 28:32], func=AF.Exp, scale=-1.0), reads=["prm"], writes=["cneg"])
        P.add("act", lambda e: e.activation(out=cneg[:, 0:4], in_=cneg[:, 0:4], func=AF.Ln, bias=1.0), reads=["cneg"], writes=["cneg"])
        P.add("dve", lambda e: e.tensor_scalar(out=cneg[:, 4:8], in0=cneg[:, 0:4], scalar1=-16.0, scalar2=None, op0=ALU.mult), reads=["cneg"], writes=["cneg"])
        P.add("dve", lambda e: e.tensor_scalar(out=cneg[:, 0:4], in0=cneg[:, 0:4], scalar1=-8.0, scalar2=None, op0=ALU.mult), reads=["cneg"], writes=["cneg"])

        rr = {"ev": 0, "stg": 0, "xst": 0}

        def next_bank():
            b_ = 4 + (rr["ev"] % 4)
            rr["ev"] += 1
            return b_

        def next_stg():
            i = rr["stg"] % 2
            rr["stg"] += 1
            return STG[i], "STG%d" % i

        def lk(s_, k):
            return "L%d_%d" % (s_, k)

        GBANKS = [2, 3]

        def gates_s1a(fc, n, s_):
            u = LTs[s_][0][:, 0:n]
            ub = UBs[s_][:, 0:n]
            gb = GBANKS[s_]
            gr, gi = banks[gb][:, 0:256], banks[gb][:, 256:512]
            P.add("dve", lambda e: e.tensor_copy(out=ub, in_=u), reads=[lk(s_, 0)], writes=["UB%d" % s_])
            P.add("pe", lambda e: e.matmul(gr[:, 0:n], lhsT=wra[:, fc, :], rhs=ub, start=True, stop=True, skip_group_check=True), reads=["wra", "UB%d" % s_], writes=[bkey(gb)])
            P.add("pe", lambda e: e.matmul(gi[:, 0:n], lhsT=wri[:, fc, :], rhs=ub, start=False, stop=True, skip_group_check=True), reads=["wri", "UB%d" % s_], writes=[bkey(gb)])

        def gates_s1b(fc, n, s_):
            gb = GBANKS[s_]
            gr, gi = banks[gb][:, 0:256], banks[gb][:, 256:512]
            er, ei, a = LTs[s_][1][:, 0:n], LTs[s_][2][:, 0:n], LTs[s_][4][:, 0:n]
            P.add("act", lambda e: e.activation(out=er, in_=gr[:, 0:n], func=AF.Exp, scale=-1.0, bias=prm[:, 20 + fc:21 + fc]), reads=[bkey(gb), "prm"], writes=[lk(s_, 1)])
            P.add("act", lambda e: e.activation(out=ei, in_=gi[:, 0:n], func=AF.Exp, scale=-1.0, bias=prm[:, 24 + fc:25 + fc]), reads=[bkey(gb), "prm"], writes=[lk(s_, 2)])
            if n == 256:
                both = LT2s[s_]
                P.add("act", lambda e: e.activation(out=both, in_=both, func=AF.Ln, bias=1.0), reads=[lk(s_, 1), lk(s_, 2)], writes=[lk(s_, 1), lk(s_, 2)])
                P.add("act", lambda e: e.activation(out=both, in_=both, func=AF.Exp, scale=-1.0), reads=[lk(s_, 1), lk(s_, 2)], writes=[lk(s_, 1), lk(s_, 2)])
            else:
                P.add("act", lambda e: e.activation(out=er, in_=er, func=AF.Ln, bias=1.0), reads=[lk(s_, 1)], writes=[lk(s_, 1)])
                P.add("act", lambda e: e.activation(out=ei, in_=ei, func=AF.Ln, bias=1.0), reads=[lk(s_, 2)], writes=[lk(s_, 2)])
                P.add("act", lambda e: e.activation(out=er, in_=er, func=AF.Exp, scale=-1.0), reads=[lk(s_, 1)], writes=[lk(s_, 1)])
                P.add("act", lambda e: e.activation(out=ei, in_=ei, func=AF.Exp, scale=-1.0), reads=[lk(s_, 2)], writes=[lk(s_, 2)])
            P.add("act", lambda e: e.activation(out=a, in_=er, func=AF.Exp, scale=cneg[:, fc:fc + 1]), reads=[lk(s_, 1), "cneg"], writes=[lk(s_, 4)])

        def gates_s2(fc, n, s_):
            u = LTs[s_][0][:, 0:n]
            ei, a, t1, bx = LTs[s_][2][:, 0:n], LTs[s_][4][:, 0:n], LTs[s_][3][:, 0:n], LTs[s_][5][:, 0:n]
            P.add("dve", lambda e: e.scalar_tensor_tensor(out=t1, in0=a, scalar=-1.0, in1=a, op0=ALU.mult, op1=ALU.mult), reads=[lk(s_, 4)], writes=[lk(s_, 3)])
            P.add("dve", lambda e: e.tensor_tensor(out=bx, in0=ei, in1=u, op=ALU.mult), reads=[lk(s_, 2), lk(s_, 0)], writes=[lk(s_, 5)])
            P.add("act", lambda e: e.activation(out=t1, in_=t1, func=AF.Ln, bias=1.0), reads=[lk(s_, 3)], writes=[lk(s_, 3)])
            P.add("act", lambda e: e.activation(out=t1, in_=t1, func=AF.Exp, scale=0.5), reads=[lk(s_, 3)], writes=[lk(s_, 3)])
            P.add("dve", lambda e: e.tensor_tensor(out=bx, in0=bx, in1=t1, op=ALU.mult), reads=[lk(s_, 5), lk(s_, 3)], writes=[lk(s_, 5)])
            return a, bx

        def sigmoid_times(src_bank, n, out_ap, out_key):
            eg = SIGT[:, 0:n]
            gp = banks[src_bank][:, 0:n]
            P.add("act", lambda e: e.activation(out=eg, in_=gp, func=AF.Exp, scale=-1.0), reads=[bkey(src_bank)], writes=["SIGT"])
            P.add("act", lambda e: e.activation(out=eg, in_=eg, func=AF.Ln, bias=1.0), reads=["SIGT"], writes=["SIGT"])
            P.add("act", lambda e: e.activation(out=eg, in_=eg, func=AF.Exp, scale=-1.0), reads=["SIGT"], writes=["SIGT"])
            P.add("dve", lambda e: e.tensor_tensor(out=out_ap, in0=eg, in1=gp, op=ALU.mult), reads=["SIGT", bkey(src_bank)], writes=[out_key])

        def proj(col0, n, xt_ap, bank, xkey="XTC"):
            def one(kc):
                P.add("pe", lambda e: e.matmul(banks[bank][:, 0:n], lhsT=WIN[:, kc, col0:col0 + 128], rhs=xt_ap[:, kc, :],
                                               start=(kc == 0), stop=(kc == 7)), reads=["WIN%d" % (col0 // 512), xkey], writes=[bkey(bank)])
            for kc in range(8):
                one(kc)

        def load_xblock(src_rows, nrow, dst_ap, dst_key):
            xi = rr["xst"] % 2
            rr["xst"] += 1
            xst = XSTs[xi]
            xk = "XST%d" % xi
            P.add("pool", lambda e: e.dma_start(out=xst[0:nrow, :], in_=src_rows), writes=[xk], dma=True)
            pt = banks_bf[xi]

            def one(kc):
                P.add("pe", lambda e: e.transpose(pt[:, kc * nrow:(kc + 1) * nrow], xst[0:nrow, kc * 128:(kc + 1) * 128], ident[0:nrow, 0:nrow]),
                      reads=[xk, "ident"], writes=[bkey(xi)])
            for kc in range(8):
                one(kc)
            srcv = pt[:, 0:8 * nrow].rearrange("p (c n) -> p c n", c=8)
            P.add("dve", lambda e: e.tensor_copy(out=dst_ap, in_=srcv), reads=[bkey(xi)], writes=[dst_key])

        def kv_proj(tc, which, pr):
            own = tc >= 4
            T0 = tc * 512
            o0 = T0 - 2048
            dstT = KT if which == 0 else VT
            dkey = "KT" if which == 0 else "VT"
            dout_ap = kT_d if which == 0 else vT_d
            bank = next_bank()
            proj(512 + which * 512 + pr * 128, 512, XTC, bank)
            if own:
                stg, sk = next_stg()
                P.add("act", lambda e: e.activation(out=stg, in_=banks[bank][:, :], func=AF.Copy), reads=[bkey(bank)], writes=[sk])
                P.add("sp", lambda e: e.dma_start(out=dout_ap[pr * 128:(pr + 1) * 128, o0:o0 + 512], in_=stg), reads=[sk], dma=True)
                P.add("dve", lambda e: e.tensor_copy(out=dstT[:, pr, T0:T0 + 512], in_=banks[bank][:, :]), reads=[bkey(bank)], writes=[dkey])
            else:
                P.add("act", lambda e: e.activation(out=dstT[:, pr, T0:T0 + 512], in_=banks[bank][:, :], func=AF.Copy), reads=[bkey(bank)], writes=[dkey])

        def q_proj(tc, pr):
            o0 = tc * 512 - 2048
            bank = next_bank()
            proj(pr * 128, 512, XTC, bank)
            P.add("act", lambda e: e.activation(out=QT[:, pr, o0:o0 + 512], in_=banks[bank][:, :], func=AF.Copy), reads=[bkey(bank)], writes=["QT"])

        def ga_proj(tc, pr):
            o0 = tc * 512 - 2048
            bank = next_bank()
            proj(1536 + pr * 128, 512, XTC, bank)
            sigmoid_times(bank, 512, GA[:, pr, o0:o0 + 512], "GA")

        item_ctr = [0]

        def lru_item(tc, fc, h, fillers):
            own = tc >= 4
            o0 = tc * 512 - 2048
            c0 = 256 * h
            s_ = h
            hk = "hs%d" % fc
            hak = "HALO%d" % fc
            u = LTs[s_][0][:, :]
            tmp = LTs[s_][5][:, :]
            cw = lambda k: prm[:, fc * 4 + k: fc * 4 + k + 1]

            def stage1a():
                if h == 0:
                    bank = next_bank()
                    proj(2048 + fc * 128, 512, XTC, bank)
                    P.add("act", lambda e: e.activation(out=XL[:, 3:515], in_=banks[bank][:, :], func=AF.Copy), reads=[bkey(bank)], writes=["XL"])
                    P.add("dve", lambda e: e.tensor_copy(out=XL[:, 0:3], in_=HALO[:, fc, :]), reads=[hak], writes=["XL"])
                P.add("dve", lambda e: e.tensor_scalar(out=u, in0=XL[:, c0:c0 + 256], scalar1=cw(0), scalar2=prm[:, 16 + fc:17 + fc],
                                                        op0=ALU.mult, op1=ALU.add), reads=["XL", "prm"], writes=[lk(s_, 0)])

                def tap(k):
                    P.add("dve", lambda e: e.scalar_tensor_tensor(out=u, in0=XL[:, c0 + k:c0 + k + 256], scalar=cw(k), in1=u, op0=ALU.mult, op1=ALU.add),
                          reads=["XL", lk(s_, 0), "prm"], writes=[lk(s_, 0)])
                for k in range(1, 4):
                    tap(k)
                if h == 1:
                    P.add("dve", lambda e: e.tensor_copy(out=HALO[:, fc, :], in_=XL[:, 512:515]), reads=["XL"], writes=[hak])
                gates_s1a(fc, 256, s_)

            def stage1b():
                gates_s1b(fc, 256, s_)
                for f in fillers:
                    f()

            def stage2():
                a, bx = gates_s2(fc, 256, s_)
                hseq = LTs[s_][1][:, :]
                if h == 0:
                    init_ap, init_key = hstate[:, fc:fc + 1], hk
                else:
                    init_ap, init_key = LTs[0][1][:, 255:256], lk(0, 1)
                P.add("dve", lambda e: e.tensor_tensor_scan(out=hseq, data0=a, data1=bx, initial=init_ap, op0=ALU.mult, op1=ALU.add),
                      reads=[lk(s_, 4), lk(s_, 5), init_key], writes=[lk(s_, 1)])
                if h == 1:
                    if tc == 3:
                        P.add("dve", lambda e: e.tensor_tensor(out=hstate[:, fc:fc + 1], in0=hseq[:, 255:256], in1=flag[:, 0:1], op=ALU.mult),
                              reads=[lk(s_, 1), "flag"], writes=[hk])
                    else:
                        P.add("dve", lambda e: e.tensor_copy(out=hstate[:, fc:fc + 1], in_=hseq[:, 255:256]), reads=[lk(s_, 1)], writes=[hk])
                if own:
                    if h == 0:
                        bank2 = next_bank()
                        proj(2560 + fc * 128, 512, XTC, bank2)
                        sigmoid_times(bank2, 512, SIGG[:, :], "SIGG")
                    P.add("dve", lambda e: e.tensor_tensor(out=ML[:, fc, o0 + c0:o0 + c0 + 256], in0=SIGG[:, c0:c0 + 256], in1=hseq, op=ALU.mult),
                          reads=["SIGG", lk(s_, 1)], writes=["ML"])
            return stage1a, stage1b, stage2

        prev2 = None
        for tc in range(8):
            srcx = xo if tc >= 4 else xc
            for tb in range(4):
                r0 = (tc % 4) * 512 + tb * 128
                load_xblock(srcx[r0:r0 + 128, :], 128, XTC[:, :, tb * 128:(tb + 1) * 128], "XTC")
            if tc == 0:
                for g in (0, 3, 5):
                    load_win(g)
            others = [(lambda which=which, pr=pr, tc=tc: kv_proj(tc, which, pr)) for which in range(2) for pr in range(4)]
            if tc >= 4:
                others += [(lambda pr=pr, tc=tc: q_proj(tc, pr)) for pr in range(4)]
                others += [(lambda pr=pr, tc=tc: ga_proj(tc, pr)) for pr in range(4)]
            per = (len(others) + 7) // 8
            idx = 0
            for fc in range(4):
                for h in range(2):
                    s1a, s1b, s2 = lru_item(tc, fc, h, others[idx * per:(idx + 1) * per])
                    idx += 1
                    s1a()
                    if prev2 is not None:
                        prev2()
                    s1b()
                    prev2 = s2
        prev2()
        P.add("sp", lambda e: e.dma_start(out=convT_d, in_=HALO[:]), reads=["HALO0", "HALO1", "HALO2", "HALO3"], dma=True)
        P.add("sp", lambda e: e.dma_start(out=hT_d, in_=hstate[:]), reads=["hs0", "hs1", "hs2", "hs3"], dma=True)

        SXT = sb("SXT", [128, 8, NS], BF16)
        SGA = sb("SGA", [128, 4, NS], BF16)
        SML = sb("SML", [128, 4, NS], BF16)
        SMA = sb("SMA", [128, 4, NS], BF16)
        SXL = sb("SXL", [128, 4, NB, 7], F32)
        SH = sb("SH", [128, 4, NB], F32)
        SHO = sb("SHO", [128, 4, NB], F32)

        def s_qkv(j):
            bank = next_bank()

            def one(kc):
                P.add("pe", lambda e: e.matmul(banks[bank][0:NS, :], lhsT=SXT[:, kc, :], rhs=WIN[:, kc, j * 512:(j + 1) * 512],
                                               start=(kc == 0), stop=(kc == 7)), reads=["WIN%d" % j, "SXT"], writes=[bkey(bank)])
            for kc in range(8):
                one(kc)
            stg_full, sk = next_stg()
            stg = stg_full[0:NS, :]
            P.add("act", lambda e: e.activation(out=stg, in_=banks[bank][0:NS, :], func=AF.Copy), reads=[bkey(bank)], writes=[sk])
            P.add("sp", lambda e: e.dma_start(out=(qs_d, ks_d, vs_d)[j], in_=stg), reads=[sk], writes=["scr%d" % j], dma=True)

        def s_ga(pr):
            bank = next_bank()
            proj(1536 + pr * 128, NS, SXT, bank, xkey="SXT")
            sigmoid_times(bank, NS, SGA[:, pr, :], "SGA")

        def s_lru(fc):
            n = NS
            bank = next_bank()
            proj(2048 + fc * 128, NS, SXT, bank, xkey="SXT")
            P.add("act", lambda e: e.activation(out=SXL[:, fc, :, 3:7], in_=banks[bank][:, 0:NS].rearrange("p (b t) -> p b t", t=4), func=AF.Copy),
                  reads=[bkey(bank)], writes=["SXL"])
            u = LTs[0][0][:, 0:n]
            u3 = u.rearrange("p (b t) -> p b t", t=4)
            cw = lambda k: prm[:, fc * 4 + k: fc * 4 + k + 1]
            P.add("dve", lambda e: e.tensor_scalar(out=u3, in0=SXL[:, fc, :, 0:4], scalar1=cw(0), scalar2=prm[:, 16 + fc:17 + fc],
                                                    op0=ALU.mult, op1=ALU.add), reads=["SXL", "prm"], writes=[lk(0, 0)])

            def tap(k):
                P.add("dve", lambda e: e.scalar_tensor_tensor(out=u3, in0=SXL[:, fc, :, k:k + 4], scalar=cw(k), in1=u3, op0=ALU.mult, op1=ALU.add),
                      reads=["SXL", lk(0, 0), "prm"], writes=[lk(0, 0)])
            for k in range(1, 4):
                tap(k)
            gates_s1a(fc, n, 0)
            gates_s1b(fc, n, 0)
            a, bx = gates_s2(fc, n, 0)
            a3 = a.rearrange("p (b t) -> p b t", t=4)
            b3 = bx.rearrange("p (b t) -> p b t", t=4)
            hh = LTs[0][1][:, 0:n]
            h3 = hh.rearrange("p (b t) -> p b t", t=4)

            def step(t):
                prev = SH[:, fc, :] if t == 0 else h3[:, :, t - 1]
                P.add("dve", lambda e: e.tensor_tensor(out=h3[:, :, t], in0=a3[:, :, t], in1=prev, op=ALU.mult), reads=[lk(0, 4), lk(0, 1), "SH"], writes=[lk(0, 1)])
                P.add("dve", lambda e: e.tensor_tensor(out=h3[:, :, t], in0=h3[:, :, t], in1=b3[:, :, t], op=ALU.add), reads=[lk(0, 5), lk(0, 1)], writes=[lk(0, 1)])
            for t in range(4):
                step(t)
            P.add("dve", lambda e: e.tensor_copy(out=SHO[:, fc, :], in_=h3[:, :, 3]), reads=[lk(0, 1)], writes=["SHO"])
            bank2 = next_bank()
            proj(2560 + fc * 128, NS, SXT, bank2, xkey="SXT")
            sigmoid_times(bank2, NS, SIGG[:, 0:NS], "SIGG")
            P.add("dve", lambda e: e.tensor_tensor(out=SML[:, fc, :], in0=SIGG[:, 0:NS], in1=hh, op=ALU.mult), reads=["SIGG", lk(0, 1)], writes=["SML"])

        if do_sample:
            load_xblock(xs_d, NS, SXT[:], "SXT")
            for j in range(3):
                s_qkv(j)
            for pr in range(4):
                s_ga(pr)
            P.add("sp", lambda e: e.dma_start(out=SXL[:, :, :, 0:3], in_=sconv_d), writes=["SXL"], dma=True)
            P.add("sp", lambda e: e.dma_start(out=SH[:], in_=sh_d), writes=["SH"], dma=True)
            for fc in range(4):
                s_lru(fc)
            P.add("sp", lambda e: e.dma_start(out=hs_d, in_=SHO[:]), reads=["SHO"], dma=True)
            P.add("sp", lambda e: e.dma_start(out=convs_d, in_=SXL[:, :, :, 4:7]), reads=["SXL"], dma=True)

        P.add("dve", lambda e: e.memset(STAT, 0.0), writes=PKEYS + SKEYS)

        P.add("dve", lambda e: e.memset(QZ, 0.0), reads=SKEYS[:1], writes=["QZ", "SVS2", "SVS3"] + ["WIN%d" % g for g in range(6)])
        unit_i = [0]

        sample_tasks = []
        task_ctr = [0]

        pipe = [None, None, None]

        def pop_sample_task(flush=False):
            nxt = sample_tasks.pop(0) if sample_tasks else None
            if pipe[2] is not None:
                pipe[2][3]()
            if pipe[1] is not None:
                pipe[1][2]()
            if pipe[0] is not None:
                pipe[0][1]()
            if nxt is not None:
                nxt[0]()
            pipe[2], pipe[1], pipe[0] = pipe[1], pipe[0], nxt

        pending_pieces = []

        def flush_pieces(nmax):
            for _ in range(min(nmax, len(pending_pieces))):
                pending_pieces.pop(0)()

        def maybe_sample_task():
            task_ctr[0] += 1
            if task_ctr[0] % 4 == 0 and (sample_tasks or any(p_ is not None for p_ in pipe)):
                flush_pieces(len(pending_pieces))
                pop_sample_task()
            else:
                flush_pieces(2)

        def att_unit(pr, pi, d, r, kb, nbq):
            is_ctx = kb == nbq - 1
            is_last = kb == 2 * nbq - 1
            u0 = 128 if is_ctx else 0
            n = 128 if (is_ctx or is_last) else 256
            ui = unit_i[0]
            unit_i[0] += 1
            sbank = 4 + ui % 2
            S = banks[sbank][:, :].rearrange("p (a n) -> p a n", a=2)
            msk = mask_ctx if is_ctx else mask_own
            kstart = r + 128 * d * kb
            keys = KT[:, pr, kstart: kstart + 127 * d + 1: d]
            qstart = r + d * (128 * kb + u0) - 2048
            qs_ap = QZ[:, :, qstart: qstart + (n - 1) * d + 1: d]
            vi = ui % 2
            vg = VG[vi]
            vbank = 6 + vi
            vps = banks_bf[vbank][:, 0:128]
            pt_t = PTt[ui % 2]
            pk = "PT%d" % (ui % 2)
            vk = "VG%d" % vi

            def stage1():
                P.add("pe", lambda e: e.transpose(vps, VT[:, pr, kstart: kstart + 127 * d + 1: d], ident[:]), reads=["VT", "ident"], writes=[bkey(vbank)])
                P.add("pe", lambda e: e.matmul(S[:, :, 0:n], lhsT=ident[:], rhs=msk[:, :, u0:u0 + n], start=True, stop=False),
                      reads=["ident", "mask_own", "mask_ctx"], writes=[bkey(sbank)])
                P.add("pe", lambda e: e.matmul(S[:, :, 0:n], lhsT=keys, rhs=qs_ap, start=False, stop=True), reads=["KT", "QZ"], writes=[bkey(sbank)])
                P.add("act", lambda e: e.activation(out=vg, in_=vps, func=AF.Copy), reads=[bkey(vbank)], writes=[vk])
                P.add("act", lambda e: e.activation(out=pt_t[:, :, 0:n], in_=S[:, :, 0:n], func=AF.Exp, scale=0.125), reads=[bkey(sbank)], writes=[pk])

            def pv(sub):
                qb = kb + (u0 // 128) + sub
                first = (qb == kb + 1)
                obank = qb % 2
                OD = banks[obank]
                c0 = sub * 128
                okey = bkey(obank)
                P.add("pe", lambda e: e.matmul(OD[0:64, 0:128], lhsT=vg[:, 0:64], rhs=pt_t[:, 0, c0:c0 + 128], start=first, stop=False, skip_group_check=True),
                      reads=[vk, pk], writes=[okey])
                P.add("pe", lambda e: e.matmul(OD[64:128, 0:128], lhsT=vg[:, 64:128], rhs=pt_t[:, 1, c0:c0 + 128], start=first, stop=False, skip_group_check=True),
                      reads=[vk, pk], writes=[okey])
                P.add("pe", lambda e: e.matmul(OD[0:64, 128:256], lhsT=ones[:, 0:64], rhs=pt_t[:, 0, c0:c0 + 128], start=False, stop=False, skip_group_check=True),
                      reads=["ones", pk], writes=[okey])
                P.add("pe", lambda e: e.matmul(OD[64:128, 128:256], lhsT=ones[:, 0:64], rhs=pt_t[:, 1, c0:c0 + 128], start=False, stop=False, skip_group_check=True),
                      reads=["ones", pk], writes=[okey])
                if not first:
                    t0 = r + d * 128 * qb - 2048
                    dst = ACC[:, :, t0: t0 + 127 * d + 1: d]
                    srcv = OD[:, 0:256].rearrange("p (a n) -> p a n", a=2)
                    if pi == 0:
                        P.add("act", lambda e: e.activation(out=dst, in_=srcv, func=AF.Copy), reads=[okey], writes=["ACC"])
                    else:
                        P.add("dve", lambda e: e.tensor_tensor(out=dst, in0=srcv, in1=dst, op=ALU.add), reads=[okey, "ACC"], writes=["ACC"])

            def stage2():
                for sub in range(n // 128):
                    pv(sub)
            return stage1, stage2

        def att_pair(pr):
            P.add("act", lambda e: e.activation(out=QZ[0:64, 0, :], in_=QT[0:64, pr, :], func=AF.Copy), reads=["QT"], writes=["QZ"])
            P.add("act", lambda e: e.activation(out=QZ[64:128, 1, :], in_=QT[64:128, pr, :], func=AF.Copy), reads=["QT"], writes=["QZ"])
            units = []
            for pi, d in enumerate(PATTERNS):
                nbq = NTOK // (128 * d)
                for r in range(d):
                    for kb in range(nbq - 1, 2 * nbq):
                        units.append((pr, pi, d, r, kb, nbq))
            prev2 = None
            for uargs in units:
                s1, s2 = att_unit(*uargs)
                s1()
                if prev2 is not None:
                    prev2()
                prev2 = s2
                maybe_sample_task()
            prev2()
            P.add("act", lambda e: e.activation(out=RD, in_=ACC[:, 1, :], func=AF.Ln), reads=["ACC"], writes=["RD"])
            P.add("act", lambda e: e.activation(out=RD, in_=RD, func=AF.Exp, scale=-1.0), reads=["RD"], writes=["RD"])
            P.add("dve", lambda e: e.tensor_tensor(out=RD, in0=RD, in1=ACC[:, 0, :], op=ALU.mult), reads=["RD", "ACC"], writes=["RD"])
            P.add("dve", lambda e: e.tensor_tensor(out=GA[:, pr, :], in0=RD, in1=GA[:, pr, :], op=ALU.mult), reads=["RD", "GA"], writes=["GA"])

        KN = sb("KN", [4, 512], BF16)
        VN = sb("VN", [4, 512], BF16)
        SC = sb("SC", [128, 4, 32], F32)
        PS = sb("PS", [128, 4, 32], BF16)
        M1 = sb("M1", [128, 32], F32)
        WN = sb("WN", [4, 32], F32)
        OSEL = sb("OSEL", [128, 2, 16], F32)
        qs_flat = qs_d.rearrange("(b t) c -> b (t c)", t=4)
        sbuf_i = [0]

        def sample_b_tasks(b):
            obank = 2 + b % 2
            OD = banks[obank]
            okey = bkey(obank)
            firstmm = [True]

            def mm(out_ap, lhsT, rhs, rd):
                f = firstmm[0]
                firstmm[0] = False
                P.add("pe", lambda e: e.matmul(out_ap, lhsT=lhsT, rhs=rhs, start=f, stop=False, skip_group_check=True), reads=rd, writes=[okey])

            def slot_fns(slot):
                st_ = {}
                psk = "PS%d" % slot

                def stage_a0():
                    if slot == 0:
                        P.add("pool", lambda e: e.dma_start(out=SQB.rearrange("p t c -> p (t c)"), in_=qs_flat[b:b + 1, :].broadcast_to([128, 2048])),
                              reads=["scr0"], writes=["SQB"], dma=True)
                    if slot == 3:
                        P.add("pool", lambda e: e.dma_start(out=KN[:], in_=ks_d[4 * b:4 * b + 4, :]), reads=["scr1"], writes=["KN"], dma=True)
                        P.add("pool", lambda e: e.dma_start(out=VN[:], in_=vs_d[4 * b:4 * b + 4, :]), reads=["scr2"], writes=["VN"], dma=True)
                    if slot < 3:
                        ki = sbuf_i[0] % 2
                        vi_ = sbuf_i[0] % 4
                        sbuf_i[0] += 1
                        ks, vs = SKS[ki], SVS[vi_]
                        kk, vk = "SKS%d" % ki, "SVS%d" % vi_
                        if slot == 0:
                            ksrc = ck_d[b, :, :].rearrange("(i t) c -> i t c", t=16)[:, 0:4, :]
                            vsrc = cv_d[b, :, :].rearrange("(i t) c -> i t c", t=16)[:, 0:4, :]
                            kdst, vdst = ks, vs
                        elif slot == 1:
                            ksrc = ck_d[b, 1536:2048, :].rearrange("(i t) c -> i t c", t=4)
                            vsrc = cv_d[b, 1536:2048, :].rearrange("(i t) c -> i t c", t=4)
                            kdst, vdst = ks, vs
                        else:
                            ksrc = ck_d[b, 1920:2048, :]
                            vsrc = cv_d[b, 1920:2048, :]
                            kdst, vdst = ks[:, 0, :], vs[:, 0, :]
                        P.add("pool", lambda e: e.dma_start(out=kdst, in_=ksrc), writes=[kk], dma=True)
                        P.add("pool", lambda e: e.dma_start(out=vdst, in_=vsrc), writes=[vk], dma=True)
                        st_["kin"] = ks if slot < 2 else ks[:, 0:1, :].broadcast_to([128, 4, 512])
                        st_["kk"] = kk
                        st_["npart"] = 128
                        st_["vs"], st_["vk"] = vs, vk
                    else:
                        st_["kin"] = KN[:].unsqueeze(1).broadcast_to([4, 4, 512])
                        st_["kk"] = "KN"
                        st_["npart"] = 4

                def stage_a1():
                    kin, kk, npart = st_["kin"], st_["kk"], st_["npart"]
                    P.add("dve", lambda e: e.tensor_tensor(out=SPR[0:npart], in0=kin, in1=SQB[0:npart], op=ALU.mult), reads=[kk, "SQB"], writes=["SPR"])

                    def piece(t):
                        P.add("dve", lambda e: e.tensor_reduce(out=SC[0:npart, slot, t * 8:(t + 1) * 8],
                                                               in_=SPR[0:npart, t, :].rearrange("p (h d) -> p h d", d=64),
                                                               axis=AX.X, op=ALU.add), reads=["SPR"], writes=["SC%d" % slot])
                    if npart == 4:
                        for t in range(4):
                            piece(t)
                    else:
                        for t in range(4):
                            pending_pieces.append(lambda t=t: piece(t))

                def stage_a2():
                    npart = st_["npart"]
                    sck = "SC%d" % slot
                    P.add("act", lambda e: e.activation(out=SC[0:npart, slot, :], in_=SC[0:npart, slot, :], func=AF.Exp, scale=0.125), reads=[sck], writes=[sck])
                    if slot == 2:
                        P.add("dve", lambda e: e.tensor_tensor(out=PS[:, 2, :], in0=SC[:, 2, :], in1=M1[:], op=ALU.mult), reads=[sck, "M1"], writes=[psk])
                    elif slot == 3:
                        P.add("dve", lambda e: e.tensor_tensor(out=PS[0:4, 3, :], in0=SC[0:4, 3, :], in1=WN[:], op=ALU.mult), reads=[sck, "WN"], writes=[psk])
                    else:
                        P.add("dve", lambda e: e.tensor_copy(out=PS[0:npart, slot, :], in_=SC[0:npart, slot, :]), reads=[sck], writes=[psk])

                def stage_b():
                    for t in range(4):
                        for pr in range(4):
                            oc = (pr * 4 + t) * 2
                            c = t * 8 + 2 * pr
                            if slot < 2:
                                mm(OD[:, oc:oc + 2], st_["vs"][:, t, pr * 128:(pr + 1) * 128], PS[:, slot, c:c + 2], [st_["vk"], psk])
                            elif slot == 2:
                                mm(OD[:, oc:oc + 2], st_["vs"][:, 0, pr * 128:(pr + 1) * 128], PS[:, 2, c:c + 2], [st_["vk"], psk])
                            else:
                                mm(OD[:, oc:oc + 2], VN[:, pr * 128:(pr + 1) * 128], PS[0:4, 3, c:c + 2], ["VN", psk])
                    if slot < 3:
                        mm(OD[:, 32:64], ones[:, :], PS[:, slot, :], ["ones", psk])
                    else:
                        mm(OD[:, 32:64], ones[0:4, :], PS[0:4, 3, :], ["ones", psk])
                        epilogue()
                return stage_a0, stage_a1, stage_a2, stage_b

            def epilogue():
                num = OD[:, 0:32].rearrange("p (q t two) -> p q t two", q=4, t=4)
                den = OD[:, 32:64].rearrange("p (t q two) -> p q t two", q=4, two=2)
                osel = OSEL[:].rearrange("p a (q t) -> p a q t", q=4)

                def sel(hb, lo, hi):
                    P.add("dve", lambda e: e.tensor_copy(out=osel[lo:hi, 0], in_=num[lo:hi, :, :, hb]), reads=[okey], writes=["OSEL"])
                    P.add("dve", lambda e: e.tensor_copy(out=osel[lo:hi, 1], in_=den[lo:hi, :, :, hb]), reads=[okey], writes=["OSEL"])
                sel(0, 0, 64)
                sel(1, 64, 128)
                P.add("dve", lambda e: e.reciprocal(out=OSEL[:, 1, :], in_=OSEL[:, 1, :]), reads=["OSEL"], writes=["OSEL"])
                P.add("dve", lambda e: e.tensor_tensor(out=OSEL[:, 0, :], in0=OSEL[:, 0, :], in1=OSEL[:, 1, :], op=ALU.mult), reads=["OSEL"], writes=["OSEL"])
                P.add("dve", lambda e: e.tensor_tensor(out=SMA[:, :, 4 * b:4 * b + 4], in0=osel[:, 0], in1=SGA[:, :, 4 * b:4 * b + 4], op=ALU.mult),
                      reads=["OSEL", "SGA"], writes=["SMA"])

            return [slot_fns(slot) for slot in range(4)]

        if do_sample:
            P.add("sp", lambda e: e.dma_start(out=M1[:], in_=m1_d), writes=["M1"], dma=True)
            P.add("sp", lambda e: e.dma_start(out=WN[:], in_=wnew_d), writes=["WN"], dma=True)
            P.add("dve", lambda e: e.memset(SC[:], 0.0), writes=["SC0", "SC1", "SC2", "SC3"])
        if do_sample:
            for b in range(NB):
                sample_tasks.extend(sample_b_tasks(b))
        for pr in range(4):
            att_pair(pr)
        while sample_tasks or any(p_ is not None for p_ in pipe):
            flush_pieces(len(pending_pieces))
            pop_sample_task()
        flush_pieces(len(pending_pieces))
        P.add("pool", lambda e: e.dma_start(out=WOUT, in_=w_out.rearrange("(c p) n -> p c n", p=128)), writes=["WOUT", "SVS2", "SVS3"], dma=True)

        XR3 = [arena[:, 1024 * i: 1024 * (i + 1)] for i in range(3)]
        ZT3 = [arena[:, 3072 + 1024 * i: 3072 + 1024 * (i + 1)] for i in range(3)]
        O3KEYS = ["XR%d" % i for i in range(3)] + ["ZT%d" % i for i in range(3)]
        P.add("dve", lambda e: e.memset(STAT, 0.0), writes=SKEYS + OKEYS + O3KEYS + ["ACC", "QZ", "PT0", "PT1", "VG0", "VG1", "RD"])
        P.add("sp", lambda e: e.dma_start(out=LNG, in_=lng_d), writes=["LNG"], dma=True)
        P.add("sp", lambda e: e.dma_start(out=LNB, in_=lnb_d), writes=["LNB"], dma=True)

        def out_block(i, nrow, xrows, mixT, ydst, mkeys):
            xr = XR3[i % 3][0:nrow, :]
            z = ZT3[i % 3][0:nrow, :]
            xk, zk, sk = "XR%d" % (i % 3), "ZT%d" % (i % 3), "STAT%d" % (i % 2)
            so = 16 * (i % 2)
            st6 = STAT[0:nrow, so:so + 12].rearrange("p (c s) -> p c s", c=2)
            mv = STAT[0:nrow, so + 12:so + 14]
            rstd = STAT[0:nrow, so + 14:so + 15]

            def stage1():
                def half_fn(half):
                    bank = 2 + 2 * (i % 3) + half

                    def one(kc):
                        P.add("pe", lambda e: e.matmul(banks[bank][0:nrow, :], lhsT=mixT(kc), rhs=WOUT[:, kc, half * 512:(half + 1) * 512],
                                                       start=(kc == 0), stop=(kc == 7)), reads=mkeys + ["WOUT"], writes=[bkey(bank)])
                    for kc in range(8):
                        one(kc)
                    P.add("dve", lambda e: e.scalar_tensor_tensor(out=z[:, half * 512:(half + 1) * 512], in0=xr[:, half * 512:(half + 1) * 512],
                                                                  scalar=ALPHA, in1=banks[bank][0:nrow, :], op0=ALU.mult, op1=ALU.add),
                          reads=[bkey(bank), xk], writes=[zk])
                    P.add("dve", lambda e: e.bn_stats(out=st6[:, half, :], in_=z[:, half * 512:(half + 1) * 512]), reads=[zk], writes=[sk])
                half_fn(0)
                half_fn(1)
                P.add("dve", lambda e: e.bn_aggr(out=mv, in_=st6), reads=[sk], writes=[sk])
                P.add("dve", lambda e: e.tensor_scalar(out=rstd, in0=mv[:, 1:2], scalar1=LN_EPS, scalar2=None, op0=ALU.add), reads=[sk], writes=[sk])
                P.add("act", lambda e: e.activation(out=rstd, in_=rstd, func=AF.Ln), reads=[sk], writes=[sk])
                P.add("act", lambda e: e.activation(out=rstd, in_=rstd, func=AF.Exp, scale=-0.5), reads=[sk], writes=[sk])

            def xload():
                P.add("sp", lambda e: e.dma_start(out=xr, in_=xrows), writes=[xk], dma=True)

            def stage2():
                P.add("dve", lambda e: e.tensor_scalar(out=z, in0=z, scalar1=mv[:, 0:1], scalar2=rstd, op0=ALU.subtract, op1=ALU.mult), reads=[zk, sk], writes=[zk])
                P.add("pool", lambda e: e.tensor_tensor(out=z, in0=z, in1=LNG[0:nrow, :], op=ALU.mult), reads=[zk, "LNG"], writes=[zk])

            def stage3():
                P.add("dve", lambda e: e.tensor_tensor(out=z, in0=z, in1=LNB[0:nrow, :], op=ALU.add), reads=[zk, "LNB"], writes=[zk])
                P.add("pool", lambda e: e.dma_start(out=ydst, in_=z), reads=[zk], dma=True)
            return stage1, stage2, stage3, xload

        def prompt_block(tb):
            def mixT(kc):
                src = GA if kc < 4 else ML
                return src[:, kc % 4, tb * 128:(tb + 1) * 128]
            return out_block(tb, 128, xo[tb * 128:(tb + 1) * 128, :], mixT, y_d[tb * 128:(tb + 1) * 128, :], ["GA", "ML"])

        blocks = [prompt_block(tb) for tb in range(16)]
        if do_sample:
            def mixTs(kc):
                src = SMA if kc < 4 else SML
                return src[:, kc % 4, :]
            blocks.append(out_block(16, NS, xs_d, mixTs, ys_d, ["SMA", "SML"]))
        nblk = len(blocks)
        blocks[0][3]()
        for j in range(nblk + 2):
            if j + 1 < nblk:
                blocks[j + 1][3]()
            if j < nblk:
                blocks[j][0]()
            if 0 <= j - 1 < nblk:
                blocks[j - 1][1]()
            if 0 <= j - 2 < nblk:
                blocks[j - 2][2]()

        P.emit(nc)
    return nc


def _band_mask():
    s = np.arange(128)[:, None]
    u = np.arange(256)[None, :]
    ok = (u - s >= 0) & (u - s <= 128)
    m = np.where(ok, 0.0, NEG).astype(np.float32)
    return np.ascontiguousarray(np.broadcast_to(m[:, None, :], (128, 2, 256)))


def _fm(v):
    return np.ascontiguousarray(np.asarray(v, np.float32).reshape(4, 128).T)


_NC_CACHE = {}


def kernel(x_prompt, x_sample, cache_k, cache_v, state_conv, state_h, w_in, conv_w, conv_b,
           w_ra, b_ra, w_ri, b_ri, lru_lambda, w_out, ln_g, ln_b):
    f32 = np.float32
    x_prompt = np.asarray(x_prompt, f32)
    x_sample = np.asarray(x_sample, f32)
    cache_k = np.asarray(cache_k, f32)
    cache_v = np.asarray(cache_v, f32)
    state_conv = np.asarray(state_conv, f32)
    state_h = np.asarray(state_h, f32)
    w_in0 = np.ascontiguousarray(np.asarray(w_in, f32)[0])
    w_out0 = np.ascontiguousarray(np.asarray(w_out, f32)[0])
    conv_w = np.asarray(conv_w, f32)[0]
    w_ra = np.asarray(w_ra, f32)[0]
    w_ri = np.asarray(w_ri, f32)[0]

    if "nc" not in _NC_CACHE:
        _NC_CACHE["nc"] = build_program(True)
    nc = _NC_CACHE["nc"]

    prm = np.zeros((128, 32), f32)
    for fc in range(4):
        for k in range(4):
            prm[:, fc * 4 + k] = conv_w[k, fc * 128:(fc + 1) * 128]
    prm[:, 16:20] = _fm(np.asarray(conv_b, f32)[0])
    prm[:, 20:24] = -_fm(np.asarray(b_ra, f32)[0])
    prm[:, 24:28] = -_fm(np.asarray(b_ri, f32)[0])
    prm[:, 28:32] = _fm(np.asarray(lru_lambda, f32)[0])

    def bd(w):
        o = np.zeros((128, 4, 128), f32)
        for fc in range(4):
            o[0:64, fc, 0:64] = w[2 * fc]
            o[64:128, fc, 64:128] = w[2 * fc + 1]
        return o

    wra_bd, wri_bd = bd(w_ra), bd(w_ri)
    lng = np.ascontiguousarray(np.broadcast_to(np.asarray(ln_g, f32)[0][None, :], (128, 1024)))
    lnb = np.ascontiguousarray(np.broadcast_to(np.asarray(ln_b, f32)[0][None, :], (128, 1024)))
    ident = np.eye(128, dtype=f32)
    band = _band_mask()
    allneg = np.full((128, 2, 256), NEG, f32)
    m1 = (np.arange(128)[:, None, None] >= np.arange(4)[None, :, None]).astype(f32)
    m1 = np.ascontiguousarray(np.broadcast_to(m1, (128, 4, 8)).reshape(128, 32))
    wn = np.zeros((4, 4), f32)
    for tp in range(4):
        for t in range(4):
            wn[tp, t] = 3.0 if tp == t else (1.0 if tp < t else 0.0)
    wnew = np.ascontiguousarray(np.broadcast_to(wn[:, :, None], (4, 4, 8)).reshape(4, 32))

    in_maps = []
    for c in range(8):
        b, h = c // 2, c % 2
        xo = np.ascontiguousarray(x_prompt[b, h * 2048:(h + 1) * 2048])
        xc = np.ascontiguousarray(x_prompt[b, 0:2048]) if h == 1 else np.zeros((2048, 1024), f32)
        sl = slice(16 * c, 16 * (c + 1))
        sconv = state_conv[0, sl]
        sconv_fm = np.ascontiguousarray(sconv.reshape(16, 3, 4, 128).transpose(3, 2, 0, 1))
        sh_fm = np.ascontiguousarray(state_h[0, sl].reshape(16, 4, 128).transpose(2, 1, 0))
        in_maps.append({
            "xo": xo, "xc": xc, "w_in": w_in0, "w_out": w_out0, "ident": ident,
            "mask_own": band, "mask_ctx": band if h == 1 else allneg, "prm": prm,
            "wra_bd": wra_bd, "wri_bd": wri_bd, "lng": lng, "lnb": lnb,
            "flag": np.full((128, 1), float(h), f32),
            "xs": np.ascontiguousarray(x_sample[sl].reshape(64, 1024)),
            "ck": np.ascontiguousarray(cache_k[0, sl].reshape(16, 2048, 512)),
            "cv": np.ascontiguousarray(cache_v[0, sl].reshape(16, 2048, 512)),
            "sconv": sconv_fm, "sh": sh_fm, "m1": m1, "wnew": wnew,
        })
    res = run_bass_kernel_spmd(nc, in_maps, core_ids=list(range(8)))
    R = res.results

    y_prompt = np.zeros((4, 4096, 1024), f32)
    k_prompt = np.zeros((1, 4, 2048, 8, 64), f32)
    v_prompt = np.zeros((1, 4, 2048, 8, 64), f32)
    conv_prompt = np.zeros((1, 4, 3, 512), f32)
    h_prompt = np.zeros((1, 4, 512), f32)
    y_sample = np.zeros((128, 4, 1024), f32)
    k_sample = np.zeros((1, 128, 4, 8, 64), f32)
    v_sample = np.zeros((1, 128, 4, 8, 64), f32)
    conv_sample = np.zeros((1, 128, 3, 512), f32)
    h_sample = np.zeros((1, 128, 512), f32)
    for c in range(8):
        b, h = c // 2, c % 2
        r = R[c]
        y_prompt[b, h * 2048:(h + 1) * 2048] = r["y"]
        if h == 1:
            k_prompt[0, b] = r["kT"].T.reshape(2048, 8, 64)
            v_prompt[0, b] = r["vT"].T.reshape(2048, 8, 64)
            conv_prompt[0, b] = r["convT"].transpose(2, 1, 0).reshape(3, 512)
            h_prompt[0, b] = r["hT"].T.reshape(512)
        sl = slice(16 * c, 16 * (c + 1))
        y_sample[sl] = r["ys"].reshape(16, 4, 1024)
        k_sample[0, sl] = r["ks"].reshape(16, 4, 8, 64)
        v_sample[0, sl] = r["vs"].reshape(16, 4, 8, 64)
        conv_sample[0, sl] = r["convs"].transpose(2, 3, 1, 0).reshape(16, 3, 512)
        h_sample[0, sl] = r["hs"].transpose(2, 1, 0).reshape(16, 512)
    return (y_prompt, y_sample, k_prompt, v_prompt, conv_prompt, h_prompt,
            k_sample, v_sample, conv_sample, h_sample)
```

```python
import contextlib
import numpy as np
import concourse.bass as bass
import concourse.mybir as mybir
from concourse.bass_utils import run_bass_kernel_spmd

F32 = mybir.dt.float32
BF16 = mybir.dt.bfloat16
ALU = mybir.AluOpType
AF = mybir.ActivationFunctionType
AX = mybir.AxisListType

ENGS = ("pe", "act", "dve", "pool", "sp")
N_DMA_SLOTS = 40
NEG = -30000.0
ALPHA = 2.0 ** 0.25
LN_EPS = 1e-5


class Op:
    __slots__ = ("eng", "fn", "deps", "is_dma", "slot", "slot_val", "has_dep", "val")

    def __init__(self, eng, fn, is_dma):
        self.eng = eng
        self.fn = fn
        self.deps = []
        self.is_dma = is_dma
        self.slot = None
        self.slot_val = 0
        self.has_dep = False
        self.val = 0


class Prog:
    def __init__(self):
        self.ops = {e: [] for e in ENGS}
        self.res = {}
        self.slot_last = [None] * N_DMA_SLOTS
        self.slot_cnt = [0] * N_DMA_SLOTS
        self.next_slot = 0

    def add(self, eng, fn, reads=(), writes=(), dma=False):
        op = Op(eng, fn, dma)
        deps = []
        for k in reads:
            st = self.res.get(k)
            if st is not None and st[0] is not None:
                deps.append(st[0])
            if st is not None and k.startswith("bank"):
                deps.extend(r for r in st[1] if r.eng != eng)
        for k in writes:
            st = self.res.get(k)
            if st is not None:
                if st[0] is not None:
                    deps.append(st[0])
                deps.extend(st[1])
        if dma:
            s = self.next_slot
            self.next_slot = (s + 1) % N_DMA_SLOTS
            prev = self.slot_last[s]
            if prev is not None:
                deps.append(prev)
            self.slot_cnt[s] += 1
            op.slot = s
            op.slot_val = 16 * self.slot_cnt[s]
            self.slot_last[s] = op
        seen = set()
        for d in deps:
            if d is op or id(d) in seen:
                continue
            seen.add(id(d))
            if (not d.is_dma) and d.eng == eng and eng == "pe":
                continue
            op.deps.append(d)
            d.has_dep = True
        for k in reads:
            st = self.res.setdefault(k, [None, []])
            st[1].append(op)
        for k in writes:
            self.res[k] = [op, []]
        self.ops[eng].append(op)
        return op

    def emit(self, nc):
        for e in ENGS:
            v = 0
            for op in self.ops[e]:
                if not op.is_dma and op.has_dep:
                    v += 1
                    op.val = v
        with contextlib.ExitStack() as st:
            esem = {e: st.enter_context(nc.semaphore("s_" + e)) for e in ENGS}
            dsem = [st.enter_context(nc.semaphore("d%d" % i)) for i in range(N_DMA_SLOTS)]
            block = st.enter_context(nc.Block())

            def run(e, eh):
                waited = {}
                for op in self.ops[e]:
                    for d in op.deps:
                        if d.is_dma:
                            key, val, sem = ("d", d.slot), d.slot_val, dsem[d.slot]
                        else:
                            key, val, sem = ("e", d.eng), d.val, esem[d.eng]
                        if waited.get(key, 0) >= val:
                            continue
                        waited[key] = val
                        eh.wait_ge(sem, val)
                    ins = op.fn(eh)
                    if op.is_dma:
                        ins.then_inc(dsem[op.slot], 16)
                    elif op.has_dep:
                        ins.then_inc(esem[e], 1)
                if e == "sp":
                    for s in range(N_DMA_SLOTS):
                        if self.slot_cnt[s] and waited.get(("d", s), 0) < 16 * self.slot_cnt[s]:
                            eh.wait_ge(dsem[s], 16 * self.slot_cnt[s])

            block.tensor(lambda eh: run("pe", eh))
            block.scalar(lambda eh: run("act", eh))
            block.vector(lambda eh: run("dve", eh))
            block.gpsimd(lambda eh: run("pool", eh))
            block.sync(lambda eh: run("sp", eh))


NTOK = 2048
NCTX = 2048
NS = 64
NB = 16
PATTERNS = (1, 4, 16)


def build_program(do_sample=True):
    nc = bass.Bass("TRN2", target_bir_lowering=False)
    P = Prog()

    def din(name, shape):
        return nc.dram_tensor(name, list(shape), F32, kind="ExternalInput").ap()

    def dout(name, shape):
        return nc.dram_tensor(name, list(shape), F32, kind="ExternalOutput").ap()

    xo = din("xo", [NTOK, 1024])
    xc = din("xc", [NCTX, 1024])
    w_in = din("w_in", [1024, 3072])
    w_out = din("w_out", [1024, 1024])
    ident_d = din("ident", [128, 128])
    mask_own_d = din("mask_own", [128, 2, 256])
    mask_ctx_d = din("mask_ctx", [128, 2, 256])
    prm_d = din("prm", [128, 32])
    wra_d = din("wra_bd", [128, 4, 128])
    wri_d = din("wri_bd", [128, 4, 128])
    lng_d = din("lng", [128, 1024])
    lnb_d = din("lnb", [128, 1024])
    flag_d = din("flag", [128, 1])
    xs_d = din("xs", [NS, 1024])
    ck_d = din("ck", [NB, 2048, 512])
    cv_d = din("cv", [NB, 2048, 512])
    sconv_d = din("sconv", [128, 4, NB, 3])
    sh_d = din("sh", [128, 4, NB])
    m1_d = din("m1", [128, 32])
    wnew_d = din("wnew", [4, 32])

    y_d = dout("y", [NTOK, 1024])
    kT_d = dout("kT", [512, NTOK])
    vT_d = dout("vT", [512, NTOK])
    convT_d = dout("convT", [128, 4, 3])
    hT_d = dout("hT", [128, 4])
    ys_d = dout("ys", [NS, 1024])
    qs_d = dout("qs", [NS, 512])
    ks_d = dout("ks", [NS, 512])
    vs_d = dout("vs", [NS, 512])
    convs_d = dout("convs", [128, 4, NB, 3])
    hs_d = dout("hs", [128, 4, NB])

    with contextlib.ExitStack() as st:
        def sb(name, shape, dt):
            return st.enter_context(nc.sbuf_tensor("sb_" + name, list(shape), dt))

        def psb(name, shape, dt):
            return st.enter_context(nc.psum_tensor("ps_" + name, list(shape), dt))

        KT = sb("KT", [128, 4, 4096], BF16)
        VT = sb("VT", [128, 4, 4096], BF16)
        QT = sb("QT", [128, 4, NTOK], BF16)
        GA = sb("GA", [128, 4, NTOK], BF16)
        ML = sb("ML", [128, 4, NTOK], BF16)
        ident = sb("identb", [128, 128], BF16)
        ones = sb("ones", [128, 128], BF16)
        mask_own = sb("mask_own", [128, 2, 256], BF16)
        mask_ctx = sb("mask_ctx", [128, 2, 256], BF16)
        prm = sb("prm", [128, 32], F32)
        cneg = sb("cneg", [128, 8], F32)
        wra = sb("wra", [128, 4, 128], BF16)
        wri = sb("wri", [128, 4, 128], BF16)
        flag = sb("flag", [128, 1], F32)
        hstate = sb("hstate", [128, 4], F32)
        zcol = sb("zcol", [128, 1], F32)
        XL = sb("XL", [128, 3 + 512], F32)
        HALO = sb("HALO", [128, 4, 3], F32)
        SIGG = sb("SIGG", [128, 512], F32)
        arena = sb("arena", [128, 20224], F32)

        WIN = arena[:, 0:12288].bitcast(BF16).rearrange("p (c n) -> p c n", c=8)
        XTC = arena[:, 12288:14336].bitcast(BF16).rearrange("p (c n) -> p c n", c=8)
        XSTs = [arena[:, 14336:14848].bitcast(BF16), arena[:, 19712:20224].bitcast(BF16)]
        LTs = [[arena[:, 14848 + 1536 * s_ + 256 * k: 14848 + 1536 * s_ + 256 * (k + 1)] for k in range(6)] for s_ in range(2)]
        LT2s = [arena[:, 14848 + 1536 * s_ + 256: 14848 + 1536 * s_ + 768] for s_ in range(2)]
        SIGT = arena[:, 17920:18432]
        STG = [arena[:, 18432:18944], arena[:, 18944:19456]]
        UBs = [arena[:, 19456:19584].bitcast(BF16), arena[:, 19584:19712].bitcast(BF16)]
        ACC = arena[:, 0:4096].rearrange("p (a n) -> p a n", a=2)
        QZ = arena[:, 4096:6144].bitcast(BF16).rearrange("p (a n) -> p a n", a=2)
        PTt = [arena[:, 6144 + 256 * i: 6144 + 256 * (i + 1)].bitcast(BF16).rearrange("p (a n) -> p a n", a=2)
               for i in range(2)]
        VG = [arena[:, 6656 + 64 * i: 6656 + 64 * (i + 1)].bitcast(BF16) for i in range(2)]
        RD = arena[:, 7168:9216]
        WOUT = arena[:, 9216:13312].bitcast(BF16).rearrange("p (c n) -> p c n", c=8)
        SKS = [arena[:, 13312 + 1024 * i: 13312 + 1024 * (i + 1)].bitcast(BF16).rearrange("p (t c) -> p t c", t=4) for i in range(2)]
        SVS = [arena[:, 15360 + 1024 * i: 15360 + 1024 * (i + 1)].bitcast(BF16).rearrange("p (t c) -> p t c", t=4) for i in range(2)] + \
              [arena[:, 9216 + 1024 * i: 9216 + 1024 * (i + 1)].bitcast(BF16).rearrange("p (t c) -> p t c", t=4) for i in range(2)]
        SQB = arena[:, 17408:18432].bitcast(BF16).rearrange("p (t c) -> p t c", t=4)
        SPR = arena[:, 18432:19456].bitcast(BF16).rearrange("p (t c) -> p t c", t=4)
        XR = [arena[:, 13312 + 1024 * i: 13312 + 1024 * (i + 1)] for i in range(2)]
        ZT = [arena[:, 15360 + 1024 * i: 15360 + 1024 * (i + 1)] for i in range(2)]
        LNG = arena[:, 17408:18432]
        LNB = arena[:, 18432:19456]
        STAT = arena[:, 19456:19520]
        PKEYS = ["XTC", "XST0", "XST1", "UB0", "UB1", "SIGT", "STG0", "STG1"] + ["L%d_%d" % (a_, k_) for a_ in range(2) for k_ in range(6)]
        SKEYS = ["SKS0", "SKS1", "SVS0", "SVS1", "SQB", "SPR"]
        OKEYS = ["XR0", "XR1", "ZT0", "ZT1", "LNG", "LNB", "STAT0", "STAT1"]

        banks = [psb("bank%d" % i, [128, 512], F32) for i in range(8)]
        banks_bf = [bk[:, :].bitcast(BF16) for bk in banks]

        def bkey(i):
            return "bank%d" % i

        w_in_v = w_in.rearrange("(c p) n -> p c n", p=128)

        def load_win(g):
            P.add("pool", lambda e: e.dma_start(out=WIN[:, :, g * 512:(g + 1) * 512], in_=w_in_v[:, :, g * 512:(g + 1) * 512]), writes=["WIN%d" % g], dma=True)
        P.add("pool", lambda e: e.dma_start(out=ident[:], in_=ident_d), writes=["ident"], dma=True)
        P.add("pool", lambda e: e.dma_start(out=mask_own[:], in_=mask_own_d), writes=["mask_own"], dma=True)
        P.add("pool", lambda e: e.dma_start(out=mask_ctx[:], in_=mask_ctx_d), writes=["mask_ctx"], dma=True)
        P.add("pool", lambda e: e.dma_start(out=wra[:], in_=wra_d), writes=["wra"], dma=True)
        P.add("pool", lambda e: e.dma_start(out=wri[:], in_=wri_d), writes=["wri"], dma=True)
        P.add("sp", lambda e: e.dma_start(out=prm[:], in_=prm_d), writes=["prm"], dma=True)
        P.add("sp", lambda e: e.dma_start(out=flag[:], in_=flag_d), writes=["flag"], dma=True)
        P.add("dve", lambda e: e.memset(ones[:], 1.0), writes=["ones"])
        P.add("dve", lambda e: e.memset(zcol[:], 0.0), writes=["zcol"])
        P.add("dve", lambda e: e.memset(hstate[:], 0.0), writes=["hs0", "hs1", "hs2", "hs3"])
        P.add("dve", lambda e: e.memset(HALO[:], 0.0), writes=["HALO0", "HALO1", "HALO2", "HALO3"])
        P.add("act", lambda e: e.activation(out=cneg[:, 0:4], in_=prm[:, 28:32], func=AF.Exp, scale=-1.0), reads=["prm"], writes=["cneg"])
        P.add("act", lambda e: e.activation(out=cneg[:, 0:4], in_=cneg[:, 0:4], func=AF.Ln, bias=1.0), reads=["cneg"], writes=["cneg"])
        P.add("dve", lambda e: e.tensor_scalar(out=cneg[:, 4:8], in0=cneg[:, 0:4], scalar1=-16.0, scalar2=None, op0=ALU.mult), reads=["cneg"], writes=["cneg"])
        P.add("dve", lambda e: e.tensor_scalar(out=cneg[:, 0:4], in0=cneg[:, 0:4], scalar1=-8.0, scalar2=None, op0=ALU.mult), reads=["cneg"], writes=["cneg"])

        rr = {"ev": 0, "stg": 0, "xst": 0}

        def next_bank():
            b_ = 4 + (rr["ev"] % 4)
            rr["ev"] += 1
            return b_

        def next_stg():
            i = rr["stg"] % 2
            rr["stg"] += 1
            return STG[i], "STG%d" % i

        def lk(s_, k):
            return "L%d_%d" % (s_, k)

        GBANKS = [2, 3]

        def gates_s1a(fc, n, s_):
            u = LTs[s_][0][:, 0:n]
            ub = UBs[s_][:, 0:n]
            gb = GBANKS[s_]
            gr, gi = banks[gb][:, 0:256], banks[gb][:, 256:512]
            P.add("dve", lambda e: e.tensor_copy(out=ub, in_=u), reads=[lk(s_, 0)], writes=["UB%d" % s_])
            P.add("pe", lambda e: e.matmul(gr[:, 0:n], lhsT=wra[:, fc, :], rhs=ub, start=True, stop=True, skip_group_check=True), reads=["wra", "UB%d" % s_], writes=[bkey(gb)])
            P.add("pe", lambda e: e.matmul(gi[:, 0:n], lhsT=wri[:, fc, :], rhs=ub, start=False, stop=True, skip_group_check=True), reads=["wri", "UB%d" % s_], writes=[bkey(gb)])

        def gates_s1b(fc, n, s_):
            gb = GBANKS[s_]
            gr, gi = banks[gb][:, 0:256], banks[gb][:, 256:512]
            er, ei, a = LTs[s_][1][:, 0:n], LTs[s_][2][:, 0:n], LTs[s_][4][:, 0:n]
            P.add("act", lambda e: e.activation(out=er, in_=gr[:, 0:n], func=AF.Exp, scale=-1.0, bias=prm[:, 20 + fc:21 + fc]), reads=[bkey(gb), "prm"], writes=[lk(s_, 1)])
            P.add("act", lambda e: e.activation(out=ei, in_=gi[:, 0:n], func=AF.Exp, scale=-1.0, bias=prm[:, 24 + fc:25 + fc]), reads=[bkey(gb), "prm"], writes=[lk(s_, 2)])
            if n == 256:
                both = LT2s[s_]
                P.add("act", lambda e: e.activation(out=both, in_=both, func=AF.Ln, bias=1.0), reads=[lk(s_, 1), lk(s_, 2)], writes=[lk(s_, 1), lk(s_, 2)])
                P.add("act", lambda e: e.activation(out=both, in_=both, func=AF.Exp, scale=-1.0), reads=[lk(s_, 1), lk(s_, 2)], writes=[lk(s_, 1), lk(s_, 2)])
            else:
                P.add("act", lambda e: e.activation(out=er, in_=er, func=AF.Ln, bias=1.0), reads=[lk(s_, 1)], writes=[lk(s_, 1)])
                P.add("act", lambda e: e.activation(out=ei, in_=ei, func=AF.Ln, bias=1.0), reads=[lk(s_, 2)], writes=[lk(s_, 2)])
                P.add("act", lambda e: e.activation(out=er, in_=er, func=AF.Exp, scale=-1.0), reads=[lk(s_, 1)], writes=[lk(s_, 1)])
                P.add("act", lambda e: e.activation(out=ei, in_=ei, func=AF.Exp, scale=-1.0), reads=[lk(s_, 2)], writes=[lk(s_, 2)])
            P.add("act", lambda e: e.activation(out=a, in_=er, func=AF.Exp, scale=cneg[:, fc:fc + 1]), reads=[lk(s_, 1), "cneg"], writes=[lk(s_, 4)])

        def gates_s2(fc, n, s_):
            u = LTs[s_][0][:, 0:n]
            ei, a, t1, bx = LTs[s_][2][:, 0:n], LTs[s_][4][:, 0:n], LTs[s_][3][:, 0:n], LTs[s_][5][:, 0:n]
            P.add("dve", lambda e: e.scalar_tensor_tensor(out=t1, in0=a, scalar=-1.0, in1=a, op0=ALU.mult, op1=ALU.mult), reads=[lk(s_, 4)], writes=[lk(s_, 3)])
            P.add("dve", lambda e: e.tensor_tensor(out=bx, in0=ei, in1=u, op=ALU.mult), reads=[lk(s_, 2), lk(s_, 0)], writes=[lk(s_, 5)])
            P.add("act", lambda e: e.activation(out=t1, in_=t1, func=AF.Ln, bias=1.0), reads=[lk(s_, 3)], writes=[lk(s_, 3)])
            P.add("act", lambda e: e.activation(out=t1, in_=t1, func=AF.Exp, scale=0.5), reads=[lk(s_, 3)], writes=[lk(s_, 3)])
            P.add("dve", lambda e: e.tensor_tensor(out=bx, in0=bx, in1=t1, op=ALU.mult), reads=[lk(s_, 5), lk(s_, 3)], writes=[lk(s_, 5)])
            return a, bx

        def sigmoid_times(src_bank, n, out_ap, out_key):
            eg = SIGT[:, 0:n]
            gp = banks[src_bank][:, 0:n]
            P.add("act", lambda e: e.activation(out=eg, in_=gp, func=AF.Exp, scale=-1.0), reads=[bkey(src_bank)], writes=["SIGT"])
            P.add("act", lambda e: e.activation(out=eg, in_=eg, func=AF.Ln, bias=1.0), reads=["SIGT"], writes=["SIGT"])
            P.add("act", lambda e: e.activation(out=eg, in_=eg, func=AF.Exp, scale=-1.0), reads=["SIGT"], writes=["SIGT"])
            P.add("dve", lambda e: e.tensor_tensor(out=out_ap, in0=eg, in1=gp, op=ALU.mult), reads=["SIGT", bkey(src_bank)], writes=[out_key])

        def proj(col0, n, xt_ap, bank, xkey="XTC"):
            def one(kc):
                P.add("pe", lambda e: e.matmul(banks[bank][:, 0:n], lhsT=WIN[:, kc, col0:col0 + 128], rhs=xt_ap[:, kc, :],
                                               start=(kc == 0), stop=(kc == 7)), reads=["WIN%d" % (col0 // 512), xkey], writes=[bkey(bank)])
            for kc in range(8):
                one(kc)

        def load_xblock(src_rows, nrow, dst_ap, dst_key):
            xi = rr["xst"] % 2
            rr["xst"] += 1
            xst = XSTs[xi]
            xk = "XST%d" % xi
            P.add("pool", lambda e: e.dma_start(out=xst[0:nrow, :], in_=src_rows), writes=[xk], dma=True)
            pt = banks_bf[xi]

            def one(kc):
                P.add("pe", lambda e: e.transpose(pt[:, kc * nrow:(kc + 1) * nrow], xst[0:nrow, kc * 128:(kc + 1) * 128], ident[0:nrow, 0:nrow]),
                      reads=[xk, "ident"], writes=[bkey(xi)])
            for kc in range(8):
                one(kc)
            srcv = pt[:, 0:8 * nrow].rearrange("p (c n) -> p c n", c=8)
            P.add("dve", lambda e: e.tensor_copy(out=dst_ap, in_=srcv), reads=[bkey(xi)], writes=[dst_key])

        def kv_proj(tc, which, pr):
            own = tc >= 4
            T0 = tc * 512
            o0 = T0 - 2048
            dstT = KT if which == 0 else VT
            dkey = "KT" if which == 0 else "VT"
            dout_ap = kT_d if which == 0 else vT_d
            bank = next_bank()
            proj(512 + which * 512 + pr * 128, 512, XTC, bank)
            if own:
                stg, sk = next_stg()
                P.add("act", lambda e: e.activation(out=stg, in_=banks[bank][:, :], func=AF.Copy), reads=[bkey(bank)], writes=[sk])
                P.add("sp", lambda e: e.dma_start(out=dout_ap[pr * 128:(pr + 1) * 128, o0:o0 + 512], in_=stg), reads=[sk], dma=True)
                P.add("dve", lambda e: e.tensor_copy(out=dstT[:, pr, T0:T0 + 512], in_=banks[bank][:, :]), reads=[bkey(bank)], writes=[dkey])
            else:
                P.add("act", lambda e: e.activation(out=dstT[:, pr, T0:T0 + 512], in_=banks[bank][:, :], func=AF.Copy), reads=[bkey(bank)], writes=[dkey])

        def q_proj(tc, pr):
            o0 = tc * 512 - 2048
            bank = next_bank()
            proj(pr * 128, 512, XTC, bank)
            P.add("act", lambda e: e.activation(out=QT[:, pr, o0:o0 + 512], in_=banks[bank][:, :], func=AF.Copy), reads=[bkey(bank)], writes=["QT"])

        def ga_proj(tc, pr):
            o0 = tc * 512 - 2048
            bank = next_bank()
            proj(1536 + pr * 128, 512, XTC, bank)
            sigmoid_times(bank, 512, GA[:, pr, o0:o0 + 512], "GA")

        item_ctr = [0]

        def lru_item(tc, fc, h, fillers):
            own = tc >= 4
            o0 = tc * 512 - 2048
            c0 = 256 * h
            s_ = h
            hk = "hs%d" % fc
            hak = "HALO%d" % fc
            u = LTs[s_][0][:, :]
            tmp = LTs[s_][5][:, :]
            cw = lambda k: prm[:, fc * 4 + k: fc * 4 + k + 1]

            def stage1a():
                if h == 0:
                    bank = next_bank()
                    proj(2048 + fc * 128, 512, XTC, bank)
                    P.add("act", lambda e: e.activation(out=XL[:, 3:515], in_=banks[bank][:, :], func=AF.Copy), reads=[bkey(bank)], writes=["XL"])
                    P.add("dve", lambda e: e.tensor_copy(out=XL[:, 0:3], in_=HALO[:, fc, :]), reads=[hak], writes=["XL"])
                P.add("dve", lambda e: e.tensor_scalar(out=u, in0=XL[:, c0:c0 + 256], scalar1=cw(0), scalar2=prm[:, 16 + fc:17 + fc],
                                                        op0=ALU.mult, op1=ALU.add), reads=["XL", "prm"], writes=[lk(s_, 0)])

                def tap(k):
                    P.add("dve", lambda e: e.scalar_tensor_tensor(out=u, in0=XL[:, c0 + k:c0 + k + 256], scalar=cw(k), in1=u, op0=ALU.mult, op1=ALU.add),
                          reads=["XL", lk(s_, 0), "prm"], writes=[lk(s_, 0)])
                for k in range(1, 4):
                    tap(k)
                if h == 1:
                    P.add("dve", lambda e: e.tensor_copy(out=HALO[:, fc, :], in_=XL[:, 512:515]), reads=["XL"], writes=[hak])
                gates_s1a(fc, 256, s_)

            def stage1b():
                gates_s1b(fc, 256, s_)
                for f in fillers:
                    f()

            def stage2():
                a, bx = gates_s2(fc, 256, s_)
                hseq = LTs[s_][1][:, :]
                if h == 0:
                    init_ap, init_key = hstate[:, fc:fc + 1], hk
                else:
                    init_ap, init_key = LTs[0][1][:, 255:256], lk(0, 1)
                P.add("dve", lambda e: e.tensor_tensor_scan(out=hseq, data0=a, data1=bx, initial=init_ap, op0=ALU.mult, op1=ALU.add),
                      reads=[lk(s_, 4), lk(s_, 5), init_key], writes=[lk(s_, 1)])
                if h == 1:
                    if tc == 3:
                        P.add("dve", lambda e: e.tensor_tensor(out=hstate[:, fc:fc + 1], in0=hseq[:, 255:256], in1=flag[:, 0:1], op=ALU.mult),
                              reads=[lk(s_, 1), "flag"], writes=[hk])
                    else:
                        P.add("dve", lambda e: e.tensor_copy(out=hstate[:, fc:fc + 1], in_=hseq[:, 255:256]), reads=[lk(s_, 1)], writes=[hk])
                if own:
                    if h == 0:
                        bank2 = next_bank()
                        proj(2560 + fc * 128, 512, XTC, bank2)
                        sigmoid_times(bank2, 512, SIGG[:, :], "SIGG")
                    P.add("dve", lambda e: e.tensor_tensor(out=ML[:, fc, o0 + c0:o0 + c0 + 256], in0=SIGG[:, c0:c0 + 256], in1=hseq, op=ALU.mult),
                          reads=["SIGG", lk(s_, 1)], writes=["ML"])
            return stage1a, stage1b, stage2

        prev2 = None
        for tc in range(8):
            srcx = xo if tc >= 4 else xc
            for tb in range(4):
                r0 = (tc % 4) * 512 + tb * 128
                load_xblock(srcx[r0:r0 + 128, :], 128, XTC[:, :, tb * 128:(tb + 1) * 128], "XTC")
                if tc == 0 and tb == 1:
                    for g in (1, 2, 4):
                        load_win(g)
            if tc == 0:
                for g in (0, 3, 5):
                    load_win(g)
            others = [(lambda which=which, pr=pr, tc=tc: kv_proj(tc, which, pr)) for which in range(2) for pr in range(4)]
            if tc >= 4:
                others += [(lambda pr=pr, tc=tc: q_proj(tc, pr)) for pr in range(4)]
                others += [(lambda pr=pr, tc=tc: ga_proj(tc, pr)) for pr in range(4)]
            per = (len(others) + 7) // 8
            idx = 0
            for fc in range(4):
                for h in range(2):
                    s1a, s1b, s2 = lru_item(tc, fc, h, others[idx * per:(idx + 1) * per])
                    idx += 1
                    s1a()
                    if prev2 is not None:
                        prev2()
                    s1b()
                    prev2 = s2
        prev2()
        P.add("sp", lambda e: e.dma_start(out=convT_d, in_=HALO[:]), reads=["HALO0", "HALO1", "HALO2", "HALO3"], dma=True)
        P.add("sp", lambda e: e.dma_start(out=hT_d, in_=hstate[:]), reads=["hs0", "hs1", "hs2", "hs3"], dma=True)

        SXT = sb("SXT", [128, 8, NS], BF16)
        SGA = sb("SGA", [128, 4, NS], BF16)
        SML = sb("SML", [128, 4, NS], BF16)
        SMA = sb("SMA", [128, 4, NS], BF16)
        SXL = sb("SXL", [128, 4, NB, 7], F32)
        SH = sb("SH", [128, 4, NB], F32)
        SHO = sb("SHO", [128, 4, NB], F32)

        def s_qkv(j):
            bank = next_bank()

            def one(kc):
                P.add("pe", lambda e: e.matmul(banks[bank][0:NS, :], lhsT=SXT[:, kc, :], rhs=WIN[:, kc, j * 512:(j + 1) * 512],
                                               start=(kc == 0), stop=(kc == 7)), reads=["WIN%d" % j, "SXT"], writes=[bkey(bank)])
            for kc in range(8):
                one(kc)
            stg_full, sk = next_stg()
            stg = stg_full[0:NS, :]
            P.add("act", lambda e: e.activation(out=stg, in_=banks[bank][0:NS, :], func=AF.Copy), reads=[bkey(bank)], writes=[sk])
            P.add("sp", lambda e: e.dma_start(out=(qs_d, ks_d, vs_d)[j], in_=stg), reads=[sk], writes=["scr%d" % j], dma=True)

        def s_ga(pr):
            bank = next_bank()
            proj(1536 + pr * 128, NS, SXT, bank, xkey="SXT")
            sigmoid_times(bank, NS, SGA[:, pr, :], "SGA")

        def s_lru(fc):
            n = NS
            bank = next_bank()
            proj(2048 + fc * 128, NS, SXT, bank, xkey="SXT")
            P.add("act", lambda e: e.activation(out=SXL[:, fc, :, 3:7], in_=banks[bank][:, 0:NS].rearrange("p (b t) -> p b t", t=4), func=AF.Copy),
                  reads=[bkey(bank)], writes=["SXL"])
            u = LTs[0][0][:, 0:n]
            u3 = u.rearrange("p (b t) -> p b t", t=4)
            cw = lambda k: prm[:, fc * 4 + k: fc * 4 + k + 1]
            P.add("dve", lambda e: e.tensor_scalar(out=u3, in0=SXL[:, fc, :, 0:4], scalar1=cw(0), scalar2=prm[:, 16 + fc:17 + fc],
                                                    op0=ALU.mult, op1=ALU.add), reads=["SXL", "prm"], writes=[lk(0, 0)])

            def tap(k):
                P.add("dve", lambda e: e.scalar_tensor_tensor(out=u3, in0=SXL[:, fc, :, k:k + 4], scalar=cw(k), in1=u3, op0=ALU.mult, op1=ALU.add),
                      reads=["SXL", lk(0, 0), "prm"], writes=[lk(0, 0)])
            for k in range(1, 4):
                tap(k)
            gates_s1a(fc, n, 0)
            gates_s1b(fc, n, 0)
            a, bx = gates_s2(fc, n, 0)
            a3 = a.rearrange("p (b t) -> p b t", t=4)
            b3 = bx.rearrange("p (b t) -> p b t", t=4)
            hh = LTs[0][1][:, 0:n]
            h3 = hh.rearrange("p (b t) -> p b t", t=4)

            def step(t):
                prev = SH[:, fc, :] if t == 0 else h3[:, :, t - 1]
                P.add("dve", lambda e: e.tensor_tensor(out=h3[:, :, t], in0=a3[:, :, t], in1=prev, op=ALU.mult), reads=[lk(0, 4), lk(0, 1), "SH"], writes=[lk(0, 1)])
                P.add("dve", lambda e: e.tensor_tensor(out=h3[:, :, t], in0=h3[:, :, t], in1=b3[:, :, t], op=ALU.add), reads=[lk(0, 5), lk(0, 1)], writes=[lk(0, 1)])
            for t in range(4):
                step(t)
            P.add("dve", lambda e: e.tensor_copy(out=SHO[:, fc, :], in_=h3[:, :, 3]), reads=[lk(0, 1)], writes=["SHO"])
            bank2 = next_bank()
            proj(2560 + fc * 128, NS, SXT, bank2, xkey="SXT")
            sigmoid_times(bank2, NS, SIGG[:, 0:NS], "SIGG")
            P.add("dve", lambda e: e.tensor_tensor(out=SML[:, fc, :], in0=SIGG[:, 0:NS], in1=hh, op=ALU.mult), reads=["SIGG", lk(0, 1)], writes=["SML"])

        if do_sample:
            load_xblock(xs_d, NS, SXT[:], "SXT")
            for j in range(3):
                s_qkv(j)
            for pr in range(4):
                s_ga(pr)
            P.add("sp", lambda e: e.dma_start(out=SXL[:, :, :, 0:3], in_=sconv_d), writes=["SXL"], dma=True)
            P.add("sp", lambda e: e.dma_start(out=SH[:], in_=sh_d), writes=["SH"], dma=True)
            for fc in range(4):
                s_lru(fc)
            P.add("sp", lambda e: e.dma_start(out=hs_d, in_=SHO[:]), reads=["SHO"], dma=True)
            P.add("sp", lambda e: e.dma_start(out=convs_d, in_=SXL[:, :, :, 4:7]), reads=["SXL"], dma=True)

        P.add("dve", lambda e: e.memset(STAT, 0.0), writes=PKEYS + SKEYS)

        P.add("dve", lambda e: e.memset(QZ, 0.0), reads=SKEYS[:1], writes=["QZ", "SVS2", "SVS3"] + ["WIN%d" % g for g in range(6)])
        unit_i = [0]

        sample_tasks = []
        task_ctr = [0]

        pipe = [None, None, None]

        def pop_sample_task(flush=False):
            nxt = sample_tasks.pop(0) if sample_tasks else None
            if pipe[2] is not None:
                pipe[2][3]()
            if pipe[1] is not None:
                pipe[1][2]()
            if pipe[0] is not None:
                pipe[0][1]()
            if nxt is not None:
                nxt[0]()
            pipe[2], pipe[1], pipe[0] = pipe[1], pipe[0], nxt

        pending_pieces = []

        def flush_pieces(nmax):
            for _ in range(min(nmax, len(pending_pieces))):
                pending_pieces.pop(0)()

        def maybe_sample_task():
            task_ctr[0] += 1
            if task_ctr[0] % 4 == 0 and (sample_tasks or any(p_ is not None for p_ in pipe)):
                flush_pieces(len(pending_pieces))
                pop_sample_task()
            else:
                flush_pieces(2)

        def att_unit(pr, pi, d, r, kb, nbq):
            is_ctx = kb == nbq - 1
            is_last = kb == 2 * nbq - 1
            u0 = 128 if is_ctx else 0
            n = 128 if (is_ctx or is_last) else 256
            ui = unit_i[0]
            unit_i[0] += 1
            sbank = 4 + ui % 2
            S = banks[sbank][:, :].rearrange("p (a n) -> p a n", a=2)
            msk = mask_ctx if is_ctx else mask_own
            kstart = r + 128 * d * kb
            keys = KT[:, pr, kstart: kstart + 127 * d + 1: d]
            qstart = r + d * (128 * kb + u0) - 2048
            qs_ap = QZ[:, :, qstart: qstart + (n - 1) * d + 1: d]
            vi = ui % 2
            vg = VG[vi]
            vbank = 6 + vi
            vps = banks_bf[vbank][:, 0:128]
            pt_t = PTt[ui % 2]
            pk = "PT%d" % (ui % 2)
            vk = "VG%d" % vi

            def stage1():
                P.add("pe", lambda e: e.transpose(vps, VT[:, pr, kstart: kstart + 127 * d + 1: d], ident[:]), reads=["VT", "ident"], writes=[bkey(vbank)])
                P.add("pe", lambda e: e.matmul(S[:, :, 0:n], lhsT=ident[:], rhs=msk[:, :, u0:u0 + n], start=True, stop=False),
                      reads=["ident", "mask_own", "mask_ctx"], writes=[bkey(sbank)])
                P.add("pe", lambda e: e.matmul(S[:, :, 0:n], lhsT=keys, rhs=qs_ap, start=False, stop=True), reads=["KT", "QZ"], writes=[bkey(sbank)])
                P.add("act", lambda e: e.activation(out=vg, in_=vps, func=AF.Copy), reads=[bkey(vbank)], writes=[vk])
                P.add("act", lambda e: e.activation(out=pt_t[:, :, 0:n], in_=S[:, :, 0:n], func=AF.Exp, scale=0.125), reads=[bkey(sbank)], writes=[pk])

            def pv(sub):
                qb = kb + (u0 // 128) + sub
                first = (qb == kb + 1)
                obank = qb % 2
                OD = banks[obank]
                c0 = sub * 128
                okey = bkey(obank)
                P.add("pe", lambda e: e.matmul(OD[0:64, 0:128], lhsT=vg[:, 0:64], rhs=pt_t[:, 0, c0:c0 + 128], start=first, stop=False, skip_group_check=True),
                      reads=[vk, pk], writes=[okey])
                P.add("pe", lambda e: e.matmul(OD[64:128, 0:128], lhsT=vg[:, 64:128], rhs=pt_t[:, 1, c0:c0 + 128], start=first, stop=False, skip_group_check=True),
                      reads=[vk, pk], writes=[okey])
                P.add("pe", lambda e: e.matmul(OD[0:64, 128:256], lhsT=ones[:, 0:64], rhs=pt_t[:, 0, c0:c0 + 128], start=False, stop=False, skip_group_check=True),
                      reads=["ones", pk], writes=[okey])
                P.add("pe", lambda e: e.matmul(OD[64:128, 128:256], lhsT=ones[:, 0:64], rhs=pt_t[:, 1, c0:c0 + 128], start=False, stop=False, skip_group_check=True),
                      reads=["ones", pk], writes=[okey])
                if not first:
                    t0 = r + d * 128 * qb - 2048
                    dst = ACC[:, :, t0: t0 + 127 * d + 1: d]
                    srcv = OD[:, 0:256].rearrange("p (a n) -> p a n", a=2)
                    if pi == 0:
                        P.add("act", lambda e: e.activation(out=dst, in_=srcv, func=AF.Copy), reads=[okey], writes=["ACC"])
                    else:
                        P.add("dve", lambda e: e.tensor_tensor(out=dst, in0=srcv, in1=dst, op=ALU.add), reads=[okey, "ACC"], writes=["ACC"])

            def stage2():
                for sub in range(n // 128):
                    pv(sub)
            return stage1, stage2

        def att_pair(pr):
            P.add("act", lambda e: e.activation(out=QZ[0:64, 0, :], in_=QT[0:64, pr, :], func=AF.Copy), reads=["QT"], writes=["QZ"])
            P.add("act", lambda e: e.activation(out=QZ[64:128, 1, :], in_=QT[64:128, pr, :], func=AF.Copy), reads=["QT"], writes=["QZ"])
            units = []
            for pi, d in enumerate(PATTERNS):
                nbq = NTOK // (128 * d)
                for r in range(d):
                    for kb in range(nbq - 1, 2 * nbq):
                        units.append((pr, pi, d, r, kb, nbq))
            prev2 = None
            for uargs in units:
                s1, s2 = att_unit(*uargs)
                s1()
                if prev2 is not None:
                    prev2()
                prev2 = s2
                maybe_sample_task()
            prev2()
            P.add("act", lambda e: e.activation(out=RD, in_=ACC[:, 1, :], func=AF.Ln), reads=["ACC"], writes=["RD"])
            P.add("act", lambda e: e.activation(out=RD, in_=RD, func=AF.Exp, scale=-1.0), reads=["RD"], writes=["RD"])
            P.add("dve", lambda e: e.tensor_tensor(out=RD, in0=RD, in1=ACC[:, 0, :], op=ALU.mult), reads=["RD", "ACC"], writes=["RD"])
            P.add("dve", lambda e: e.tensor_tensor(out=GA[:, pr, :], in0=RD, in1=GA[:, pr, :], op=ALU.mult), reads=["RD", "GA"], writes=["GA"])

        KN = sb("KN", [4, 512], BF16)
        VN = sb("VN", [4, 512], BF16)
        SC = sb("SC", [128, 4, 32], F32)
        PS = sb("PS", [128, 4, 32], BF16)
        M1 = sb("M1", [128, 32], F32)
        WN = sb("WN", [4, 32], F32)
        OSEL = sb("OSEL", [128, 2, 16], F32)
        qs_flat = qs_d.rearrange("(b t) c -> b (t c)", t=4)
        sbuf_i = [0]

        def sample_b_tasks(b):
            obank = 2 + b % 2
            OD = banks[obank]
            okey = bkey(obank)
            firstmm = [True]

            def mm(out_ap, lhsT, rhs, rd):
                f = firstmm[0]
                firstmm[0] = False
                P.add("pe", lambda e: e.matmul(out_ap, lhsT=lhsT, rhs=rhs, start=f, stop=False, skip_group_check=True), reads=rd, writes=[okey])

            def slot_fns(slot):
                st_ = {}
                psk = "PS%d" % slot

                def stage_a0():
                    if slot == 0:
                        P.add("pool", lambda e: e.dma_start(out=SQB.rearrange("p t c -> p (t c)"), in_=qs_flat[b:b + 1, :].broadcast_to([128, 2048])),
                              reads=["scr0"], writes=["SQB"], dma=True)
                    if slot == 3:
                        P.add("pool", lambda e: e.dma_start(out=KN[:], in_=ks_d[4 * b:4 * b + 4, :]), reads=["scr1"], writes=["KN"], dma=True)
                        P.add("pool", lambda e: e.dma_start(out=VN[:], in_=vs_d[4 * b:4 * b + 4, :]), reads=["scr2"], writes=["VN"], dma=True)
                    if slot < 3:
                        ki = sbuf_i[0] % 2
                        vi_ = sbuf_i[0] % 4
                        sbuf_i[0] += 1
                        ks, vs = SKS[ki], SVS[vi_]
                        kk, vk = "SKS%d" % ki, "SVS%d" % vi_
                        if slot == 0:
                            ksrc = ck_d[b, :, :].rearrange("(i t) c -> i t c", t=16)[:, 0:4, :]
                            vsrc = cv_d[b, :, :].rearrange("(i t) c -> i t c", t=16)[:, 0:4, :]
                            kdst, vdst = ks, vs
                        elif slot == 1:
                            ksrc = ck_d[b, 1536:2048, :].rearrange("(i t) c -> i t c", t=4)
                            vsrc = cv_d[b, 1536:2048, :].rearrange("(i t) c -> i t c", t=4)
                            kdst, vdst = ks, vs
                        else:
                            ksrc = ck_d[b, 1920:2048, :]
                            vsrc = cv_d[b, 1920:2048, :]
                            kdst, vdst = ks[:, 0, :], vs[:, 0, :]
                        P.add("pool", lambda e: e.dma_start(out=kdst, in_=ksrc), writes=[kk], dma=True)
                        P.add("pool", lambda e: e.dma_start(out=vdst, in_=vsrc), writes=[vk], dma=True)
                        st_["kin"] = ks if slot < 2 else ks[:, 0:1, :].broadcast_to([128, 4, 512])
                        st_["kk"] = kk
                        st_["npart"] = 128
                        st_["vs"], st_["vk"] = vs, vk
                    else:
                        st_["kin"] = KN[:].unsqueeze(1).broadcast_to([4, 4, 512])
                        st_["kk"] = "KN"
                        st_["npart"] = 4

                def stage_a1():
                    kin, kk, npart = st_["kin"], st_["kk"], st_["npart"]
                    P.add("dve", lambda e: e.tensor_tensor(out=SPR[0:npart], in0=kin, in1=SQB[0:npart], op=ALU.mult), reads=[kk, "SQB"], writes=["SPR"])

                    def piece(t):
                        P.add("dve", lambda e: e.tensor_reduce(out=SC[0:npart, slot, t * 8:(t + 1) * 8],
                                                               in_=SPR[0:npart, t, :].rearrange("p (h d) -> p h d", d=64),
                                                               axis=AX.X, op=ALU.add), reads=["SPR"], writes=["SC%d" % slot])
                    if npart == 4:
                        for t in range(4):
                            piece(t)
                    else:
                        for t in range(4):
                            pending_pieces.append(lambda t=t: piece(t))

                def stage_a2():
                    npart = st_["npart"]
                    sck = "SC%d" % slot
                    P.add("act", lambda e: e.activation(out=SC[0:npart, slot, :], in_=SC[0:npart, slot, :], func=AF.Exp, scale=0.125), reads=[sck], writes=[sck])
                    if slot == 2:
                        P.add("dve", lambda e: e.tensor_tensor(out=PS[:, 2, :], in0=SC[:, 2, :], in1=M1[:], op=ALU.mult), reads=[sck, "M1"], writes=[psk])
                    elif slot == 3:
                        P.add("dve", lambda e: e.tensor_tensor(out=PS[0:4, 3, :], in0=SC[0:4, 3, :], in1=WN[:], op=ALU.mult), reads=[sck, "WN"], writes=[psk])
                    else:
                        P.add("dve", lambda e: e.tensor_copy(out=PS[0:npart, slot, :], in_=SC[0:npart, slot, :]), reads=[sck], writes=[psk])

                def stage_b():
                    for t in range(4):
                        for pr in range(4):
                            oc = (pr * 4 + t) * 2
                            c = t * 8 + 2 * pr
                            if slot < 2:
                                mm(OD[:, oc:oc + 2], st_["vs"][:, t, pr * 128:(pr + 1) * 128], PS[:, slot, c:c + 2], [st_["vk"], psk])
                            elif slot == 2:
                                mm(OD[:, oc:oc + 2], st_["vs"][:, 0, pr * 128:(pr + 1) * 128], PS[:, 2, c:c + 2], [st_["vk"], psk])
                            else:
                                mm(OD[:, oc:oc + 2], VN[:, pr * 128:(pr + 1) * 128], PS[0:4, 3, c:c + 2], ["VN", psk])
                    if slot < 3:
                        mm(OD[:, 32:64], ones[:, :], PS[:, slot, :], ["ones", psk])
                    else:
                        mm(OD[:, 32:64], ones[0:4, :], PS[0:4, 3, :], ["ones", psk])
                        epilogue()
                return stage_a0, stage_a1, stage_a2, stage_b

            def epilogue():
                num = OD[:, 0:32].rearrange("p (q t two) -> p q t two", q=4, t=4)
                den = OD[:, 32:64].rearrange("p (t q two) -> p q t two", q=4, two=2)
                osel = OSEL[:].rearrange("p a (q t) -> p a q t", q=4)

                def sel(hb, lo, hi):
                    P.add("dve", lambda e: e.tensor_copy(out=osel[lo:hi, 0], in_=num[lo:hi, :, :, hb]), reads=[okey], writes=["OSEL"])
                    P.add("dve", lambda e: e.tensor_copy(out=osel[lo:hi, 1], in_=den[lo:hi, :, :, hb]), reads=[okey], writes=["OSEL"])
                sel(0, 0, 64)
                sel(1, 64, 128)
                P.add("dve", lambda e: e.reciprocal(out=OSEL[:, 1, :], in_=OSEL[:, 1, :]), reads=["OSEL"], writes=["OSEL"])
                P.add("dve", lambda e: e.tensor_tensor(out=OSEL[:, 0, :], in0=OSEL[:, 0, :], in1=OSEL[:, 1, :], op=ALU.mult), reads=["OSEL"], writes=["OSEL"])
                P.add("dve", lambda e: e.tensor_tensor(out=SMA[:, :, 4 * b:4 * b + 4], in0=osel[:, 0], in1=SGA[:, :, 4 * b:4 * b + 4], op=ALU.mult),
                      reads=["OSEL", "SGA"], writes=["SMA"])

            return [slot_fns(slot) for slot in range(4)]

        if do_sample:
            P.add("sp", lambda e: e.dma_start(out=M1[:], in_=m1_d), writes=["M1"], dma=True)
            P.add("sp", lambda e: e.dma_start(out=WN[:], in_=wnew_d), writes=["WN"], dma=True)
            P.add("dve", lambda e: e.memset(SC[:], 0.0), writes=["SC0", "SC1", "SC2", "SC3"])
        if do_sample:
            for b in range(NB):
                sample_tasks.extend(sample_b_tasks(b))
        for pr in range(4):
            att_pair(pr)
        while sample_tasks or any(p_ is not None for p_ in pipe):
            flush_pieces(len(pending_pieces))
            pop_sample_task()
        flush_pieces(len(pending_pieces))
        P.add("pool", lambda e: e.dma_start(out=WOUT, in_=w_out.rearrange("(c p) n -> p c n", p=128)), writes=["WOUT", "SVS2", "SVS3"], dma=True)

        XR3 = [arena[:, 1024 * i: 1024 * (i + 1)] for i in range(3)]
        ZT3 = [arena[:, 3072 + 1024 * i: 3072 + 1024 * (i + 1)] for i in range(3)]
        O3KEYS = ["XR%d" % i for i in range(3)] + ["ZT%d" % i for i in range(3)]
        P.add("dve", lambda e: e.memset(STAT, 0.0), writes=SKEYS + OKEYS + O3KEYS + ["ACC", "QZ", "PT0", "PT1", "VG0", "VG1", "RD"])
        P.add("sp", lambda e: e.dma_start(out=LNG, in_=lng_d), writes=["LNG"], dma=True)
        P.add("sp", lambda e: e.dma_start(out=LNB, in_=lnb_d), writes=["LNB"], dma=True)

        def out_block(i, nrow, xrows, mixT, ydst, mkeys):
            xr = XR3[i % 3][0:nrow, :]
            z = ZT3[i % 3][0:nrow, :]
            xk, zk, sk = "XR%d" % (i % 3), "ZT%d" % (i % 3), "STAT%d" % (i % 2)
            so = 16 * (i % 2)
            st6 = STAT[0:nrow, so:so + 12].rearrange("p (c s) -> p c s", c=2)
            mv = STAT[0:nrow, so + 12:so + 14]
            rstd = STAT[0:nrow, so + 14:so + 15]

            def stage1():
                def half_fn(half):
                    bank = 2 + 2 * (i % 3) + half

                    def one(kc):
                        P.add("pe", lambda e: e.matmul(banks[bank][0:nrow, :], lhsT=mixT(kc), rhs=WOUT[:, kc, half * 512:(half + 1) * 512],
                                                       start=(kc == 0), stop=(kc == 7)), reads=mkeys + ["WOUT"], writes=[bkey(bank)])
                    for kc in range(8):
                        one(kc)
                    P.add("dve", lambda e: e.scalar_tensor_tensor(out=z[:, half * 512:(half + 1) * 512], in0=xr[:, half * 512:(half + 1) * 512],
                                                                  scalar=ALPHA, in1=banks[bank][0:nrow, :], op0=ALU.mult, op1=ALU.add),
                          reads=[bkey(bank), xk], writes=[zk])
                    P.add("dve", lambda e: e.bn_stats(out=st6[:, half, :], in_=z[:, half * 512:(half + 1) * 512]), reads=[zk], writes=[sk])
                half_fn(0)
                half_fn(1)
                P.add("dve", lambda e: e.bn_aggr(out=mv, in_=st6), reads=[sk], writes=[sk])
                P.add("dve", lambda e: e.tensor_scalar(out=rstd, in0=mv[:, 1:2], scalar1=LN_EPS, scalar2=None, op0=ALU.add), reads=[sk], writes=[sk])
                P.add("act", lambda e: e.activation(out=rstd, in_=rstd, func=AF.Ln), reads=[sk], writes=[sk])
                P.add("act", lambda e: e.activation(out=rstd, in_=rstd, func=AF.Exp, scale=-0.5), reads=[sk], writes=[sk])

            def xload():
                P.add("sp", lambda e: e.dma_start(out=xr, in_=xrows), writes=[xk], dma=True)

            def stage2():
                P.add("dve", lambda e: e.tensor_scalar(out=z, in0=z, scalar1=mv[:, 0:1], scalar2=rstd, op0=ALU.subtract, op1=ALU.mult), reads=[zk, sk], writes=[zk])
                P.add("pool", lambda e: e.tensor_tensor(out=z, in0=z, in1=LNG[0:nrow, :], op=ALU.mult), reads=[zk, "LNG"], writes=[zk])

            def stage3():
                P.add("dve", lambda e: e.tensor_tensor(out=z, in0=z, in1=LNB[0:nrow, :], op=ALU.add), reads=[zk, "LNB"], writes=[zk])
                P.add("pool", lambda e: e.dma_start(out=ydst, in_=z), reads=[zk], dma=True)
            return stage1, stage2, stage3, xload

        def prompt_block(tb):
            def mixT(kc):
                src = GA if kc < 4 else ML
                return src[:, kc % 4, tb * 128:(tb + 1) * 128]
            return out_block(tb, 128, xo[tb * 128:(tb + 1) * 128, :], mixT, y_d[tb * 128:(tb + 1) * 128, :], ["GA", "ML"])

        blocks = [prompt_block(tb) for tb in range(16)]
        if do_sample:
            def mixTs(kc):
                src = SMA if kc < 4 else SML
                return src[:, kc % 4, :]
            blocks.append(out_block(16, NS, xs_d, mixTs, ys_d, ["SMA", "SML"]))
        nblk = len(blocks)
        blocks[0][3]()
        for j in range(nblk + 2):
            if j + 1 < nblk:
                blocks[j + 1][3]()
            if j < nblk:
                blocks[j][0]()
            if 0 <= j - 1 < nblk:
                blocks[j - 1][1]()
            if 0 <= j - 2 < nblk:
                blocks[j - 2][2]()

        P.emit(nc)
    return nc


def _band_mask():
    s = np.arange(128)[:, None]
    u = np.arange(256)[None, :]
    ok = (u - s >= 0) & (u - s <= 128)
    m = np.where(ok, 0.0, NEG).astype(np.float32)
    return np.ascontiguousarray(np.broadcast_to(m[:, None, :], (128, 2, 256)))


def _fm(v):
    return np.ascontiguousarray(np.asarray(v, np.float32).reshape(4, 128).T)


_NC_CACHE = {}


def kernel(x_prompt, x_sample, cache_k, cache_v, state_conv, state_h, w_in, conv_w, conv_b,
           w_ra, b_ra, w_ri, b_ri, lru_lambda, w_out, ln_g, ln_b):
    f32 = np.float32
    x_prompt = np.asarray(x_prompt, f32)
    x_sample = np.asarray(x_sample, f32)
    cache_k = np.asarray(cache_k, f32)
    cache_v = np.asarray(cache_v, f32)
    state_conv = np.asarray(state_conv, f32)
    state_h = np.asarray(state_h, f32)
    w_in0 = np.ascontiguousarray(np.asarray(w_in, f32)[0])
    w_out0 = np.ascontiguousarray(np.asarray(w_out, f32)[0])
    conv_w = np.asarray(conv_w, f32)[0]
    w_ra = np.asarray(w_ra, f32)[0]
    w_ri = np.asarray(w_ri, f32)[0]

    if "nc" not in _NC_CACHE:
        _NC_CACHE["nc"] = build_program(True)
    nc = _NC_CACHE["nc"]

    prm = np.zeros((128, 32), f32)
    for fc in range(4):
        for k in range(4):
            prm[:, fc * 4 + k] = conv_w[k, fc * 128:(fc + 1) * 128]
    prm[:, 16:20] = _fm(np.asarray(conv_b, f32)[0])
    prm[:, 20:24] = -_fm(np.asarray(b_ra, f32)[0])
    prm[:, 24:28] = -_fm(np.asarray(b_ri, f32)[0])
    prm[:, 28:32] = _fm(np.asarray(lru_lambda, f32)[0])

    def bd(w):
        o = np.zeros((128, 4, 128), f32)
        for fc in range(4):
            o[0:64, fc, 0:64] = w[2 * fc]
            o[64:128, fc, 64:128] = w[2 * fc + 1]
        return o

    wra_bd, wri_bd = bd(w_ra), bd(w_ri)
    lng = np.ascontiguousarray(np.broadcast_to(np.asarray(ln_g, f32)[0][None, :], (128, 1024)))
    lnb = np.ascontiguousarray(np.broadcast_to(np.asarray(ln_b, f32)[0][None, :], (128, 1024)))
    ident = np.eye(128, dtype=f32)
    band = _band_mask()
    allneg = np.full((128, 2, 256), NEG, f32)
    m1 = (np.arange(128)[:, None, None] >= np.arange(4)[None, :, None]).astype(f32)
    m1 = np.ascontiguousarray(np.broadcast_to(m1, (128, 4, 8)).reshape(128, 32))
    wn = np.zeros((4, 4), f32)
    for tp in range(4):
        for t in range(4):
            wn[tp, t] = 3.0 if tp == t else (1.0 if tp < t else 0.0)
    wnew = np.ascontiguousarray(np.broadcast_to(wn[:, :, None], (4, 4, 8)).reshape(4, 32))

    in_maps = []
    for c in range(8):
        b, h = c // 2, c % 2
        xo = np.ascontiguousarray(x_prompt[b, h * 2048:(h + 1) * 2048])
        xc = np.ascontiguousarray(x_prompt[b, 0:2048]) if h == 1 else np.zeros((2048, 1024), f32)
        sl = slice(16 * c, 16 * (c + 1))
        sconv = state_conv[0, sl]
        sconv_fm = np.ascontiguousarray(sconv.reshape(16, 3, 4, 128).transpose(3, 2, 0, 1))
        sh_fm = np.ascontiguousarray(state_h[0, sl].reshape(16, 4, 128).transpose(2, 1, 0))
        in_maps.append({
            "xo": xo, "xc": xc, "w_in": w_in0, "w_out": w_out0, "ident": ident,
            "mask_own": band, "mask_ctx": band if h == 1 else allneg, "prm": prm,
            "wra_bd": wra_bd, "wri_bd": wri_bd, "lng": lng, "lnb": lnb,
            "flag": np.full((128, 1), float(h), f32),
            "xs": np.ascontiguousarray(x_sample[sl].reshape(64, 1024)),
            "ck": np.ascontiguousarray(cache_k[0, sl].reshape(16, 2048, 512)),
            "cv": np.ascontiguousarray(cache_v[0, sl].reshape(16, 2048, 512)),
            "sconv": sconv_fm, "sh": sh_fm, "m1": m1, "wnew": wnew,
        })
    res = run_bass_kernel_spmd(nc, in_maps, core_ids=list(range(8)))
    R = res.results

    y_prompt = np.zeros((4, 4096, 1024), f32)
    k_prompt = np.zeros((1, 4, 2048, 8, 64), f32)
    v_prompt = np.zeros((1, 4, 2048, 8, 64), f32)
    conv_prompt = np.zeros((1, 4, 3, 512), f32)
    h_prompt = np.zeros((1, 4, 512), f32)
    y_sample = np.zeros((128, 4, 1024), f32)
    k_sample = np.zeros((1, 128, 4, 8, 64), f32)
    v_sample = np.zeros((1, 128, 4, 8, 64), f32)
    conv_sample = np.zeros((1, 128, 3, 512), f32)
    h_sample = np.zeros((1, 128, 512), f32)
    for c in range(8):
        b, h = c // 2, c % 2
        r = R[c]
        y_prompt[b, h * 2048:(h + 1) * 2048] = r["y"]
        if h == 1:
            k_prompt[0, b] = r["kT"].T.reshape(2048, 8, 64)
            v_prompt[0, b] = r["vT"].T.reshape(2048, 8, 64)
            conv_prompt[0, b] = r["convT"].transpose(2, 1, 0).reshape(3, 512)
            h_prompt[0, b] = r["hT"].T.reshape(512)
        sl = slice(16 * c, 16 * (c + 1))
        y_sample[sl] = r["ys"].reshape(16, 4, 1024)
        k_sample[0, sl] = r["ks"].reshape(16, 4, 8, 64)
        v_sample[0, sl] = r["vs"].reshape(16, 4, 8, 64)
        conv_sample[0, sl] = r["convs"].transpose(2, 3, 1, 0).reshape(16, 3, 512)
        h_sample[0, sl] = r["hs"].transpose(2, 1, 0).reshape(16, 512)
    return (y_prompt, y_sample, k_prompt, v_prompt, conv_prompt, h_prompt,
            k_sample, v_sample, conv_sample, h_sample)
```
